# Optimizing a Trainium2 kernel written in Bass

```python
import jax, jax.numpy as jnp
from jax import lax
import numpy as np

D_MODEL = 2048
BATCH = 4
SEQ = 2048
DEPTH = 1
DEC_BATCH = 128
DEC_SEQ = 8
PAST_LEN = 2048
PAGE_SIZE = 128

RET_HEADS = 4
RET_DK = 256
RET_DV = 256
RET_WIDTH = RET_HEADS * RET_DV
RET_CHUNK = 128
NSA_HEADS = 16
NSA_KV = 4
NSA_REP = NSA_HEADS // NSA_KV
HEAD_DIM = 64
NSA_WIDTH = NSA_HEADS * HEAD_DIM
CMP_LEN = 32
CMP_STRIDE = 16
CMP_HIDDEN = 256
SEL_BLK = 64
SEL_TOPK = 16
WINDOW = 512
WIN_QBLK = 128
SEL_QBLK = 32
SCALE = HEAD_DIM ** -0.5
D_MIX = RET_WIDTH + NSA_WIDTH
SPLITS = (RET_HEADS * RET_DK, RET_HEADS * RET_DK, RET_WIDTH, RET_WIDTH, NSA_WIDTH, 6 * NSA_KV * HEAD_DIM, 3 * NSA_HEADS)
D_IN = sum(SPLITS)
D_FF = 5632
CONV_W = 3
EPS = 1e-6

kernel_name = 'retnet_nsa_hymba_convffn_step'


def _rms(x, g):
    xf = x.astype(jnp.float32)
    y = xf * lax.rsqrt(jnp.mean(xf * xf, axis=-1, keepdims=True) + EPS)
    return (y * g.astype(jnp.float32)).astype(x.dtype)


def _masked_softmax(s, valid):
    s = jnp.where(valid, s, -jnp.inf)
    m = jnp.max(s, axis=-1, keepdims=True)
    m = jnp.where(jnp.isfinite(m), m, 0.0)
    e = jnp.where(valid, jnp.exp(s - m), 0.0)
    return e / jnp.maximum(jnp.sum(e, axis=-1, keepdims=True), 1e-30)


def _alibi_slopes():
    h = jnp.arange(1, NSA_HEADS + 1, dtype=jnp.float32)
    return jnp.exp2(-8.0 * h / NSA_HEADS).reshape(NSA_KV, NSA_REP)


def _rotate(x, pos):
    half = x.shape[-1] // 2
    inv = 1.0 / (10000.0 ** jnp.linspace(0.0, 1.0, half, dtype=jnp.float32))
    ang = pos.astype(jnp.float32)[:, None] * inv[None, :]
    cos = jnp.cos(ang)[:, None, :]
    sin = jnp.sin(ang)[:, None, :]
    xf = x.astype(jnp.float32)
    x1, x2 = xf[..., :half], xf[..., half:]
    return jnp.concatenate([x1 * cos - x2 * sin, x2 * cos + x1 * sin], axis=-1).astype(x.dtype)


def _retention(q, k, v, state0, chunk):
    B, T, H, DK = q.shape
    DV = v.shape[-1]
    n = T // chunk
    lg = jnp.log(1.0 - jnp.exp2(-5.0 - jnp.arange(H, dtype=jnp.float32)))
    i = jnp.arange(chunk, dtype=jnp.float32)
    diff = i[:, None] - i[None, :]
    dmask = jnp.where(diff >= 0, jnp.exp(lg[:, None, None] * jnp.maximum(diff, 0.0)), 0.0)
    xi = jnp.exp(lg[None, :] * (i[:, None] + 1.0))
    zeta = jnp.exp(lg[None, :] * (chunk - 1.0 - i[:, None]))
    g_chunk = jnp.exp(lg * chunk)

    def to_chunks(a):
        return a.astype(jnp.float32).reshape(B, n, chunk, H, a.shape[-1]).swapaxes(0, 1)

    def step(S, inp):
        qi, ki, vi = inp
        s = jnp.einsum('bihd,bjhd->bhij', qi, ki) * dmask
        o = jnp.einsum('bhij,bjhe->bihe', s, vi) + jnp.einsum('bihd,bhde->bihe', qi, S) * xi[:, :, None]
        S = S * g_chunk[:, None, None] + jnp.einsum('bjhd,bjhe->bhde', ki * zeta[:, :, None], vi)
        return S, o

    S, o = lax.scan(step, state0.astype(jnp.float32), (to_chunks(q), to_chunks(k), to_chunks(v)))
    return o.swapaxes(0, 1).reshape(B, T, H, DV).astype(q.dtype), S


def _block_cover(n_cmp, n_sel):
    cs = np.arange(n_cmp)[:, None] * CMP_STRIDE
    js = np.arange(n_sel)[None, :] * SEL_BLK
    cov = np.clip(np.minimum(cs + CMP_LEN, js + SEL_BLK) - np.maximum(cs, js), 0, None)
    return jnp.asarray(cov / CMP_LEN, dtype=jnp.float32)


def _compress(kv_raw, pe, w1, b1, w2, b2):
    B, Tk = kv_raw.shape[:2]
    n_half = Tk // CMP_STRIDE
    hb = kv_raw[:, :n_half * CMP_STRIDE].reshape(B, n_half, CMP_STRIDE, 2, NSA_KV, HEAD_DIM)
    pe = pe.reshape(2, 2, CMP_STRIDE, HEAD_DIM)
    w1 = w1.reshape(2, 2, CMP_STRIDE, HEAD_DIM, CMP_HIDDEN)

    def half(j):
        pej = pe[:, j].transpose(1, 0, 2)[:, :, None, :]
        return jnp.einsum('bnlcgd,cldh->bncgh', hb + pej, w1[:, j])

    hid = jax.nn.silu(half(0)[:, :-1] + half(1)[:, 1:] + b1[:, None, :])
    return jnp.einsum('bncgh,chd->bncgd', hid, w2) + b2[:, None, :]


def _band_attend(q, qpos, k, v, kpos):
    slopes = _alibi_slopes()
    dist = qpos[:, None] - kpos[None, :]
    valid = (kpos[None, :] >= 0) & (dist >= 0) & (dist < WINDOW)
    s = jnp.einsum('bqgrd,bsgd->bgrqs', q, k).astype(jnp.float32) * SCALE \
        - slopes[None, :, :, None, None] * dist.astype(jnp.float32)
    p = _masked_softmax(s, valid)
    return jnp.einsum('bgrqs,bsgd->bqgrd', p.astype(v.dtype), v)


def _window_prompt(q, win):
    B, T = q.shape[:2]
    nb = T // WIN_QBLK
    kp = jnp.pad(win, ((0, 0), (WINDOW, 0), (0, 0), (0, 0), (0, 0)))
    qb = q.reshape((B, nb, WIN_QBLK) + q.shape[2:]).swapaxes(0, 1)
    starts = jnp.arange(nb) * WIN_QBLK

    def blk(args):
        qi, st = args
        ctx = lax.dynamic_slice_in_dim(kp, st, WINDOW + WIN_QBLK, axis=1)
        return _band_attend(qi, st + jnp.arange(WIN_QBLK), ctx[:, :, 0], ctx[:, :, 1],
                            st - WINDOW + jnp.arange(WINDOW + WIN_QBLK))

    o = lax.map(blk, (qb, starts))
    return o.swapaxes(0, 1).reshape(q.shape)


def _nsa_cmp_sel(q, q0, rows, sel_qblk, p):
    B, Tq = q.shape[:2]
    Tk = rows.shape[1]
    slopes = _alibi_slopes()
    qpos = q0 + jnp.arange(Tq)
    cmp = _compress(rows[:, :, 0:2], p['cmp_pe'], p['cmp_w1'], p['cmp_b1'], p['cmp_w2'], p['cmp_b2'])
    kc = _rms(cmp[:, :, 0], p['k_norm_cmp'])
    vc = cmp[:, :, 1]
    n_cmp = kc.shape[1]
    cend = jnp.arange(n_cmp) * CMP_STRIDE + (CMP_LEN - 1)
    dist = qpos[:, None] - cend[None, :]
    s = jnp.einsum('btgrd,bngd->bgrtn', q, kc).astype(jnp.float32) * SCALE \
        - slopes[None, :, :, None, None] * dist.astype(jnp.float32)
    pc = _masked_softmax(s, dist >= 0)
    o_cmp = jnp.einsum('bgrtn,bngd->btgrd', pc.astype(vc.dtype), vc)
    n_sel = -(-Tk // SEL_BLK)
    k_sel = min(SEL_TOPK, n_sel)
    imp = jnp.einsum('bgrtn,nj->btgj', pc, _block_cover(n_cmp, n_sel))
    j = jnp.arange(n_sel)[None, :]
    cur = (qpos // SEL_BLK)[:, None]
    valid_j = (j <= cur)[None, :, None, :]
    forced = ((j == 0) | (j == cur) | (j == cur - 1))[None, :, None, :]
    score = jnp.where(valid_j, jnp.where(forced, jnp.inf, imp), -jnp.inf)
    top_val, top_idx = lax.top_k(score, k_sel)
    top_ok = top_val > -jnp.inf
    pad = n_sel * SEL_BLK - Tk

    def blocks(a):
        a = jnp.pad(a, ((0, 0), (0, pad), (0, 0), (0, 0)))
        return a.reshape(B, n_sel, SEL_BLK, NSA_KV, HEAD_DIM).transpose(0, 3, 1, 2, 4)

    kb = blocks(rows[:, :, 2])
    vb = blocks(rows[:, :, 3])
    nqb = Tq // sel_qblk

    def split_q(a):
        return a.reshape((B, nqb, sel_qblk) + a.shape[2:]).swapaxes(0, 1)

    bi = jnp.arange(B)[:, None, None, None]
    gi = jnp.arange(NSA_KV)[None, None, :, None]

    def sel_block(args):
        qi, ii, oki, pi = args
        kg = kb[bi, gi, ii]
        vg = vb[bi, gi, ii]
        kpos = ii[..., None] * SEL_BLK + jnp.arange(SEL_BLK)
        d = pi[None, :, None, None, None] - kpos
        ok = (oki[..., None] & (d >= 0))[:, :, :, None]
        sc = jnp.einsum('bqgrd,bqgkld->bqgrkl', qi, kg).astype(jnp.float32) * SCALE \
            - slopes[None, None, :, :, None, None] * d[:, :, :, None].astype(jnp.float32)
        shp = sc.shape
        flat = shp[:4] + (shp[4] * shp[5],)
        pr = _masked_softmax(sc.reshape(flat), jnp.broadcast_to(ok, shp).reshape(flat)).reshape(shp)
        return jnp.einsum('bqgrkl,bqgkld->bqgrd', pr.astype(vg.dtype), vg)

    o_sel = lax.map(sel_block, (split_q(q), split_q(top_idx), split_q(top_ok), qpos.reshape(nqb, sel_qblk)))
    return o_cmp, o_sel.swapaxes(0, 1).reshape(q.shape)


def _project(x, pos, p):
    B, T = x.shape[:2]
    z = _rms(x, p['g_attn']) @ p['w_in']
    points = [int(c) for c in np.cumsum(SPLITS)[:-1]]
    rq, rk, rv, rg, nq, nkv, ng = jnp.split(z, points, axis=-1)
    rq = _rotate(rq.reshape(B, T, RET_HEADS, RET_DK), pos)
    rk = _rotate(rk.reshape(B, T, RET_HEADS, RET_DK), pos) * (RET_DK ** -0.5)
    rv = rv.reshape(B, T, RET_HEADS, RET_DV)
    nq = _rms(nq.reshape(B, T, NSA_KV, NSA_REP, HEAD_DIM), p['q_norm'])
    nkv = nkv.reshape(B, T, 6, NSA_KV, HEAD_DIM)
    k_slc = _rms(nkv[:, :, 2], p['k_norm_slc'])
    k_win = _rms(nkv[:, :, 4], p['k_norm_win'])
    rows = jnp.stack([nkv[:, :, 0], nkv[:, :, 1], k_slc, nkv[:, :, 3]], axis=2)
    win = jnp.stack([k_win, nkv[:, :, 5]], axis=2)
    gates = jax.nn.sigmoid(ng.reshape(B, T, NSA_KV, NSA_REP, 3))
    return rq, rk, rv, rg, nq, rows, win, gates


def _mixer_out(x, ret_o, rg, o_cmp, o_sel, o_win, gates, p):
    B, T = x.shape[:2]
    ret = _rms(ret_o, p['ret_gn']) * jax.nn.silu(rg.reshape(ret_o.shape))
    nsa = gates[..., 0:1] * o_cmp + gates[..., 1:2] * o_sel + gates[..., 2:3] * o_win
    mix = jnp.concatenate([ret.reshape(B, T, RET_WIDTH), nsa.reshape(B, T, NSA_WIDTH)], axis=-1)
    return x + mix @ p['w_out']


def _conv_ffn(x, prev, p):
    T = x.shape[1]
    u = _rms(x, p['g_ffn']) @ p['w_up']
    up = jnp.concatenate([prev.astype(u.dtype), u], axis=1)
    c = p['conv_b'] + p['conv_w'][CONV_W - 1] * up[:, CONV_W - 1:]
    for j in range(CONV_W - 1):
        c = c + p['conv_w'][j] * up[:, j:j + T]
    a, b = jnp.split(c, 2, axis=-1)
    return x + (jax.nn.silu(a) * b) @ p['w_down'], up[:, T:]


def _layer_prompt(x, p):
    B, T = x.shape[:2]
    pos = jnp.arange(T)
    rq, rk, rv, rg, nq, rows, win, gates = _project(x, pos, p)
    s0 = jnp.zeros((B, RET_HEADS, RET_DK, RET_DV), jnp.float32)
    ret_o, ret_s = _retention(rq, rk, rv, s0, min(RET_CHUNK, T))
    o_cmp, o_sel = _nsa_cmp_sel(nq, 0, rows, SEL_QBLK, p)
    o_win = _window_prompt(nq, win)
    h = _mixer_out(x, ret_o, rg, o_cmp, o_sel, o_win, gates, p)
    y, conv_s = _conv_ffn(h, jnp.zeros((B, CONV_W - 1, 2 * D_FF), h.dtype), p)
    return y, rows, win[:, -min(WINDOW, T):], ret_s, conv_s


def _layer_sample(x, cache_kv, cache_win, state_ret, state_conv, page_table, p):
    B, T = x.shape[:2]
    past = page_table.shape[1] * cache_kv.shape[1]
    pos = past + jnp.arange(T)
    rq, rk, rv, rg, nq, rows, win, gates = _project(x, pos, p)
    hist = cache_kv[page_table].reshape((B, past) + cache_kv.shape[2:])
    full = jnp.concatenate([hist.astype(rows.dtype), rows], axis=1)
    ret_o, ret_s = _retention(rq, rk, rv, state_ret, T)
    o_cmp, o_sel = _nsa_cmp_sel(nq, past, full, 1, p)
    wb = cache_win.shape[1]
    ctx = jnp.concatenate([cache_win.astype(win.dtype), win], axis=1)
    o_win = _band_attend(nq, pos, ctx[:, :, 0], ctx[:, :, 1], past - wb + jnp.arange(wb + T))
    h = _mixer_out(x, ret_o, rg, o_cmp, o_sel, o_win, gates, p)
    y, conv_s = _conv_ffn(h, state_conv, p)
    return y, rows, ctx[:, T:], ret_s, conv_s


def setup_inputs(seed: int = 0) -> dict:
    key = jax.random.key(seed)
    ks = iter(jax.random.split(key, 32))

    def nrm(shape, scale):
        return jax.random.normal(next(ks), shape, jnp.float32) * scale

    n_pages = PAST_LEN // PAGE_SIZE
    n_used = DEC_BATCH * n_pages
    n_pool = n_used + max(1, n_used // 4)
    win_buf = min(WINDOW, PAST_LEN)
    L = DEPTH
    return {
        'x_prompt': nrm((BATCH, SEQ, D_MODEL), 1.0),
        'x_sample': nrm((DEC_BATCH, DEC_SEQ, D_MODEL), 1.0),
        'cache_kv': nrm((L, n_pool, PAGE_SIZE, 4, NSA_KV, HEAD_DIM), 1.0),
        'cache_win': nrm((L, DEC_BATCH, win_buf, 2, NSA_KV, HEAD_DIM), 1.0),
        'state_ret': nrm((L, DEC_BATCH, RET_HEADS, RET_DK, RET_DV), 0.5),
        'state_conv': nrm((L, DEC_BATCH, CONV_W - 1, 2 * D_FF), 1.0),
        'page_table': jax.random.permutation(next(ks), n_pool)[:n_used].reshape(DEC_BATCH, n_pages).astype(jnp.int32),
        'g_attn': 1.0 + nrm((L, D_MODEL), 0.1),
        'w_in': nrm((L, D_MODEL, D_IN), D_MODEL ** -0.5),
        'q_norm': 1.0 + nrm((L, HEAD_DIM), 0.1),
        'k_norm_cmp': 1.0 + nrm((L, HEAD_DIM), 0.1),
        'k_norm_slc': 1.0 + nrm((L, HEAD_DIM), 0.1),
        'k_norm_win': 1.0 + nrm((L, HEAD_DIM), 0.1),
        'cmp_pe': nrm((L, 2, CMP_LEN, HEAD_DIM), 0.5),
        'cmp_w1': nrm((L, 2, CMP_LEN, HEAD_DIM, CMP_HIDDEN), (CMP_LEN * HEAD_DIM) ** -0.5),
        'cmp_b1': nrm((L, 2, CMP_HIDDEN), 0.01),
        'cmp_w2': nrm((L, 2, CMP_HIDDEN, HEAD_DIM), CMP_HIDDEN ** -0.5),
        'cmp_b2': nrm((L, 2, HEAD_DIM), 0.01),
        'ret_gn': 1.0 + nrm((L, RET_HEADS, RET_DV), 0.1),
        'w_out': nrm((L, D_MIX, D_MODEL), D_MIX ** -0.5),
        'g_ffn': 1.0 + nrm((L, D_MODEL), 0.1),
        'w_up': nrm((L, D_MODEL, 2 * D_FF), D_MODEL ** -0.5),
        'conv_w': nrm((L, CONV_W, 2 * D_FF), CONV_W ** -0.5),
        'conv_b': nrm((L, 2 * D_FF), 0.01),
        'w_down': nrm((L, D_FF, D_MODEL), D_FF ** -0.5),
    }


def reference(x_prompt, x_sample, cache_kv, cache_win, state_ret, state_conv, page_table,
              g_attn, w_in, q_norm, k_norm_cmp, k_norm_slc, k_norm_win, cmp_pe, cmp_w1, cmp_b1,
              cmp_w2, cmp_b2, ret_gn, w_out, g_ffn, w_up, conv_w, conv_b, w_down):
    xp, xs = x_prompt, x_sample
    kv_p, kv_s, win_p, win_s, ret_p, ret_s, conv_p, conv_s = [], [], [], [], [], [], [], []
    for l in range(DEPTH):
        p = dict(g_attn=g_attn[l], w_in=w_in[l], q_norm=q_norm[l], k_norm_cmp=k_norm_cmp[l],
                 k_norm_slc=k_norm_slc[l], k_norm_win=k_norm_win[l], cmp_pe=cmp_pe[l], cmp_w1=cmp_w1[l],
                 cmp_b1=cmp_b1[l], cmp_w2=cmp_w2[l], cmp_b2=cmp_b2[l], ret_gn=ret_gn[l], w_out=w_out[l],
                 g_ffn=g_ffn[l], w_up=w_up[l], conv_w=conv_w[l], conv_b=conv_b[l], w_down=w_down[l])
        xp, a, b, c, d = _layer_prompt(xp, p)
        kv_p.append(a); win_p.append(b); ret_p.append(c); conv_p.append(d)
        xs, a, b, c, d = _layer_sample(xs, cache_kv[l], cache_win[l], state_ret[l], state_conv[l], page_table, p)
        kv_s.append(a); win_s.append(b); ret_s.append(c); conv_s.append(d)
    return (xp, xs, jnp.stack(kv_p), jnp.stack(kv_s), jnp.stack(win_p), jnp.stack(win_s),
            jnp.stack(ret_p), jnp.stack(ret_s), jnp.stack(conv_p), jnp.stack(conv_s))
```

```python
import numpy as np
from contextlib import ExitStack
import ml_dtypes
import concourse.bass as bass
import concourse.mybir as mybir
from concourse.bass_utils import run_bass_kernel_spmd

F32 = mybir.dt.float32
BF16 = mybir.dt.bfloat16
I32 = mybir.dt.int32
AF = mybir.ActivationFunctionType
ALU = mybir.AluOpType
AX = mybir.AxisListType

NCORES = 8
D = 2048
EPS = 1e-6
EP = 2000
NDMASEM = 24
NEG = -30000.0
WARM = 1


class Sched:
    ENG = ['pe', 'act', 'dve', 'pool', 'sp']

    def __init__(self, nc):
        self.nc = nc
        self.rec = {e: [] for e in self.ENG}
        self.n = {e: 0 for e in self.ENG}
        self.seen = {e: {f: 0 for f in self.ENG} for e in self.ENG}
        self.dseen = {e: {} for e in self.ENG}
        self.esem = {e: [] for e in self.ENG}
        self.lastw = {}
        self.readers = {}
        self.clock = {}
        self.dq = {}
        self.nwaits = 0

    def _esem(self, e, epoch):
        while len(self.esem[e]) <= epoch:
            self.esem[e].append(self.nc.alloc_semaphore(name=f"s_{e}_{len(self.esem[e])}"))
        return self.esem[e][epoch]

    def _wait(self, eng, ev, force=False):
        if ev[0] == 'eng':
            _, e2, n2 = ev
            if self.seen[eng][e2] >= n2:
                return
            if eng == 'pe' and e2 == 'pe':
                return
            epoch = (n2 - 1) // EP
            sem = self._esem(e2, epoch)
            val = n2 - epoch * EP
            self.rec[eng].append(lambda en, sem=sem, val=val: en.wait_ge(sem, val))
            self.nwaits += 1
            self.seen[eng][e2] = n2
        else:
            _, q, idx, val = ev
            key = (q, idx)
            if self.dseen[eng].get(key, 0) >= val and not force:
                return
            sem = self.dq[q]['sems'][idx]
            self.rec[eng].append(lambda en, sem=sem, val=val: en.wait_ge(sem, val))
            self.nwaits += 1
            self.dseen[eng][key] = val
        ck = self.clock.get(ev)
        if ck is not None:
            for f, v in ck[0].items():
                if self.seen[eng][f] < v:
                    self.seen[eng][f] = v
            for k, v in ck[1].items():
                if self.dseen[eng].get(k, 0) < v:
                    self.dseen[eng][k] = v

    def _deps(self, eng, reads, writes):
        evs = []
        for k in reads:
            if k in self.lastw:
                evs.append(self.lastw[k])
        for k in writes:
            if k in self.lastw:
                evs.append(self.lastw[k])
            evs.extend(self.readers.get(k, ()))
        for ev in evs:
            self._wait(eng, ev)

    def _commit(self, ev, eng, reads, writes):
        self.clock[ev] = (dict(self.seen[eng]), dict(self.dseen[eng]))
        for k in writes:
            self.lastw[k] = ev
            self.readers[k] = []
        for k in reads:
            if k in writes:
                continue
            self.readers.setdefault(k, []).append(ev)
            if len(self.readers[k]) > 64:
                self.readers[k] = self.readers[k][-64:]

    def op(self, eng, fn, reads=(), writes=()):
        self._deps(eng, reads, writes)
        self.n[eng] += 1
        n = self.n[eng]
        epoch = (n - 1) // EP
        sem = self._esem(eng, epoch)
        self.rec[eng].append(lambda en, fn=fn, sem=sem: fn(en).then_inc(sem, 1))
        self.seen[eng][eng] = n if eng == 'pe' else self.seen[eng][eng]
        ev = ('eng', eng, n)
        self._commit(ev, eng, reads, writes)
        return ev

    def dma(self, q, out, in_, reads=(), writes=(), **kw):
        if q not in self.dq:
            self.dq[q] = {'sems': [self.nc.alloc_semaphore(name=f"d_{q}_{i}") for i in range(NDMASEM)],
                          'count': 0}
        st = self.dq[q]
        i = st['count']
        st['count'] += 1
        idx = i % NDMASEM
        val = 16 * (i // NDMASEM + 1)
        if i >= NDMASEM:
            self._wait(q, ('dma', q, idx, val - 16), force=(q == 'pool'))
        self._deps(q, reads, writes)
        sem = st['sems'][idx]
        self.rec[q].append(lambda en, out=out, in_=in_, sem=sem, kw=kw:
                           en.dma_start(out=out, in_=in_, **kw).then_inc(sem, 16))
        ev = ('dma', q, idx, val)
        self._commit(ev, q, reads, writes)
        return ev

    def dma_gather(self, out, in_, idx_ap, nrows, reads=(), writes=()):
        q = 'pool'
        if q not in self.dq:
            self.dq[q] = {'sems': [self.nc.alloc_semaphore(name=f"d_{q}_{i}") for i in range(NDMASEM)], 'count': 0}
        st = self.dq[q]
        i = st['count']
        st['count'] += 1
        idx = i % NDMASEM
        val = 16 * (i // NDMASEM + 1)
        if i >= NDMASEM:
            self._wait(q, ('dma', q, idx, val - 16))
        self._deps(q, reads, writes)
        sem = st['sems'][idx]
        self.rec[q].append(lambda en, out=out, in_=in_, sem=sem, idx_ap=idx_ap: en.indirect_dma_start(
            out=out, out_offset=None, in_=in_, in_offset=bass.IndirectOffsetOnAxis(ap=idx_ap, axis=0),
            bounds_check=nrows - 1, oob_is_err=False).then_inc(sem, 16))
        ev = ('dma', q, idx, val)
        self._commit(ev, q, reads, writes)
        return ev

    def dma_dyn(self, q, out, in_fn, pt_ap, maxv, reads=(), writes=()):
        if q not in self.dq:
            self.dq[q] = {'sems': [self.nc.alloc_semaphore(name=f"d_{q}_{i}") for i in range(NDMASEM)], 'count': 0}
        st = self.dq[q]
        i = st['count']
        st['count'] += 1
        idx = i % NDMASEM
        val = 16 * (i // NDMASEM + 1)
        if i >= NDMASEM:
            self._wait(q, ('dma', q, idx, val - 16))
        self._deps(q, reads, writes)
        sem = st['sems'][idx]

        def f(en, out=out, in_fn=in_fn, sem=sem, pt_ap=pt_ap):
            pg = en.value_load(pt_ap, min_val=0, max_val=maxv)
            en.dma_start(out=out, in_=in_fn(pg)).then_inc(sem, 16)
        self.rec[q].append(f)
        ev = ('dma', q, idx, val)
        self._commit(ev, q, reads, writes)
        return ev

    def barrier(self):
        evs = []
        for e in self.ENG:
            if self.n[e] > 0:
                evs.append(('eng', e, self.n[e]))
        for q, st in self.dq.items():
            for i in range(max(0, st['count'] - NDMASEM), st['count']):
                evs.append(('dma', q, i % NDMASEM, 16 * (i // NDMASEM + 1)))
        for e in self.ENG:
            for ev in evs:
                if ev[0] == 'eng' and ev[1] == e and e == 'pe':
                    continue
                self._wait(e, ev)

    def finish(self):
        for q, st in self.dq.items():
            for i in range(max(0, st['count'] - NDMASEM), st['count']):
                self._wait('sp', ('dma', q, i % NDMASEM, 16 * (i // NDMASEM + 1)))
        for e in self.ENG:
            if e != 'sp' and self.n[e] > 0:
                self._wait('sp', ('eng', e, self.n[e]))

    def emit(self):
        nc = self.nc
        rec = self.rec
        with nc.Block() as block:
            @block.tensor
            def _(en):
                for f in rec['pe']:
                    f(en)

            @block.scalar
            def _(en):
                for f in rec['act']:
                    f(en)

            @block.vector
            def _(en):
                for f in rec['dve']:
                    f(en)

            @block.gpsimd
            def _(en):
                for f in rec['pool']:
                    f(en)

            @block.sync
            def _(en):
                for f in rec['sp']:
                    f(en)


def build(stage=99):
    nc = bass.Bass("TRN2", target_bir_lowering=False)
    S = Sched(nc)

    def din(name, shape, dt=F32):
        return nc.dram_tensor(name, list(shape), dt, kind="ExternalInput").ap()

    def dout(name, shape, dt=F32):
        return nc.dram_tensor(name, list(shape), dt, kind="ExternalOutput").ap()

    def sb(name, shape, dt=F32):
        return nc.alloc_sbuf_tensor(name, list(shape), dt).ap()

    xp = din("xp", [2048, D])
    xs = din("xs", [128, D])
    w_in = din("w_in", [D, 6704])
    g_attn = din("g_attn", [D])
    k_norm_slc = din("k_norm_slc", [64])
    k_norm_win = din("k_norm_win", [64])
    ident_d = din("ident", [128, 128], BF16)

    cache_win = din("cache_win", [16, 512, 512])
    state_ret_d = din("state_ret", [16, 4, 256, 256])
    cs_d = din("cs_tab", [17, 128, 2, 128])
    sct_d = din("sct_tab", [128, 2, 2, 4])
    rmask_d = din("rmask_tab", [128, 16])
    cmask_d = din("cmask_tab", [128, 16, 128], BF16)
    mT_d = din("mT_tab", [128, 2, 128])
    q_norm = din("q_norm", [64])
    ret_gn = din("ret_gn", [4, 256])
    k_norm_cmp = din("k_norm_cmp", [64])
    cache_kv = din("cache_kv", [2560 * 16, 8192])
    page_table_d = din("page_table", [16, 16], I32)
    upt_d = din("upt_tab", [128, 1])
    Es_d = din("Es_tab", [33, 17, 128], BF16)
    cmS_d = din("cmS_tab", [128, 2, 128], BF16)
    covS_d = din("covS_tab", [128, 33])
    ABs_d = din("ABs_tab", [128, 2, 33])
    Rsum_d = din("Rsum_tab", [32, 16, 128], BF16)
    kaugS_d = din("kaugS_tab", [10, 2176], BF16)
    kaugW_d = din("kaugW_tab", [10, 640], BF16)
    kaugcS_d = din("kaugcS_tab", [10, 128], BF16)
    qaugs_d = din("qaugs_tab", [2, 10, 16, 128], BF16)
    w_out = din("w_out", [D, D])
    g_ffn = din("g_ffn", [D])
    w_up = din("w_up", [D, 11264])
    conv_w = din("conv_w", [3, 11264])
    conv_b = din("conv_b", [11264])
    w_down = din("w_down", [5632, D])
    state_conv_d = din("state_conv", [16, 2, 11264])
    identf_d = din("identf", [128, 128])
    hflag_d = din("hflag", [1])
    y_out = dout("y_out", [1152, D])
    conv_p_out = dout("conv_p_out", [2, 11264])
    conv_s_out = dout("conv_s_out", [32, 11264])
    cmp_pe = din("cmp_pe", [2, 32, 64])
    cmp_w1 = din("cmp_w1", [2, 32, 64, 256])
    cmp_b1 = din("cmp_b1", [2, 256])
    cmp_w2 = din("cmp_w2", [2, 256, 64])
    cmp_b2 = din("cmp_b2", [2, 64])
    E_d = din("E_tab", [32, 16, 128], BF16)
    cm_d = din("cm_tab", [128, 2, 128], BF16)
    cmpm_d = din("cmpm_tab", [128, 9, 128], BF16)
    cov_d = din("cov_tab", [128, 32])
    AB_d = din("AB_tab", [128, 2, 9, 32])
    kaug_d = din("kaug_tab", [10, 2048], BF16)
    kaugc_d = din("kaugc_tab", [10, 128], BF16)
    qaug_d = din("qaug_tab", [10, 16, 1152], BF16)

    kv_out = dout("kv_out", [1152, 1024])
    win_p_out = dout("win_p_out", [512, 512])
    win_s_out = dout("win_s_out", [16, 512, 512])
    ret_p_out = dout("ret_p_out", [4, 256, 256])
    ret_s_out = dout("ret_s_out", [16, 4, 256, 256])

    ident = sb("ident_sb", [128, 128], BF16)
    S.dma('sp', ident, ident_d, writes=['ident'])
    gT = sb("gT", [128, 16])
    S.dma('sp', gT, g_attn.rearrange("(k p) -> p k", p=128), writes=['gT'])
    kn_slc = sb("kn_slc", [128, 64])
    kn_win = sb("kn_win", [128, 64])
    S.dma('sp', kn_slc, k_norm_slc.partition_broadcast(128), writes=['kn_slc'])
    S.dma('sp', kn_win, k_norm_win.partition_broadcast(128), writes=['kn_win'])

    epsb = sb("epsb", [128, 1])
    S.op('pool', lambda en: en.memset(epsb, EPS), writes=['epsb'])
    hist_all = sb("hist_all", [128, 2, 1024])
    dmy = sb("dmy", [128, 512], BF16)
    S.op('pool', lambda en: en.memset(dmy, 1.0), writes=['dmy'])

    def warm(bank_i, n=1):
        for _ in range(n * WARM):
            S.op('pe', lambda en: en.matmul(banks[bank_i], lhsT=ident, rhs=dmy, start=True, stop=True),
                 reads=['ident', 'dmy'], writes=[('bank', bank_i)])
    idx_i_perm = sb("idx_i_perm", [128, 32], I32)
    banks = [nc.alloc_psum_tensor(f"bank{i}", [128, 512], F32).ap() for i in range(8)]

    def dscr(name, shape, dt=BF16):
        return nc.dram_tensor(name, list(shape), dt, kind="Internal").ap()

    kvs_s = dscr("kvs_s", [17, 128, 1536])
    kd_s = dscr("kd_s", [17, 128, 1024])
    rv_s = dscr("rv_s", [17, 128, 1024])
    qx_s = dscr("qx_s", [17, 128, 1024])
    rg_s = dscr("rg_s", [17, 128, 1024])
    nq_s = dscr("nq_s", [17, 128, 1024])
    gate_s = dscr("gate_s", [17, 128, 48], F32)
    mix_s = dscr("mix_s", [17, 128, 2048])
    osc_s = [dscr(f"osc_s{i}", [16, 8, 4, 4, 65], F32) for i in range(3)]
    QT = list(range(7, 17))
    ALLT = list(range(17))
    w_in_v = w_in.rearrange("(k p) n -> p k n", p=128)
    bank_rr = {'n': 0}

    def next_bank(lo=2, n=6):
        b = lo + bank_rr['n'] % n
        bank_rr['n'] += 1
        return b

    GAM = [1.0 - 2.0 ** (-5 - hh) for hh in range(4)]

    with ExitStack() as esA:
        def sA(name, shape, dt=F32):
            return esA.enter_context(nc.sbuf_tensor(name, list(shape), dt)).ap()
        xnT = sA("xnT", [128, 17, 16, 128], BF16)
        wbuf = [sA("wbuf0", [128, 16, 512], BF16), sA("wbuf1", [128, 16, 512], BF16)]
        wstate = {'n': 0}

        def load_w(src_v, c0, ncols=512):
            i = wstate['n'] % 2
            wstate['n'] += 1
            S.dma('pool', wbuf[i][:, :, 0:ncols], src_v[:, :, c0:c0 + ncols], writes=[('wbuf', i)])
            return i

        def x_tile_ap(t):
            return xp[128 * t:128 * (t + 1), :] if t < 16 else xs[:, :]

        with ExitStack() as es1:
            def s1(name, shape, dt=F32):
                return es1.enter_context(nc.sbuf_tensor(name, list(shape), dt)).ap()
            xt = [s1("xt0", [128, D]), s1("xt1", [128, D])]
            xb = [s1("xb0", [128, D], BF16), s1("xb1", [128, D], BF16)]
            ss = [s1("ss0", [128, 1]), s1("ss1", [128, 1])]
            rstd = [s1("rstd0", [128, 1]), s1("rstd1", [128, 1])]
            for t in range(17):
                i = t % 2
                S.dma('sp', xt[i], x_tile_ap(t), writes=[('xt', i)])
                S.op('act', lambda en, i=i: en.activation(out=xb[i], in_=xt[i], func=AF.Square, accum_out=ss[i]),
                     reads=[('xt', i)], writes=[('xb', i), ('ss', i)])
                S.op('act', lambda en, i=i: en.activation(out=ss[i], in_=ss[i], func=AF.Sqrt, scale=1.0 / D, bias=epsb),
                     reads=[('ss', i), 'epsb'], writes=[('ss', i)])
                S.op('dve', lambda en, i=i: en.reciprocal(out=rstd[i], in_=ss[i]),
                     reads=[('ss', i)], writes=[('rstd', i)])
                S.op('act', lambda en, i=i: en.activation(out=xb[i], in_=xt[i], func=AF.Copy, scale=rstd[i]),
                     reads=[('xt', i), ('rstd', i)], writes=[('xb', i)])
                for half in range(2):
                    pb = banks[half].bitcast(BF16)
                    for kk in range(8):
                        k = half * 8 + kk
                        S.op('pe', lambda en, pb=pb, kk=kk, k=k, i=i: en.transpose(
                            out=pb[:, 128 * kk:128 * (kk + 1)], in_=xb[i][:, 128 * k:128 * (k + 1)], identity=ident),
                            reads=[('xb', i), 'ident'], writes=[('bank', half)])
                    S.op('dve', lambda en, pb=pb, half=half, t=t: en.tensor_tensor(
                        out=xnT[:, t, 8 * half:8 * half + 8, :],
                        in0=pb.rearrange("p (k n) -> p k n", k=8),
                        in1=gT[:, 8 * half:8 * half + 8].rearrange("p (k o) -> p k o", o=1).broadcast_to([128, 8, 128]),
                        op=ALU.mult),
                        reads=[('bank', half), 'gT'], writes=[('xnT', t)])
            S.barrier()

        def project_tm(t, wi, bk, ncols=512):
            for k in range(16):
                S.op('pe', lambda en, bk=bk, t=t, k=k, wi=wi: en.matmul(
                    banks[bk][:, 0:ncols], lhsT=xnT[:, t, k, :], rhs=wbuf[wi][:, k, 0:ncols],
                    start=(k == 0), stop=(k == 15)),
                    reads=[('xnT', t), ('wbuf', wi)], writes=[('bank', bk)])

        with ExitStack() as es2:
            def s2(name, shape, dt=F32):
                return es2.enter_context(nc.sbuf_tensor(name, list(shape), dt)).ap()
            rows = [s2("rows0", [128, 512]), s2("rows1", [128, 512])]
            stg = [s2("stg0", [128, 512], BF16), s2("stg1", [128, 512], BF16), s2("stg2", [128, 512], BF16)]
            gst = [s2("gst0", [128, 48]), s2("gst1", [128, 48])]
            sq, ssk, krot, ktmp = s2("sq", [128, 512]), s2("ssk", [128, 8]), s2("krot", [128, 512]), s2("ktmp", [128, 256])
            cs_all, sct, qn_t = s2("cs_all", [128, 2, 2, 128]), s2("sct", [128, 2, 2, 4]), s2("qn_t", [128, 64])
            S.dma('sp', sct, sct_d, writes=['sct'])
            S.dma('sp', qn_t, q_norm.partition_broadcast(128), writes=['qn_t'])
            S.op('dve', lambda en: en.tensor_scalar(out=qn_t, in0=qn_t, scalar1=0.125, scalar2=None, op0=ALU.mult),
                 reads=['qn_t'], writes=['qn_t'])
            cntr = {'r': 0, 's': 0, 'g': 0}

            def rms_heads(src, nh, kn, knk, key):
                v3 = src.rearrange("p (g d) -> p g d", d=64)
                sq3 = sq[:, 0:nh * 64].rearrange("p (g d) -> p g d", d=64)
                S.op('dve', lambda en: en.tensor_tensor(out=sq3, in0=v3, in1=v3, op=ALU.mult),
                     reads=[key], writes=['sq'])
                S.op('dve', lambda en: en.tensor_reduce(out=ssk[:, 0:nh], in_=sq3, axis=AX.X, op=ALU.add),
                     reads=['sq'], writes=['ssk'])
                S.op('act', lambda en: en.activation(out=ssk[:, 0:nh], in_=ssk[:, 0:nh], func=AF.Sqrt,
                                                     scale=1.0 / 64, bias=epsb),
                     reads=['ssk', 'epsb'], writes=['ssk'])
                S.op('dve', lambda en: en.reciprocal(out=ssk[:, 0:nh], in_=ssk[:, 0:nh]), reads=['ssk'], writes=['ssk'])
                S.op('dve', lambda en: en.tensor_tensor(
                    out=v3, in0=v3, in1=ssk[:, 0:nh].rearrange("p (g o) -> p g o", o=1).broadcast_to([128, nh, 64]),
                    op=ALU.mult), reads=['ssk', key], writes=[key])
                S.op('dve', lambda en: en.tensor_tensor(
                    out=v3, in0=v3, in1=kn.rearrange("p (o d) -> p o d", o=1).broadcast_to([128, nh, 64]),
                    op=ALU.mult), reads=[knk, key], writes=[key])

            def to_scratch_bf16(src, dst, keys_r, key_w, eng='act'):
                si = cntr['s'] % 3
                cntr['s'] += 1
                if eng == 'act':
                    S.op('act', lambda en, si=si: en.copy(out=stg[si], in_=src), reads=keys_r, writes=[('stg', si)])
                else:
                    S.op('dve', lambda en, si=si: en.tensor_copy(out=stg[si], in_=src), reads=keys_r, writes=[('stg', si)])
                S.dma('sp', dst, stg[si], reads=[('stg', si)], writes=[key_w])

            for j in range(3):
                wi = load_w(w_in_v, 5120 + 512 * j)
                for t in ALLT:
                    i = cntr['r'] % 2
                    cntr['r'] += 1
                    bk = next_bank()
                    project_tm(t, wi, bk)
                    S.op('act', lambda en, bk=bk, i=i: en.copy(out=rows[i], in_=banks[bk]),
                         reads=[('bank', bk)], writes=[('rows', i)])
                    if j >= 1:
                        kn, knk = (kn_slc, 'kn_slc') if j == 1 else (kn_win, 'kn_win')
                        rms_heads(rows[i][:, 0:256], 4, kn, knk, ('rows', i))
                    to_scratch_bf16(rows[i], kvs_s[t, :, 512 * j:512 * (j + 1)], [('rows', i)], ('kvs_s', t, j), eng='dve')
                    if t >= 8:
                        if j < 2:
                            r0 = 128 * (t - 8)
                            S.dma('sp', kv_out[r0:r0 + 128, 512 * j:512 * (j + 1)], rows[i],
                                  reads=[('rows', i)], writes=[('kv_out', t, j)])
                        elif 12 <= t < 16:
                            r0 = 128 * (t - 12)
                            S.dma('sp', win_p_out[r0:r0 + 128, :], rows[i], reads=[('rows', i)], writes=[('win_p', t)])
                        elif t == 16:
                            for sq_i in range(16):
                                S.dma('sp', win_s_out[sq_i, 504:512, :], rows[i][8 * sq_i:8 * sq_i + 8, :],
                                      reads=[('rows', i)], writes=[('win_s', 1, sq_i)])
            for sq_i in range(16):
                S.dma('sp', win_s_out[sq_i, 0:504, :], cache_win[sq_i, 8:512, :], writes=[('win_s', 0, sq_i)])

            def rot_group(c0, tiles, which, dst_s, dkey):
                for j in range(2):
                    wi = load_w(w_in_v, c0 + 512 * j)
                    for t in tiles:
                        bk = next_bank()
                        project_tm(t, wi, bk)
                        x = banks[bk].rearrange("p (h c f) -> p h c f", h=2, c=2)
                        x1, x2 = x[:, :, 0, :], x[:, :, 1, :]
                        ci = cntr['g'] % 2
                        cntr['g'] += 1
                        csk = ('cs_all', ci)
                        S.dma('sp', cs_all[:, ci], cs_d[t], writes=[csk])
                        cosb = cs_all[:, ci, 0:1, :].broadcast_to([128, 2, 128])
                        sinb = cs_all[:, ci, 1:2, :].broadcast_to([128, 2, 128])
                        r = krot.rearrange("p (h c f) -> p h c f", h=2, c=2)
                        r1, r2 = r[:, :, 0, :], r[:, :, 1, :]
                        tm = ktmp.rearrange("p (h f) -> p h f", h=2)
                        bkk = ('bank', bk)
                        S.op('dve', lambda en, x1=x1, cosb=cosb, r1=r1: en.tensor_tensor(out=r1, in0=x1, in1=cosb, op=ALU.mult),
                             reads=[bkk, csk], writes=['krot'])
                        S.op('dve', lambda en, x2=x2, sinb=sinb, tm=tm: en.tensor_tensor(out=tm, in0=x2, in1=sinb, op=ALU.mult),
                             reads=[bkk, csk], writes=['ktmp'])
                        S.op('dve', lambda en, r1=r1, tm=tm: en.tensor_tensor(out=r1, in0=r1, in1=tm, op=ALU.subtract),
                             reads=['krot', 'ktmp'], writes=['krot'])
                        S.op('dve', lambda en, x2=x2, cosb=cosb, r2=r2: en.tensor_tensor(out=r2, in0=x2, in1=cosb, op=ALU.mult),
                             reads=[bkk, csk], writes=['krot'])
                        S.op('dve', lambda en, x1=x1, sinb=sinb, tm=tm: en.tensor_tensor(out=tm, in0=x1, in1=sinb, op=ALU.mult),
                             reads=[bkk, csk, 'krot'], writes=['ktmp'])
                        S.op('dve', lambda en, r2=r2, tm=tm: en.tensor_tensor(out=r2, in0=r2, in1=tm, op=ALU.add),
                             reads=['krot', 'ktmp'], writes=['krot'])
                        ti = 0 if t < 16 else 1
                        si = cntr['s'] % 3
                        cntr['s'] += 1
                        S.op('dve', lambda en, j=j, ti=ti, si=si: en.tensor_tensor(
                            out=stg[si].rearrange("p (h f) -> p h f", h=2),
                            in0=krot.rearrange("p (h f) -> p h f", h=2),
                            in1=sct[:, ti, which, 2 * j:2 * j + 2].rearrange("p (h o) -> p h o", o=1).broadcast_to([128, 2, 256]),
                            op=ALU.mult), reads=['krot', 'sct'], writes=[('stg', si)])
                        S.dma('sp', dst_s[t, :, 512 * j:512 * (j + 1)], stg[si], reads=[('stg', si)], writes=[(dkey, t, j)])

            rot_group(1024, ALLT, 0, kd_s, 'kd_s')
            rot_group(0, QT, 1, qx_s, 'qx_s')
            for j in range(2):
                wi = load_w(w_in_v, 2048 + 512 * j)
                for t in ALLT:
                    bk = next_bank()
                    project_tm(t, wi, bk)
                    to_scratch_bf16(banks[bk], rv_s[t, :, 512 * j:512 * (j + 1)], [('bank', bk)], ('rv_s', t, j))
            for j in range(2):
                wi = load_w(w_in_v, 3072 + 512 * j)
                for t in QT:
                    bk = next_bank()
                    project_tm(t, wi, bk)
                    si = cntr['s'] % 3
                    cntr['s'] += 1
                    S.op('act', lambda en, bk=bk, si=si: en.activation(out=stg[si], in_=banks[bk], func=AF.Silu),
                         reads=[('bank', bk)], writes=[('stg', si)])
                    S.dma('sp', rg_s[t, :, 512 * j:512 * (j + 1)], stg[si], reads=[('stg', si)], writes=[('rg_s', t, j)])
            for j in range(2):
                wi = load_w(w_in_v, 4096 + 512 * j)
                for t in QT:
                    i = cntr['r'] % 2
                    cntr['r'] += 1
                    bk = next_bank()
                    project_tm(t, wi, bk)
                    S.op('act', lambda en, bk=bk, i=i: en.copy(out=rows[i], in_=banks[bk]),
                         reads=[('bank', bk)], writes=[('rows', i)])
                    rms_heads(rows[i], 8, qn_t, 'qn_t', ('rows', i))
                    to_scratch_bf16(rows[i], nq_s[t, :, 512 * j:512 * (j + 1)], [('rows', i)], ('nq_s', t, j), eng='dve')
            wi = load_w(w_in_v, 6656, ncols=48)
            for t in QT:
                bk = next_bank()
                project_tm(t, wi, bk, ncols=48)
                gi = t % 2
                S.op('act', lambda en, bk=bk, gi=gi: en.activation(out=gst[gi], in_=banks[bk][:, 0:48], func=AF.Sigmoid),
                     reads=[('bank', bk)], writes=[('gst', gi)])
                S.dma('sp', gate_s[t], gst[gi], reads=[('gst', gi)], writes=[('gate_s', t)])
            S.barrier()

    with ExitStack() as esB:
        def sB(name, shape, dt=F32):
            return esB.enter_context(nc.sbuf_tensor(name, list(shape), dt)).ap()
        S32, Sbf = sB("S32", [128, 8, 256]), sB("Sbf", [128, 8, 256], BF16)
        kdb = [sB("kd0", [128, 1024], BF16), sB("kd1", [128, 1024], BF16)]
        rvb = [sB("rv0", [128, 1024], BF16), sB("rv1", [128, 1024], BF16)]
        qxb = [sB("qx0", [128, 1024], BF16), sB("qx1", [128, 1024], BF16)]
        rgb = [sB("rg0", [128, 1024], BF16), sB("rg1", [128, 1024], BF16)]
        kdT, qxT = sB("kdT", [128, 8, 128], BF16), sB("qxT", [128, 8, 128], BF16)
        AT, osb, osq, oss = sB("AT", [128, 4, 128], BF16), sB("osb", [128, 4, 256]), sB("osq", [128, 4, 256]), sB("oss", [128, 4])
        retb = [sB("retb0", [128, 1024], BF16), sB("retb1", [128, 1024], BF16)]
        gn_t, mT, rmask, cmask = sB("gn_t", [128, 1024]), sB("mT", [128, 2, 128]), sB("rmask", [128, 16]), sB("cmask", [128, 16, 128], BF16)
        kdm = [sB("kdm0", [128, 1024], BF16), sB("kdm1", [128, 1024], BF16)]
        qxm = [sB("qxm0", [128, 8, 128], BF16), sB("qxm1", [128, 8, 128], BF16)]
        S0 = [sB("S0a", [128, 8, 256]), sB("S0b", [128, 8, 256])]
        S0bf = [sB("S0bfa", [128, 8, 256], BF16), sB("S0bfb", [128, 8, 256], BF16)]
        S.dma('sp', gn_t, ret_gn.rearrange("h e -> (h e)").partition_broadcast(128), writes=['gn_t'])
        S.dma('sp', mT, mT_d, writes=['mT'])
        S.dma('sp', rmask, rmask_d, writes=['rmask'])
        S.dma('sp', cmask, cmask_d, writes=['cmask'])
        S.op('pool', lambda en: en.memset(S32, 0.0), writes=['S32'])
        S.op('pool', lambda en: en.memset(Sbf, 0.0), writes=['Sbf'])

        def transpose8(src, dst, bank_i, rkey, wkey):
            pb = banks[bank_i].bitcast(BF16)
            for k in range(8):
                S.op('pe', lambda en, k=k: en.transpose(out=pb[:, 128 * k:128 * (k + 1)],
                                                        in_=src[:, 128 * k:128 * (k + 1)], identity=ident),
                     reads=[rkey, 'ident'], writes=[('bank', bank_i)])
            S.op('act', lambda en: en.copy(out=dst, in_=pb.rearrange("p (k n) -> p k n", k=8)),
                 reads=[('bank', bank_i)], writes=[wkey])

        def intra(i, mi):
            for hh in range(4):
                for c in range(2):
                    S.op('pe', lambda en, hh=hh, c=c: en.matmul(
                        banks[4][:, 128 * hh:128 * hh + 128], lhsT=kdT[:, 2 * hh + c, :], rhs=qxT[:, 2 * hh + c, :],
                        start=(c == 0), stop=(c == 1)), reads=['kdT', 'qxT'], writes=[('bank', 4)])
            S.op('dve', lambda en: en.tensor_tensor(
                out=AT, in0=banks[4].rearrange("p (h n) -> p h n", h=4),
                in1=mT[:, mi:mi + 1, :].broadcast_to([128, 4, 128]), op=ALU.mult),
                reads=[('bank', 4), 'mT'], writes=['AT'])

        def gate_out(t, i):
            S.op('act', lambda en: en.copy(out=osb[:, 0:2, :], in_=banks[2].rearrange("p (h e) -> p h e", h=2)),
                 reads=[('bank', 2)], writes=['osb'])
            S.op('act', lambda en: en.copy(out=osb[:, 2:4, :], in_=banks[3].rearrange("p (h e) -> p h e", h=2)),
                 reads=[('bank', 3)], writes=['osb'])
            S.op('dve', lambda en: en.tensor_tensor(out=osq, in0=osb, in1=osb, op=ALU.mult), reads=['osb'], writes=['osq'])
            S.op('dve', lambda en: en.tensor_reduce(out=oss, in_=osq, axis=AX.X, op=ALU.add), reads=['osq'], writes=['oss'])
            S.op('act', lambda en: en.activation(out=oss, in_=oss, func=AF.Sqrt, scale=1.0 / 256, bias=epsb),
                 reads=['oss', 'epsb'], writes=['oss'])
            S.op('dve', lambda en: en.reciprocal(out=oss, in_=oss), reads=['oss'], writes=['oss'])
            S.op('dve', lambda en: en.tensor_tensor(
                out=osb, in0=osb, in1=oss.rearrange("p (h o) -> p h o", o=1).broadcast_to([128, 4, 256]), op=ALU.mult),
                reads=['oss', 'osb'], writes=['osb'])
            S.op('dve', lambda en: en.tensor_tensor(out=osb, in0=osb, in1=gn_t.rearrange("p (h e) -> p h e", h=4), op=ALU.mult),
                 reads=['gn_t', 'osb'], writes=['osb'])
            ri = t % 2
            S.op('dve', lambda en, ri=ri, i=i: en.tensor_tensor(
                out=retb[ri].rearrange("p (h e) -> p h e", h=4), in0=osb,
                in1=rgb[i].rearrange("p (h e) -> p h e", h=4), op=ALU.mult),
                reads=['osb', ('rgb', i)], writes=[('retb', ri)])
            S.dma('sp', mix_s[t, :, 0:1024], retb[ri], reads=[('retb', ri)], writes=[('mix_s', t, 0)])

        def state_psum(lhs, lkeys, rvt, rkey):
            for hh in range(4):
                bk = (5, 6, 7, 4)[hh]
                for c in range(2):
                    S.op('pe', lambda en, hh=hh, c=c, bk=bk: en.matmul(
                        banks[bk][:, 256 * c:256 * c + 256], lhsT=lhs[:, 256 * hh + 128 * c:256 * hh + 128 * c + 128],
                        rhs=rvt[:, 256 * hh:256 * hh + 256], start=True, stop=True),
                        reads=lkeys + [rkey], writes=[('bank', bk)])

        for t in range(16):
            i = t % 2
            S.dma('sp', kdb[i], kd_s[t], reads=[('kd_s', t, 0), ('kd_s', t, 1)], writes=[('kdb', i)])
            S.dma('sp', rvb[i], rv_s[t], reads=[('rv_s', t, 0), ('rv_s', t, 1)], writes=[('rvb', i)])
            if t >= 7:
                S.dma('sp', qxb[i], qx_s[t], reads=[('qx_s', t, 0), ('qx_s', t, 1)], writes=[('qxb', i)])
                S.dma('sp', rgb[i], rg_s[t], reads=[('rg_s', t, 0), ('rg_s', t, 1)], writes=[('rgb', i)])
                transpose8(kdb[i], kdT, 0, ('kdb', i), 'kdT')
                transpose8(qxb[i], qxT, 1, ('qxb', i), 'qxT')
                intra(i, 0)
                for hh in range(4):
                    ob = banks[2 + hh // 2][:, 256 * (hh % 2):256 * (hh % 2) + 256]
                    S.op('pe', lambda en, hh=hh, ob=ob, i=i: en.matmul(ob, lhsT=AT[:, hh, :], rhs=rvb[i][:, 256 * hh:256 * hh + 256],
                                                                 start=True, stop=False),
                         reads=['AT', ('rvb', i)], writes=[('bank', 2 + hh // 2)])
                    for c in range(2):
                        S.op('pe', lambda en, hh=hh, c=c, ob=ob: en.matmul(ob, lhsT=qxT[:, 2 * hh + c, :], rhs=Sbf[:, 2 * hh + c, :],
                                                                        start=False, stop=(c == 1)),
                             reads=['qxT', 'Sbf'], writes=[('bank', 2 + hh // 2)])
                gate_out(t, i)
            state_psum(kdb[i], [('kdb', i)], rvb[i], ('rvb', i))
            for hh in range(4):
                bk = (5, 6, 7, 4)[hh]
                sv = S32[:, 2 * hh:2 * hh + 2, :]
                S.op('dve', lambda en, sv=sv, bk=bk: en.tensor_tensor(
                    out=sv, in0=sv, in1=banks[bk].rearrange("p (c e) -> p c e", c=2), op=ALU.add),
                    reads=[('bank', bk), 'S32'], writes=['S32'])
                S.op('act', lambda en, sv=sv, hh=hh: en.activation(out=sv, in_=sv, func=AF.Copy, scale=float(GAM[hh] ** 128)),
                     reads=['S32'], writes=['S32'])
            S.op('act', lambda en: en.copy(out=Sbf, in_=S32), reads=['S32'], writes=['Sbf'])
        S.dma('sp', ret_p_out.rearrange("h (c p) e -> p h c e", p=128),
              S32.rearrange("p (h c) e -> p h c e", c=2), reads=['S32'], writes=['ret_p_out'])

        t = 16
        S.dma('sp', kdb[0], kd_s[t], reads=[('kd_s', t, 0), ('kd_s', t, 1)], writes=[('kdb', 0)])
        S.dma('sp', rvb[0], rv_s[t], reads=[('rv_s', t, 0), ('rv_s', t, 1)], writes=[('rvb', 0)])
        S.dma('sp', qxb[0], qx_s[t], reads=[('qx_s', t, 0), ('qx_s', t, 1)], writes=[('qxb', 0)])
        S.dma('sp', rgb[0], rg_s[t], reads=[('rg_s', t, 0), ('rg_s', t, 1)], writes=[('rgb', 0)])
        transpose8(kdb[0], kdT, 0, ('kdb', 0), 'kdT')
        transpose8(qxb[0], qxT, 1, ('qxb', 0), 'qxT')
        intra(0, 1)
        for hh in range(4):
            ob = banks[2 + hh // 2][:, 256 * (hh % 2):256 * (hh % 2) + 256]
            S.op('pe', lambda en, hh=hh, ob=ob: en.matmul(ob, lhsT=AT[:, hh, :], rhs=rvb[0][:, 256 * hh:256 * hh + 256],
                                                         start=(hh % 2 == 0), stop=False, skip_group_check=True),
                 reads=['AT', ('rvb', 0)], writes=[('bank', 2 + hh // 2)])
        for sq_i in range(16):
            i = sq_i % 2
            S.dma('sp', S0[i].rearrange("p (h c) e -> p h c e", c=2),
                  state_ret_d[sq_i].rearrange("h (c p) e -> p h c e", p=128), writes=[('S0', i)])
            S.op('act', lambda en, i=i: en.copy(out=S0bf[i], in_=S0[i]), reads=[('S0', i)], writes=[('S0bf', i)])
            S.op('dve', lambda en, i=i, sq_i=sq_i: en.tensor_tensor(
                out=qxm[i], in0=qxT, in1=cmask[:, sq_i:sq_i + 1, :].broadcast_to([128, 8, 128]), op=ALU.mult),
                reads=['qxT', 'cmask'], writes=[('qxm', i)])
            for hh in range(4):
                ob = banks[2 + hh // 2][:, 256 * (hh % 2):256 * (hh % 2) + 256]
                for c in range(2):
                    S.op('pe', lambda en, hh=hh, c=c, ob=ob, i=i, sq_i=sq_i: en.matmul(
                        ob, lhsT=qxm[i][:, 2 * hh + c, :], rhs=S0bf[i][:, 2 * hh + c, :],
                        start=False, stop=(sq_i == 15 and c == 1), skip_group_check=True),
                        reads=[('qxm', i), ('S0bf', i)], writes=[('bank', 2 + hh // 2)])
            S.op('dve', lambda en, i=i, sq_i=sq_i: en.tensor_scalar(
                out=kdm[i], in0=kdb[0], scalar1=rmask[:, sq_i:sq_i + 1], scalar2=None, op0=ALU.mult),
                reads=[('kdb', 0), 'rmask'], writes=[('kdm', i)])
            state_psum(kdm[i], [('kdm', i)], rvb[0], ('rvb', 0))
            for hh in range(4):
                bk = (5, 6, 7, 4)[hh]
                sv = S0[i][:, 2 * hh:2 * hh + 2, :]
                S.op('dve', lambda en, sv=sv, bk=bk: en.tensor_tensor(
                    out=sv, in0=sv, in1=banks[bk].rearrange("p (c e) -> p c e", c=2), op=ALU.add),
                    reads=[('bank', bk), ('S0', i)], writes=[('S0', i)])
                S.op('act', lambda en, sv=sv, hh=hh: en.activation(out=sv, in_=sv, func=AF.Copy, scale=float(GAM[hh] ** 8)),
                     reads=[('S0', i)], writes=[('S0', i)])
            S.dma('sp', ret_s_out[sq_i].rearrange("h (c p) e -> p h c e", p=128),
                  S0[i].rearrange("p (h c) e -> p h c e", c=2), reads=[('S0', i)], writes=[('ret_s_out', sq_i)])
        gate_out(16, 0)
        S.barrier()

    def stage_C():
        NEG = -30000.0
        with ExitStack() as esC:
            def sC(name, shape, dt=F32):
                return esC.enter_context(nc.sbuf_tensor(name, list(shape), dt)).ap()
            KTc = {c: sC(f"KT{c}", [128, 4, 2048], BF16) for c in (0, 1, 2, 4)}
            Vt = {c: sC(f"V{c}", [128, 16, 4, 65], BF16) for c in (3, 5)}
            kcT = sC("kcT", [128, 4, 128], BF16)
            Vc = sC("Vc", [128, 4, 97], BF16)
            hidT = sC("hidT", [128, 16, 128], BF16)
            E_all = sC("E_all", [32, 16, 128], BF16)
            cm = sC("cm", [128, 2, 128], BF16)
            cmpm = sC("cmpm", [128, 9, 128], BF16)
            covt = sC("covt", [128, 32], F32)
            ABt = sC("ABt", [128, 2, 9, 32], F32)
            ones64 = sC("ones64", [64, 64], BF16)
            kncmp = sC("kncmp", [64, 1], F32)
            b2k = sC("b2k", [64, 1], F32)
            b2v = sC("b2v", [128, 64], F32)
            b1e = sC("b1e", [128, 4], F32)
            kvt = [sC("kvt0", [128, 1536], BF16), sC("kvt1", [128, 1536], BF16)]
            for c in (0, 1, 2, 4):
                S.op('pool', lambda en, c=c: en.memset(KTc[c], 0.0), writes=[('KT', c)])
            for c in (3, 5):
                S.op('pool', lambda en, c=c: en.memset(Vt[c], 1.0), writes=[('V', c)])
            S.op('pool', lambda en: en.memset(kcT, 0.0), writes=['kcT'])
            S.op('pool', lambda en: en.memset(Vc, 0.0), writes=['Vc'])
            S.op('pool', lambda en: en.memset(ones64, 1.0), writes=['ones64'])
            S.dma('sp', E_all, E_d, writes=['E_all'])
            S.dma('sp', cm, cm_d, writes=['cm'])
            S.dma('sp', cmpm, cmpm_d, writes=['cmpm'])
            S.dma('sp', covt, cov_d, writes=['covt'])
            S.dma('sp', ABt, AB_d, writes=['ABt'])
            S.dma('sp', kncmp, k_norm_cmp.rearrange("(d o) -> d o", o=1), writes=['kncmp'])
            S.dma('sp', b2k, cmp_b2[0].rearrange("(d o) -> d o", o=1), writes=['b2k'])
            S.dma('sp', b2v, cmp_b2[1].partition_broadcast(128), writes=['b2v'])
            for g in range(4):
                S.dma('sp', KTc[2][64:74, g, :], kaug_d, writes=[('KT', 2)])
                S.dma('sp', KTc[4][64:74, g, :], kaug_d, writes=[('KT', 4)])
                S.dma('sp', kcT[64:74, g, :], kaugc_d, writes=['kcT'])
            for t in range(16):
                i = t % 2
                S.dma('sp', kvt[i], kvs_s[t], reads=[('kvs_s', t, 0), ('kvs_s', t, 1), ('kvs_s', t, 2)], writes=[('kvt', i)])
                for ci, c in enumerate((0, 1, 2, 4)):
                    bk = ci % 2
                    pb = banks[bk].bitcast(BF16)
                    for g in range(4):
                        S.op('pe', lambda en, pb=pb, g=g, c=c, i=i: en.transpose(
                            out=pb[0:64, 128 * g:128 * g + 128], in_=kvt[i][:, 256 * c + 64 * g:256 * c + 64 * g + 64],
                            identity=ident), reads=[('kvt', i), 'ident'], writes=[('bank', bk)])
                    S.op('act', lambda en, pb=pb, c=c, t=t: en.copy(
                        out=KTc[c][0:64, :, 128 * t:128 * t + 128], in_=pb[0:64, 0:512].rearrange("p (g n) -> p g n", g=4)),
                        reads=[('bank', bk)], writes=[('KT', c)])
                for c in (3, 5):
                    S.op('dve', lambda en, c=c, t=t, i=i: en.tensor_copy(
                        out=Vt[c][:, t, :, 0:64], in_=kvt[i][:, 256 * c:256 * c + 256].rearrange("p (g d) -> p g d", g=4)),
                        reads=[('kvt', i)], writes=[('V', c)])
            if stage < 4:
                return
            with ExitStack() as esW:
                w1 = esW.enter_context(nc.sbuf_tensor("w1", [64, 2, 32, 256], BF16)).ap()
                peT = esW.enter_context(nc.sbuf_tensor("peT", [64, 2, 32], BF16)).ap()
                w2s = esW.enter_context(nc.sbuf_tensor("w2s", [128, 2, 2, 64], BF16)).ap()
                kc32 = esW.enter_context(nc.sbuf_tensor("kc32", [64, 128], F32)).ap()
                kcsq = esW.enter_context(nc.sbuf_tensor("kcsq", [64, 128], BF16)).ap()
                krs = esW.enter_context(nc.sbuf_tensor("krs", [64, 128], F32)).ap()
                b1t = esW.enter_context(nc.sbuf_tensor("b1t", [128, 4], F32)).ap()
                S.op('pool', lambda en: en.memset(kc32, 0.0), writes=['kc32'])
                for c in range(2):
                    S.dma('pool', w1[:, c], cmp_w1[c].rearrange("l d h -> d l h"), writes=['w1'])
                S.dma('pool', peT, cmp_pe.rearrange("c l d -> d c l"), writes=['peT'])
                S.dma('pool', w2s, cmp_w2.rearrange("c (k p) d -> p c k d", p=128), writes=['w2s'])
                S.dma('sp', b1t, cmp_b1.rearrange("c (k p) -> p (c k)", p=128), writes=['b1t'])
                for c in range(2):
                    for hc in range(2):
                        for l in range(32):
                            S.op('pe', lambda en, c=c, hc=hc, l=l: en.matmul(
                                banks[7][:, 2 * c + hc:2 * c + hc + 1], lhsT=w1[:, c, l, 128 * hc:128 * hc + 128],
                                rhs=peT[:, c, l:l + 1], start=(l == 0), stop=(l == 31)),
                                reads=['w1', 'peT'], writes=[('bank', 7)])
                S.op('dve', lambda en: en.tensor_tensor(out=b1e, in0=banks[7][:, 0:4], in1=b1t, op=ALU.add),
                     reads=[('bank', 7), 'b1t'], writes=['b1e'])
                for c in range(2):
                    XT = KTc[c]
                    for g in range(4):
                        Xv = XT[0:64, g, :].rearrange("p (n l) -> p l n", l=16)
                        for hc in range(2):
                            bk = 2 + (4 * c + g) % 4
                            for hf in range(2):
                                for l in range(16):
                                    first = (hf == 0 and l == 0)
                                    last = (hf == 1 and l == 15)
                                    S.op('pe', lambda en, c=c, hc=hc, hf=hf, l=l, bk=bk, Xv=Xv, first=first, last=last: en.matmul(
                                        banks[bk][:, 128 * hc:128 * hc + 127], lhsT=w1[:, c, 16 * hf + l, 128 * hc:128 * hc + 128],
                                        rhs=Xv[:, l, hf:hf + 127], start=first, stop=last),
                                        reads=['w1', ('KT', c)], writes=[('bank', bk)])
                            S.op('act', lambda en, c=c, g=g, hc=hc, bk=bk: en.activation(
                                out=hidT[:, 8 * c + 2 * g + hc, 0:127], in_=banks[bk][:, 128 * hc:128 * hc + 127], func=AF.Silu,
                                bias=b1e[:, 2 * c + hc:2 * c + hc + 1]), reads=[('bank', bk), 'b1e'], writes=['hidT'])
                for g in range(4):
                    for hc in range(2):
                        S.op('pe', lambda en, g=g, hc=hc: en.matmul(
                            banks[6][0:64, 0:127], lhsT=w2s[:, 0, hc, :], rhs=hidT[:, 2 * g + hc, 0:127],
                            start=(hc == 0), stop=(hc == 1)), reads=['w2s', 'hidT'], writes=[('bank', 6)])
                    S.op('act', lambda en: en.activation(out=kc32[:, 0:127], in_=banks[6][0:64, 0:127], func=AF.Identity, bias=b2k),
                         reads=[('bank', 6), 'b2k'], writes=['kc32'])
                    S.op('dve', lambda en: en.tensor_tensor(out=kcsq, in0=kc32, in1=kc32, op=ALU.mult), reads=['kc32'], writes=['kcsq'])
                    S.op('pe', lambda en: en.matmul(banks[7][0:64, 0:128], lhsT=ones64, rhs=kcsq, start=True, stop=True),
                         reads=['ones64', 'kcsq'], writes=[('bank', 7)])
                    S.op('act', lambda en: en.activation(out=krs, in_=banks[7][0:64, 0:128], func=AF.Sqrt, scale=1.0 / 64, bias=epsb[0:64]),
                         reads=[('bank', 7), 'epsb'], writes=['krs'])
                    S.op('dve', lambda en: en.reciprocal(out=krs, in_=krs), reads=['krs'], writes=['krs'])
                    S.op('dve', lambda en: en.tensor_tensor(out=kc32, in0=kc32, in1=krs, op=ALU.mult), reads=['kc32', 'krs'], writes=['kc32'])
                    S.op('dve', lambda en, g=g: en.tensor_scalar(out=kcT[0:64, g, 0:127], in0=kc32[:, 0:127], scalar1=kncmp, scalar2=None,
                                                                 op0=ALU.mult), reads=['kc32', 'kncmp'], writes=['kcT'])
                    for hc in range(2):
                        S.op('pe', lambda en, g=g, hc=hc: en.matmul(
                            banks[5][0:127, 0:64], lhsT=hidT[:, 8 + 2 * g + hc, 0:127], rhs=w2s[:, 1, hc, :],
                            start=(hc == 0), stop=(hc == 1)), reads=['w2s', 'hidT'], writes=[('bank', 5)])
                    S.op('dve', lambda en, g=g: en.tensor_tensor(out=Vc[0:127, g, 0:64], in0=banks[5][0:127, 0:64], in1=b2v[0:127],
                                                                 op=ALU.add), reads=[('bank', 5), 'b2v'], writes=['Vc'])
                    S.op('pool', lambda en, g=g: en.memset(Vc[:, g, 64:65], 1.0), writes=['Vc'])
                    S.op('dve', lambda en, g=g: en.tensor_copy(out=Vc[:, g, 65:97], in_=covt), reads=['covt'], writes=['Vc'])
                S.barrier()

            if stage < 5:
                return
            qT = sC("qT", [128, 16, 128], BF16)
            nqt = [sC("nqt0", [128, 1024], BF16), sC("nqt1", [128, 1024], BF16)]
            PT = [sC(f"PT{i}", [128, 4, 128], BF16) for i in range(3)]
            ob = {br: sC(f"ob{br}", [128, 16, 97], F32) for br in range(3)}
            gt = sC("gt", [128, 48], F32)
            imp = sC("imp", [128, 16, 32], F32)
            sc = sC("sc", [128, 4, 32], F32)
            sc2 = sC("sc2", [128, 32], F32)
            mx8 = sC("mx8", [128, 8], F32)
            thr = sC("thr", [128, 1], F32)
            selm = sC("selm", [128, 4, 32], F32)
            negb = sC("negb", [128, 4, 32], BF16)
            negT = sC("negT", [32, 4, 128], BF16)
            wgt = sC("wgt", [128, 16], F32)
            nsa = sC("nsa", [128, 16, 64], F32)
            ntmp = sC("ntmp", [128, 16, 64], F32)
            nsab = [sC("nsab0", [128, 1024], BF16), sC("nsab1", [128, 1024], BF16)]
            S.op('pool', lambda en: en.memset(qT, 0.0), writes=['qT'])
            cst = {'s': 0, 'p': 0, 'o': 0}

            def attend(br, g, qi, qt, kts, Kt, kkey, Vap_fn, vkey, ncol):
                obk = 5 + cst['o'] % 2
                cst['o'] += 1
                pending = [None]
                for n_, kt in enumerate(kts):
                    sb_ = 2 + cst['s'] % 3
                    cst['s'] += 1
                    extra = []
                    if br == 0:
                        extra.append((ident, cmpm[:, qi:qi + 1, :].broadcast_to([128, 4, 128]), ['ident', 'cmpm']))
                    else:
                        if br == 1:
                            extra.append((E_all[:, kt, :], negT[:, g:g + 1, :].broadcast_to([32, 4, 128]), ['E_all', 'negT']))
                        if kt == qt:
                            extra.append((ident, cm[:, 0:1, :].broadcast_to([128, 4, 128]), ['ident', 'cm']))
                        if br == 2 and kt == qt - 4:
                            extra.append((ident, cm[:, 1:2, :].broadcast_to([128, 4, 128]), ['ident', 'cm']))
                    kcols = slice(128 * kt, 128 * kt + 128) if br > 0 else slice(0, 128)
                    warm(sb_)
                    S.op('pe', lambda en, sb_=sb_, g=g, kcols=kcols, last=(len(extra) == 0): en.matmul(
                        banks[sb_], lhsT=Kt[0:74, g, kcols], rhs=qT[0:74, 4 * g:4 * g + 4, :], start=True, stop=last),
                        reads=[kkey, 'qT'], writes=[('bank', sb_)])
                    for ei, (l_, r_, ks) in enumerate(extra):
                        S.op('pe', lambda en, sb_=sb_, l_=l_, r_=r_, last=(ei == len(extra) - 1): en.matmul(
                            banks[sb_], lhsT=l_, rhs=r_, start=False, stop=last), reads=ks, writes=[('bank', sb_)])
                    pi = cst['p'] % 3
                    cst['p'] += 1
                    S.op('act', lambda en, pi=pi, sb_=sb_: en.activation(
                        out=PT[pi], in_=banks[sb_].rearrange("p (r n) -> p r n", r=4), func=AF.Exp),
                        reads=[('bank', sb_)], writes=[('PT', pi)])
                    def emit_pv(pi=pi, kt=kt, n_=n_):
                        for r in range(4):
                            S.op('pe', lambda en, r=r, pi=pi, obk=obk, kt=kt, first=(n_ == 0), last=(n_ == len(kts) - 1): en.matmul(
                                banks[obk][:, ncol * r:ncol * r + ncol], lhsT=PT[pi][:, r, :], rhs=Vap_fn(kt),
                                start=(first and r == 0), stop=last, skip_group_check=True),
                                reads=[('PT', pi), vkey], writes=[('bank', obk)])
                    if pending[0] is not None:
                        pending[0]()
                    pending[0] = emit_pv
                pending[0]()
                pending[0] = None
                S.op('act', lambda en, obk=obk, g=g, br=br: en.copy(
                    out=ob[br][:, 4 * g:4 * g + 4, 0:ncol], in_=banks[obk][:, 0:4 * ncol].rearrange("p (r c) -> p r c", r=4)),
                    reads=[('bank', obk)], writes=[('ob', br)])

            import os
            for qt in range(7, int(os.environ.get('QTL', 16))):
                qi = qt - 7
                i = qt % 2
                S.dma('sp', nqt[i], nq_s[qt], reads=[('nq_s', qt, 0), ('nq_s', qt, 1)], writes=[('nqt', i)])
                S.dma('sp', gt, gate_s[qt], reads=[('gate_s', qt)], writes=['gt'])
                S.dma('sp', qT[64:74, :, :], qaug_d[:, :, 128 * qi:128 * qi + 128], writes=['qT'])
                for half in range(2):
                    pb = banks[half].bitcast(BF16)
                    for hh in range(8):
                        hd = 8 * half + hh
                        S.op('pe', lambda en, pb=pb, hh=hh, hd=hd, i=i: en.transpose(
                            out=pb[0:64, 128 * hh:128 * hh + 128], in_=nqt[i][:, 64 * hd:64 * hd + 64], identity=ident),
                            reads=[('nqt', i), 'ident'], writes=[('bank', half)])
                    S.op('act', lambda en, pb=pb, half=half: en.copy(
                        out=qT[0:64, 8 * half:8 * half + 8, :], in_=pb[0:64, :].rearrange("p (h n) -> p h n", h=8)),
                        reads=[('bank', half)], writes=['qT'])
                for g in range(4):
                    attend(0, g, qi, qt, [0], kcT, 'kcT', lambda kt, g=g: Vc[:, g, :], 'Vc', 97)
                S.op('dve', lambda en: en.tensor_scalar(out=wgt, in0=ob[0][:, :, 64], scalar1=1e-30, scalar2=None, op0=ALU.add),
                     reads=[('ob', 0)], writes=['wgt'])
                S.op('dve', lambda en: en.reciprocal(out=wgt, in_=wgt), reads=['wgt'], writes=['wgt'])
                S.op('dve', lambda en: en.tensor_tensor(
                    out=imp, in0=ob[0][:, :, 65:97], in1=wgt.rearrange("p (h o) -> p h o", o=1).broadcast_to([128, 16, 32]),
                    op=ALU.mult), reads=[('ob', 0), 'wgt'], writes=['imp'])
                S.op('dve', lambda en: en.tensor_reduce(
                    out=sc, in_=imp.rearrange("p (g r) j -> p g j r", g=4), axis=AX.X, op=ALU.add), reads=['imp'], writes=['sc'])
                S.op('dve', lambda en, qi=qi: en.tensor_tensor(
                    out=sc, in0=sc, in1=ABt[:, 0, qi:qi + 1, :].broadcast_to([128, 4, 32]), op=ALU.mult),
                    reads=['sc', 'ABt'], writes=['sc'])
                S.op('dve', lambda en, qi=qi: en.tensor_tensor(
                    out=sc, in0=sc, in1=ABt[:, 1, qi:qi + 1, :].broadcast_to([128, 4, 32]), op=ALU.add),
                    reads=['sc', 'ABt'], writes=['sc'])
                for g in range(4):
                    S.op('dve', lambda en, g=g: en.max(out=mx8, in_=sc[:, g, :]), reads=['sc'], writes=['mx8'])
                    S.op('dve', lambda en, g=g: en.match_replace(out=sc2, in_to_replace=mx8, in_values=sc[:, g, :], imm_value=-3.0e4),
                         reads=['sc', 'mx8'], writes=['sc2'])
                    S.op('dve', lambda en: en.max(out=mx8, in_=sc2), reads=['sc2'], writes=['mx8'])
                    S.op('dve', lambda en: en.tensor_reduce(out=thr, in_=mx8, axis=AX.X, op=ALU.min), reads=['mx8'], writes=['thr'])
                    S.op('dve', lambda en, g=g: en.tensor_scalar(out=selm[:, g, :], in0=sc[:, g, :], scalar1=thr, scalar2=None,
                                                                 op0=ALU.is_ge), reads=['sc', 'thr'], writes=['selm'])
                S.op('dve', lambda en: en.tensor_scalar(out=sc, in0=sc, scalar1=-5000.0, scalar2=None, op0=ALU.is_gt),
                     reads=['sc'], writes=['sc'])
                S.op('dve', lambda en: en.tensor_tensor(out=selm, in0=selm, in1=sc, op=ALU.mult), reads=['sc', 'selm'], writes=['selm'])
                S.op('dve', lambda en: en.tensor_scalar(out=negb, in0=selm, scalar1=-1.0, scalar2=-NEG, op0=ALU.add, op1=ALU.mult),
                     reads=['selm'], writes=['negb'])
                pb7 = banks[7].bitcast(BF16)
                for g in range(4):
                    S.op('pe', lambda en, g=g: en.transpose(out=pb7[0:32, 128 * g:128 * g + 128], in_=negb[:, g, :], identity=ident),
                         reads=['negb', 'ident'], writes=[('bank', 7)])
                S.op('act', lambda en: en.copy(out=negT, in_=pb7[0:32, 0:512].rearrange("p (g n) -> p g n", g=4)),
                     reads=[('bank', 7)], writes=['negT'])
                if stage < 6:
                    continue
                for g in range(4):
                    attend(1, g, qi, qt, list(range(0, qt + 1)), KTc[2], ('KT', 2), lambda kt, g=g: Vt[3][:, kt, g, :], ('V', 3), 65)
                for g in range(4):
                    attend(2, g, qi, qt, list(range(max(0, qt - 4), qt + 1)), KTc[4], ('KT', 4),
                           lambda kt, g=g: Vt[5][:, kt, g, :], ('V', 5), 65)
                for br in range(3):
                    S.op('dve', lambda en, br=br: en.tensor_scalar(out=wgt, in0=ob[br][:, :, 64], scalar1=1e-30, scalar2=None, op0=ALU.add),
                         reads=[('ob', br)], writes=['wgt'])
                    S.op('dve', lambda en: en.reciprocal(out=wgt, in_=wgt), reads=['wgt'], writes=['wgt'])
                    S.op('dve', lambda en, br=br: en.tensor_tensor(
                        out=wgt, in0=wgt, in1=gt.rearrange("p (h b) -> p h b", b=3)[:, :, br], op=ALU.mult),
                        reads=['wgt', 'gt'], writes=['wgt'])
                    dst = nsa if br == 0 else ntmp
                    S.op('dve', lambda en, br=br, dst=dst: en.tensor_tensor(
                        out=dst, in0=ob[br][:, :, 0:64], in1=wgt.rearrange("p (h o) -> p h o", o=1).broadcast_to([128, 16, 64]),
                        op=ALU.mult), reads=[('ob', br), 'wgt'], writes=['nsa' if br == 0 else 'ntmp'])
                    if br > 0:
                        S.op('dve', lambda en: en.tensor_tensor(out=nsa, in0=nsa, in1=ntmp, op=ALU.add),
                             reads=['nsa', 'ntmp'], writes=['nsa'])
                S.op('act', lambda en, i=i: en.copy(out=nsab[i], in_=nsa.rearrange("p h d -> p (h d)")),
                     reads=['nsa'], writes=[('nsab', i)])
                S.dma('sp', mix_s[qt, :, 1024:2048], nsab[i], reads=[('nsab', i)], writes=[('mix_s', qt, 1)])
            S.barrier()


    import os
    if stage >= 3 and not os.environ.get('SKIPC'):
        stage_C()
    def stage_D():
        ckvu = cache_kv
        with ExitStack() as esD:
            def sD(name, shape, dt=F32):
                return esD.enter_context(nc.sbuf_tensor(name, list(shape), dt)).ap()
            ptb_i = sD("ptb_i", [128, 32], I32)
            ptf = sD("ptf", [128, 32])
            idx_i = idx_i_perm
            upt = sD("upt", [128, 1])
            for hf in range(2):
                srcap = page_table_d[:, 8 * hf:8 * hf + 8].rearrange("s (k o) -> k o s", o=1).broadcast_to([8, 16, 16])
                for k_ in range(8):
                    S.dma('sp', ptb_i[16 * k_:16 * k_ + 16, 16 * hf:16 * hf + 16],
                          page_table_d[:, 8 * hf + k_:8 * hf + k_ + 1].rearrange("s o -> o s").broadcast_to([16, 16]),
                          writes=['ptb_i'])
            S.dma('sp', upt, upt_d, writes=['upt'])
            S.op('dve', lambda en: en.tensor_copy(out=ptf, in_=ptb_i), reads=['ptb_i'], writes=['ptf'])
            S.op('dve', lambda en: en.tensor_scalar(out=ptf, in0=ptf, scalar1=16.0, scalar2=upt, op0=ALU.mult, op1=ALU.add),
                 reads=['ptf', 'upt'], writes=['ptf'])
            S.op('dve', lambda en: en.tensor_copy(out=idx_i[:, 0:32], in_=ptf), reads=['ptf'], writes=['idx_i'])
            E_s = sD("E_s", [33, 17, 128], BF16)
            cmS = sD("cmS", [128, 2, 128], BF16)
            covS = sD("covS", [128, 33])
            ABs = sD("ABs", [128, 2, 33])
            Rsum = sD("Rsum", [32, 16, 128], BF16)
            ones64 = sD("ones64d", [64, 64], BF16)
            kncmp, b2k, b2v, b1e = sD("kncmpd", [64, 1]), sD("b2kd", [64, 1]), sD("b2vd", [128, 64]), sD("b1ed", [128, 4])
            S.dma('sp', E_s, Es_d, writes=['E_s'])
            S.dma('sp', cmS, cmS_d, writes=['cmS'])
            S.dma('sp', covS, covS_d, writes=['covS'])
            S.dma('sp', ABs, ABs_d, writes=['ABs'])
            S.dma('sp', Rsum, Rsum_d, writes=['Rsum'])
            S.op('pool', lambda en: en.memset(ones64, 1.0), writes=['ones64'])
            S.dma('sp', kncmp, k_norm_cmp.rearrange("(d o) -> d o", o=1), writes=['kncmp'])
            S.dma('sp', b2k, cmp_b2[0].rearrange("(d o) -> d o", o=1), writes=['b2k'])
            S.dma('sp', b2v, cmp_b2[1].partition_broadcast(128), writes=['b2v'])
            qTs = [sD("qTs0", [128, 16, 128], BF16), sD("qTs1", [128, 16, 128], BF16)]
            nqt = sD("nqtd", [128, 1024], BF16)
            S.dma('sp', nqt, nq_s[16], reads=[('nq_s', 16, 0), ('nq_s', 16, 1)], writes=['nqt'])
            for v in range(2):
                S.op('pool', lambda en, v=v: en.memset(qTs[v], 0.0), writes=[('qTs', v)])
                S.dma('sp', qTs[v][64:74, :, :], qaugs_d[v], writes=[('qTs', v)])
            for half in range(2):
                pb = banks[half].bitcast(BF16)
                for hh in range(8):
                    hd = 8 * half + hh
                    S.op('pe', lambda en, pb=pb, hh=hh, hd=hd: en.transpose(
                        out=pb[0:64, 128 * hh:128 * hh + 128], in_=nqt[:, 64 * hd:64 * hd + 64], identity=ident),
                        reads=['nqt', 'ident'], writes=[('bank', half)])
                for v in range(2):
                    S.op('act', lambda en, pb=pb, half=half, v=v: en.copy(
                        out=qTs[v][0:64, 8 * half:8 * half + 8, :], in_=pb[0:64, :].rearrange("p (h n) -> p h n", h=8)),
                        reads=[('bank', half)], writes=[('qTs', v)])
            G = [sD("G0", [128, 8, 1024], BF16), sD("G1", [128, 8, 1024], BF16)]
            hb16 = [sD(f"hb16{i}", [128, 1024], BF16) for i in range(2)]
            cwt = hist_all
            XT = [sD("XTk", [64, 4, 2048], BF16), sD("XTv", [64, 4, 2048], BF16)]
            w1 = sD("w1d", [64, 2, 32, 256], BF16)
            peT = sD("peTd", [64, 2, 32], BF16)
            w2s = sD("w2sd", [128, 2, 2, 64], BF16)
            b1t = sD("b1td", [128, 4])
            hidT = sD("hidTd", [128, 4, 512], BF16)
            kcT = sD("kcTd", [128, 4, 128], BF16)
            Vc = sD("Vcd", [128, 4, 98], BF16)
            kc32 = sD("kc32d", [64, 512])
            kcsq = sD("kcsqd", [64, 512], BF16)
            krs = sD("krsd", [64, 512])
            rden = sD("rdend", [32, 4])
            Unb = sD("Unbd", [32, 4, 33], BF16)
            KS = sD("KS", [128, 4, 2176], BF16)
            KW = sD("KW", [128, 4, 640], BF16)
            VS = sD("VS", [128, 17, 4, 65], BF16)
            VW = sD("VW", [128, 5, 4, 65], BF16)
            newt = sD("newt", [128, 1536], BF16)
            obr = [sD("ocsd", [32, 4, 98]), sD("ossd", [32, 4, 65]), sD("owsd", [32, 4, 65])]
            PT = [sD(f"PTd{i}", [128, 128], BF16) for i in range(3)]
            sc = sD("scd", [128, 4, 33])
            sc2 = sD("sc2d", [128, 33])
            mx8 = sD("mx8d", [128, 8])
            thr = sD("thrd", [128, 1])
            selm = sD("selmd", [128, 4, 33])
            negb = sD("negbd", [128, 4, 33], BF16)
            negT = sD("negTd", [33, 4, 128], BF16)
            negTx = sD("negTxd", [33, 4, 4, 8], BF16)
            cst = {'h': 0, 's': 0, 'p': 0}
            for c in range(2):
                S.dma('pool', w1[:, c], cmp_w1[c].rearrange("l d h -> d l h"), writes=['w1'])
            S.dma('pool', peT, cmp_pe.rearrange("c l d -> d c l"), writes=['peT'])
            S.dma('pool', w2s, cmp_w2.rearrange("c (k p) d -> p c k d", p=128), writes=['w2s'])
            S.dma('sp', b1t, cmp_b1.rearrange("c (k p) -> p (c k)", p=128), writes=['b1t'])
            for buf, key, val in ((kcT, 'kcT', 0.0), (Vc, 'Vc', 0.0), (kc32, 'kc32', 0.0), (hidT, 'hidT', 0.0), (KS, 'KS', 0.0),
                                  (KW, 'KW', 0.0), (VS, 'VS', 1.0), (VW, 'VW', 1.0), (newt, 'newt', 0.0)):
                S.op('pool', lambda en, buf=buf, val=val: en.memset(buf, val), writes=[key])
            for g in range(4):
                S.dma('sp', kcT[64:74, g, :], kaugcS_d, writes=['kcT'])
                S.op('pool', lambda en, g=g: en.memset(Vc[:, g, 64:65], 1.0), writes=['Vc'])
                S.op('dve', lambda en, g=g: en.tensor_copy(out=Vc[:, g, 65:98], in_=covS), reads=['covS'], writes=['Vc'])
                S.dma('sp', KS[64:74, g, :], kaugS_d, writes=['KS'])
                S.dma('sp', KW[64:74, g, :], kaugW_d, writes=['KW'])
            for c in range(2):
                for hc in range(2):
                    for l in range(32):
                        S.op('pe', lambda en, c=c, hc=hc, l=l: en.matmul(
                            banks[7][:, 2 * c + hc:2 * c + hc + 1], lhsT=w1[:, c, l, 128 * hc:128 * hc + 128],
                            rhs=peT[:, c, l:l + 1], start=(l == 0), stop=(l == 31)),
                            reads=['w1', 'peT'], writes=[('bank', 7)])
            S.op('dve', lambda en: en.tensor_tensor(out=b1e, in0=banks[7][:, 0:4], in1=b1t, op=ALU.add),
                 reads=[('bank', 7), 'b1t'], writes=['b1e'])

            def transpose_g(src, skey, c0, dst_fn, dkey):
                bk = cst['s'] % 5
                cst['s'] += 1
                warm(bk)
                pb = banks[bk].bitcast(BF16)
                for g in range(4):
                    S.op('pe', lambda en, pb=pb, g=g: en.transpose(
                        out=pb[0:64, 128 * g:128 * g + 128], in_=src[:, c0 + 64 * g:c0 + 64 * g + 64], identity=ident),
                        reads=[skey, 'ident'], writes=[('bank', bk)])
                S.op('act', lambda en, pb=pb: en.copy(out=dst_fn(), in_=pb[0:64, 0:512].rearrange("p (g n) -> p g n", g=4)),
                     reads=[('bank', bk)], writes=[dkey])

            def scores(bank_i, Kt, kkey, kcols, qv, sq_i, extra):
                warm(bank_i)
                for g in range(4):
                    S.op('pe', lambda en, g=g, last=(g == 3 and not extra): en.matmul(
                        banks[bank_i][:, 32 * g:32 * g + 32], lhsT=Kt[0:74, g, kcols],
                        rhs=qTs[qv][0:74, 4 * g:4 * g + 4, 8 * sq_i:8 * sq_i + 8],
                        start=(g == 0), stop=last, skip_group_check=True),
                        reads=[kkey, ('qTs', qv)], writes=[('bank', bank_i)])
                for ei, (l_, r_, ks) in enumerate(extra):
                    S.op('pe', lambda en, l_=l_, r_=r_, last=(ei == len(extra) - 1): en.matmul(
                        banks[bank_i][:, 0:128], lhsT=l_, rhs=r_, start=False, stop=last, skip_group_check=True),
                        reads=ks, writes=[('bank', bank_i)])
                pi = cst['p'] % 3
                cst['p'] += 1
                S.op('act', lambda en, pi=pi: en.activation(out=PT[pi], in_=banks[bank_i][:, 0:128], func=AF.Exp),
                     reads=[('bank', bank_i)], writes=[('PT', pi)])
                return pi

            def pv(obk, pi, Vfn, vkey, ncol, first, last):
                for g in range(4):
                    S.op('pe', lambda en, g=g: en.matmul(
                        banks[obk][0:32, ncol * g:ncol * g + ncol], lhsT=PT[pi][:, 32 * g:32 * g + 32], rhs=Vfn(g),
                        start=(first and g == 0), stop=last, skip_group_check=True),
                        reads=[('PT', pi), vkey], writes=[('bank', obk)])

            def store_branch(br, sq_i, obk, ncol):
                S.op('act', lambda en: en.copy(out=obr[br], in_=banks[obk][0:32, 0:4 * ncol].rearrange("p (g c) -> p g c", g=4)),
                     reads=[('bank', obk)], writes=[('obr', br)])
                for r in range(4):
                    S.dma('sp', osc_s[br][sq_i, :, :, r, :], obr[br][8 * r:8 * r + 8, :, 0:65],
                          reads=[('obr', br)], writes=[('osc_s', br)])

            for sq_i in range(int(os.environ.get('SEQL', 16))):
                for hf in range(2):
                    S.dma_gather(G[hf].rearrange("p j c -> p (j c)"), ckvu, idx_i[:, 16 * hf + sq_i:16 * hf + sq_i + 1], 2560 * 16,
                                 reads=['idx_i'], writes=[('G', hf)])
                    for j in range(8):
                        kt = 2 * j + hf
                        Gj = G[hf][:, j, :]
                        for c in range(2):
                            transpose_g(Gj, ('G', hf), 256 * c,
                                        lambda c=c, kt=kt: XT[c][0:64, :, 128 * kt:128 * kt + 128], ('XT', c))
                        transpose_g(Gj, ('G', hf), 512, lambda kt=kt: KS[0:64, :, 128 * kt:128 * kt + 128], 'KS')
                        S.op('dve', lambda en, Gj=Gj, kt=kt: en.tensor_copy(
                            out=VS[:, kt, :, 0:64], in_=Gj[:, 768:1024].rearrange("p (g d) -> p g d", g=4)),
                            reads=[('G', hf)], writes=['VS'])
                for c in range(2):
                    Xv = XT[c].rearrange("p g (j n two) -> p g j two n", j=8, two=2)
                    for hc in range(2):
                        bk = 2 + (2 * c + hc) % 3
                        for hf in range(2):
                            for l in range(16):
                                S.op('pe', lambda en, c=c, hc=hc, hf=hf, l=l, bk=bk, Xv=Xv: en.matmul(
                                    banks[bk][:, 0:508].rearrange("p (g n) -> p g n", g=4),
                                    lhsT=w1[:, c, 16 * hf + l, 128 * hc:128 * hc + 128],
                                    rhs=Xv[:, :, l % 8, l // 8, hf:hf + 127], start=(hf == 0 and l == 0), stop=(hf == 1 and l == 15)),
                                    reads=['w1', ('XT', c)], writes=[('bank', bk)])
                        S.op('act', lambda en, c=c, hc=hc, bk=bk: en.activation(
                            out=hidT[:, 2 * c + hc, 0:508], in_=banks[bk][:, 0:508], func=AF.Silu,
                            bias=b1e[:, 2 * c + hc:2 * c + hc + 1]), reads=[('bank', bk), 'b1e'], writes=['hidT'])
                for hc in range(2):
                    S.op('pe', lambda en, hc=hc: en.matmul(banks[6][0:64, 0:508], lhsT=w2s[:, 0, hc, :], rhs=hidT[:, hc, 0:508],
                                                           start=(hc == 0), stop=(hc == 1)),
                         reads=['w2s', 'hidT'], writes=[('bank', 6)])
                S.op('act', lambda en: en.activation(out=kc32[:, 0:508], in_=banks[6][0:64, 0:508], func=AF.Identity, bias=b2k),
                     reads=[('bank', 6), 'b2k'], writes=['kc32'])
                S.op('dve', lambda en: en.tensor_tensor(out=kcsq, in0=kc32, in1=kc32, op=ALU.mult), reads=['kc32'], writes=['kcsq'])
                S.op('pe', lambda en: en.matmul(banks[7][0:64, 0:512], lhsT=ones64, rhs=kcsq, start=True, stop=True),
                     reads=['ones64', 'kcsq'], writes=[('bank', 7)])
                S.op('act', lambda en: en.activation(out=krs, in_=banks[7][0:64, 0:512], func=AF.Sqrt, scale=1.0 / 64, bias=epsb[0:64]),
                     reads=[('bank', 7), 'epsb'], writes=['krs'])
                S.op('dve', lambda en: en.reciprocal(out=krs, in_=krs), reads=['krs'], writes=['krs'])
                S.op('dve', lambda en: en.tensor_tensor(out=kc32, in0=kc32, in1=krs, op=ALU.mult), reads=['kc32', 'krs'], writes=['kc32'])
                S.op('dve', lambda en: en.tensor_scalar(
                    out=kcT[0:64, :, 0:127], in0=kc32[:, 0:508].rearrange("p (g n) -> p g n", g=4), scalar1=kncmp, scalar2=None,
                    op0=ALU.mult), reads=['kc32', 'kncmp'], writes=['kcT'])
                for g in range(4):
                    for hc in range(2):
                        S.op('pe', lambda en, g=g, hc=hc: en.matmul(
                            banks[5][0:127, 64 * g:64 * g + 64], lhsT=hidT[:, 2 + hc, 127 * g:127 * g + 127], rhs=w2s[:, 1, hc, :],
                            start=(g == 0 and hc == 0), stop=(hc == 1), skip_group_check=True),
                            reads=['w2s', 'hidT'], writes=[('bank', 5)])
                S.op('dve', lambda en: en.tensor_tensor(
                    out=Vc[0:127, :, 0:64], in0=banks[5][0:127, 0:256].rearrange("p (g d) -> p g d", g=4),
                    in1=b2v[0:127].rearrange("p (o d) -> p o d", o=1).broadcast_to([127, 4, 64]), op=ALU.add),
                    reads=[('bank', 5), 'b2v'], writes=['Vc'])
                pi = scores(2, kcT, 'kcT', slice(0, 128), 0, sq_i, [])
                pv(3, pi, lambda g: Vc[:, g, :], 'Vc', 98, True, True)
                store_branch(0, sq_i, 3, 98)
                S.op('dve', lambda en: en.tensor_scalar(out=rden, in0=obr[0][:, :, 64], scalar1=1e-30, scalar2=None, op0=ALU.add),
                     reads=[('obr', 0)], writes=['rden'])
                S.op('dve', lambda en: en.reciprocal(out=rden, in_=rden), reads=['rden'], writes=['rden'])
                S.op('dve', lambda en: en.tensor_tensor(
                    out=Unb, in0=obr[0][:, :, 65:98], in1=rden.rearrange("p (g o) -> p g o", o=1).broadcast_to([32, 4, 33]),
                    op=ALU.mult), reads=[('obr', 0), 'rden'], writes=['Unb'])
                S.op('pe', lambda en, sq_i=sq_i: en.matmul(
                    banks[4][:, 0:132], lhsT=Rsum[:, sq_i, :], rhs=Unb.rearrange("p g j -> p (g j)"), start=True, stop=True),
                    reads=['Rsum', 'Unb'], writes=[('bank', 4)])
                S.op('dve', lambda en: en.tensor_tensor(
                    out=sc, in0=banks[4][:, 0:132].rearrange("p (g j) -> p g j", g=4),
                    in1=ABs[:, 0:1, :].broadcast_to([128, 4, 33]), op=ALU.mult), reads=[('bank', 4), 'ABs'], writes=['sc'])
                S.op('dve', lambda en: en.tensor_tensor(out=sc, in0=sc, in1=ABs[:, 1:2, :].broadcast_to([128, 4, 33]), op=ALU.add),
                     reads=['sc', 'ABs'], writes=['sc'])
                for g in range(4):
                    S.op('dve', lambda en, g=g: en.max(out=mx8, in_=sc[:, g, :]), reads=['sc'], writes=['mx8'])
                    S.op('dve', lambda en, g=g: en.match_replace(out=sc2, in_to_replace=mx8, in_values=sc[:, g, :], imm_value=-3.0e4),
                         reads=['sc', 'mx8'], writes=['sc2'])
                    S.op('dve', lambda en: en.max(out=mx8, in_=sc2), reads=['sc2'], writes=['mx8'])
                    S.op('dve', lambda en: en.tensor_reduce(out=thr, in_=mx8, axis=AX.X, op=ALU.min), reads=['mx8'], writes=['thr'])
                    S.op('dve', lambda en, g=g: en.tensor_scalar(out=selm[:, g, :], in0=sc[:, g, :], scalar1=thr, scalar2=None,
                                                                 op0=ALU.is_ge), reads=['sc', 'thr'], writes=['selm'])
                S.op('dve', lambda en: en.tensor_scalar(out=negb, in0=selm, scalar1=-1.0, scalar2=-NEG, op0=ALU.add, op1=ALU.mult),
                     reads=['selm'], writes=['negb'])
                pb7 = banks[7].bitcast(BF16)
                for g in range(4):
                    S.op('pe', lambda en, g=g, pb7=pb7: en.transpose(out=pb7[0:33, 128 * g:128 * g + 128], in_=negb[:, g, :], identity=ident),
                         reads=['negb', 'ident'], writes=[('bank', 7)])
                S.op('act', lambda en, pb7=pb7: en.copy(out=negT, in_=pb7[0:33, 0:512].rearrange("p (g n) -> p g n", g=4)),
                     reads=[('bank', 7)], writes=['negT'])
                for g in range(4):
                    S.op('dve', lambda en, g=g, sq_i=sq_i: en.tensor_copy(
                        out=negTx[:, g, :, :], in_=negT[:, g:g + 1, 8 * sq_i:8 * sq_i + 8].broadcast_to([33, 4, 8])),
                        reads=['negT'], writes=['negTx'])
                for k in range(4):
                    S.dma('sp', cwt[:, k % 2, 0:512], cache_win[sq_i, 128 * k:128 * k + 128, :], writes=[('cwt', k % 2)])
                    hi = cst['h'] % 2
                    cst['h'] += 1
                    S.op('dve', lambda en, hi=hi, k=k: en.tensor_copy(out=hb16[hi][:, 0:512], in_=cwt[:, k % 2, 0:512]),
                         reads=[('cwt', k % 2)], writes=[('hb16', hi)])
                    transpose_g(hb16[hi], ('hb16', hi), 0, lambda k=k: KW[0:64, :, 128 * k:128 * k + 128], 'KW')
                    S.op('dve', lambda en, hi=hi, k=k: en.tensor_copy(
                        out=VW[:, k, :, 0:64], in_=hb16[hi][:, 256:512].rearrange("p (g d) -> p g d", g=4)),
                        reads=[('hb16', hi)], writes=['VW'])
                S.dma('sp', newt[0:8, :], kvs_s[16, 8 * sq_i:8 * sq_i + 8, :],
                      reads=[('kvs_s', 16, 0), ('kvs_s', 16, 1), ('kvs_s', 16, 2)], writes=['newt'])
                transpose_g(newt, 'newt', 512, lambda: KS[0:64, :, 2048:2176], 'KS')
                transpose_g(newt, 'newt', 1024, lambda: KW[0:64, :, 512:640], 'KW')
                S.op('dve', lambda en: en.tensor_copy(out=VS[:, 16, :, 0:64], in_=newt[:, 768:1024].rearrange("p (g d) -> p g d", g=4)),
                     reads=['newt'], writes=['VS'])
                S.op('dve', lambda en: en.tensor_copy(out=VW[:, 4, :, 0:64], in_=newt[:, 1280:1536].rearrange("p (g d) -> p g d", g=4)),
                     reads=['newt'], writes=['VW'])
                for kt in range(17):
                    extra = [(E_s[:, kt, :], negTx.rearrange("p g r q -> p (g r q)"), ['E_s', 'negTx'])]
                    if kt == 16:
                        extra.append((ident, cmS[:, 0, :], ['ident', 'cmS']))
                    pi = scores(2 + kt % 3, KS, 'KS', slice(128 * kt, 128 * kt + 128), 0, sq_i, extra)
                    if kt > 0:
                        pv(5, pprev[0], lambda g, kt=kt: VS[:, kt - 1, g, :], 'VS', 65, kt - 1 == 0, False)
                    pprev = [pi]
                pv(5, pprev[0], lambda g: VS[:, 16, g, :], 'VS', 65, False, True)
                store_branch(1, sq_i, 5, 65)
                for kt in range(5):
                    extra = []
                    if kt == 0:
                        extra.append((ident, cmS[:, 1, :], ['ident', 'cmS']))
                    if kt == 4:
                        extra.append((ident, cmS[:, 0, :], ['ident', 'cmS']))
                    pi = scores(2 + kt % 3, KW, 'KW', slice(128 * kt, 128 * kt + 128), 1, sq_i, extra)
                    if kt > 0:
                        pv(6, pprev[0], lambda g, kt=kt: VW[:, kt - 1, g, :], 'VW', 65, kt - 1 == 0, False)
                    pprev = [pi]
                pv(6, pprev[0], lambda g: VW[:, 4, g, :], 'VW', 65, False, True)
                store_branch(2, sq_i, 6, 65)
            S.barrier()
        with ExitStack() as esF:
            def sF(name, shape, dt=F32):
                return esF.enter_context(nc.sbuf_tensor(name, list(shape), dt)).ap()
            ob = {}
            for br in range(3):
                ob[br] = sF(f"obd{br}", [128, 16, 65])
                S.dma('sp', ob[br], osc_s[br].rearrange("s q g r c -> (s q) (g r) c"), reads=[('osc_s', br)], writes=[('ob', br)])
            gt = sF("gtd", [128, 48])
            S.dma('sp', gt, gate_s[16], reads=[('gate_s', 16)], writes=['gt'])
            wgt = sF("wgtd", [128, 16])
            nsa = sF("nsad", [128, 16, 64])
            ntmp = sF("ntmpd", [128, 16, 64])
            nsab = sF("nsabd", [128, 1024], BF16)
            for br in range(3):
                S.op('dve', lambda en, br=br: en.tensor_scalar(out=wgt, in0=ob[br][:, :, 64], scalar1=1e-30, scalar2=None, op0=ALU.add),
                     reads=[('ob', br)], writes=['wgt'])
                S.op('dve', lambda en: en.reciprocal(out=wgt, in_=wgt), reads=['wgt'], writes=['wgt'])
                S.op('dve', lambda en, br=br: en.tensor_tensor(
                    out=wgt, in0=wgt, in1=gt.rearrange("p (h b) -> p h b", b=3)[:, :, br], op=ALU.mult),
                    reads=['wgt', 'gt'], writes=['wgt'])
                dst = nsa if br == 0 else ntmp
                S.op('dve', lambda en, br=br, dst=dst: en.tensor_tensor(
                    out=dst, in0=ob[br][:, :, 0:64], in1=wgt.rearrange("p (h o) -> p h o", o=1).broadcast_to([128, 16, 64]),
                    op=ALU.mult), reads=[('ob', br), 'wgt'], writes=['nsa' if br == 0 else 'ntmp'])
                if br > 0:
                    S.op('dve', lambda en: en.tensor_tensor(out=nsa, in0=nsa, in1=ntmp, op=ALU.add),
                         reads=['nsa', 'ntmp'], writes=['nsa'])
            S.op('act', lambda en: en.copy(out=nsab, in_=nsa.rearrange("p h d -> p (h d)")), reads=['nsa'], writes=['nsab'])
            S.dma('sp', mix_s[16, :, 1024:2048], nsab, reads=['nsab'], writes=[('mix_s', 16, 1)])
            S.barrier()
    if stage >= 7:
        stage_D()

    def stage_E():
        with ExitStack() as esE:
            def sE(name, shape, dt=F32):
                return esE.enter_context(nc.sbuf_tensor(name, list(shape), dt)).ap()
            hres = sE("hres", [128, 10, D])
            for n_, t in enumerate(QT):
                src = xp[128 * t:128 * (t + 1), :] if t < 16 else xs[:, :]
                S.dma('sp', hres[:, n_, :], src, writes=[('hres', n_)])
            with ExitStack() as es1:
                wo = es1.enter_context(nc.sbuf_tensor("wo", [128, 16, D], BF16)).ap()
                mxt = [es1.enter_context(nc.sbuf_tensor(f"mxt{i}", [128, D], BF16)).ap() for i in range(2)]
                mxT = [es1.enter_context(nc.sbuf_tensor(f"mxT{i}", [128, 16, 128], BF16)).ap() for i in range(2)]
                w_out_v = w_out.rearrange("(k p) n -> p k n", p=128)
                for j in range(4):
                    S.dma('pool', wo[:, :, 512 * j:512 * j + 512], w_out_v[:, :, 512 * j:512 * j + 512], writes=[('wo', j)])
                for n_, t in enumerate(QT):
                    i = n_ % 2
                    S.dma('sp', mxt[i], mix_s[t], reads=[('mix_s', t, 0), ('mix_s', t, 1)], writes=[('mxt', i)])
                    for half in range(2):
                        pb = banks[half].bitcast(BF16)
                        for kk in range(8):
                            k = 8 * half + kk
                            S.op('pe', lambda en, pb=pb, kk=kk, k=k, i=i: en.transpose(
                                out=pb[:, 128 * kk:128 * kk + 128], in_=mxt[i][:, 128 * k:128 * k + 128], identity=ident),
                                reads=[('mxt', i), 'ident'], writes=[('bank', half)])
                        S.op('act', lambda en, pb=pb, half=half, i=i: en.copy(
                            out=mxT[i][:, 8 * half:8 * half + 8, :], in_=pb.rearrange("p (k n) -> p k n", k=8)),
                            reads=[('bank', half)], writes=[('mxT', i)])
                    for j in range(4):
                        bk = next_bank()
                        for k in range(16):
                            S.op('pe', lambda en, bk=bk, k=k, j=j, i=i: en.matmul(
                                banks[bk], lhsT=mxT[i][:, k, :], rhs=wo[:, k, 512 * j:512 * j + 512],
                                start=(k == 0), stop=(k == 15)), reads=[('mxT', i), ('wo', j)], writes=[('bank', bk)])
                        hv = hres[:, n_, 512 * j:512 * j + 512]
                        S.op('dve', lambda en, hv=hv, bk=bk: en.tensor_tensor(out=hv, in0=hv, in1=banks[bk], op=ALU.add),
                             reads=[('bank', bk), ('hres', n_)], writes=[('hres', n_)])
                S.barrier()
            hnT = sE("hnT", [128, 10, 16, 128], BF16)
            gfT = sE("gfT", [128, 16])
            S.dma('sp', gfT, g_ffn.rearrange("(k p) -> p k", p=128), writes=['gfT'])
            with ExitStack() as es2:
                hb = [es2.enter_context(nc.sbuf_tensor(f"hb{i}", [128, D], BF16)).ap() for i in range(2)]
                hss = [es2.enter_context(nc.sbuf_tensor(f"hss{i}", [128, 1], F32)).ap() for i in range(2)]
                for n_ in range(10):
                    i = n_ % 2
                    hv = hres[:, n_, :]
                    S.op('act', lambda en, i=i, hv=hv: en.activation(out=hb[i], in_=hv, func=AF.Square, accum_out=hss[i]),
                         reads=[('hres', n_)], writes=[('hb', i), ('hss', i)])
                    S.op('act', lambda en, i=i: en.activation(out=hss[i], in_=hss[i], func=AF.Sqrt, scale=1.0 / D, bias=epsb),
                         reads=[('hss', i), 'epsb'], writes=[('hss', i)])
                    S.op('dve', lambda en, i=i: en.reciprocal(out=hss[i], in_=hss[i]), reads=[('hss', i)], writes=[('hss', i)])
                    S.op('act', lambda en, i=i, hv=hv: en.activation(out=hb[i], in_=hv, func=AF.Copy, scale=hss[i]),
                         reads=[('hres', n_), ('hss', i)], writes=[('hb', i)])
                    for half in range(2):
                        pb = banks[half].bitcast(BF16)
                        for kk in range(8):
                            k = 8 * half + kk
                            S.op('pe', lambda en, pb=pb, kk=kk, k=k, i=i: en.transpose(
                                out=pb[:, 128 * kk:128 * kk + 128], in_=hb[i][:, 128 * k:128 * k + 128], identity=ident),
                                reads=[('hb', i), 'ident'], writes=[('bank', half)])
                        S.op('dve', lambda en, pb=pb, half=half, n_=n_: en.tensor_tensor(
                            out=hnT[:, n_, 8 * half:8 * half + 8, :], in0=pb.rearrange("p (k n) -> p k n", k=8),
                            in1=gfT[:, 8 * half:8 * half + 8].rearrange("p (k o) -> p k o", o=1).broadcast_to([128, 8, 128]),
                            op=ALU.mult), reads=[('bank', half), 'gfT'], writes=['hnT'])
                S.barrier()
            SBW = 256
            NSB = 5632 // SBW
            wu = {(ab, i): sE(f"wu{ab}{i}", [128, 16, SBW], BF16) for ab in range(2) for i in range(2)}
            wd = [hist_all.rearrange("p a n -> p (a n)").bitcast(BF16).rearrange("p (c n) -> p c n", c=2), sE("wd1", [128, 2, D], BF16)]
            identf = sE("identf_sb", [128, 128])
            S.dma('sp', identf, identf_d, writes=['identf'])
            hfl = sE("hfl", [128, 1])
            ust = sE("ust", [128, 32])
            S.dma('sp', hfl, hflag_d.partition_broadcast(128), writes=['hfl'])
            cw = sE("cw", [128, 3, 88])
            cbv = sE("cbv", [128, 88])
            for j in range(3):
                S.dma('sp', cw[:, j, :], conv_w[j].rearrange("(k p) -> p k", p=128), writes=['cw'])
            S.dma('sp', cbv, conv_b.rearrange("(k p) -> p k", p=128), writes=['cbv'])
            up_ = {ab: sE(f"up{ab}", [128, 1154]) for ab in range(2)}
            us_ = {ab: sE(f"us{ab}", [128, 16, 10]) for ab in range(2)}
            cp_ = {ab: sE(f"cp{ab}", [128, 1152]) for ab in range(2)}
            cs2_ = {ab: sE(f"cs2{ab}", [128, 16, 8]) for ab in range(2)}
            actT = [sE(f"actT{i}", [128, 2, 1280], BF16) for i in range(2)]
            pvt0 = sE("pvt0", [32, 2, SBW]); pvt = [pvt0, pvt0]
            cvo0 = sE("cvo0", [34, 2, SBW]); cvo = [cvo0, cvo0]
            for ab in range(2):
                S.op('pool', lambda en, ab=ab: en.memset(up_[ab][:, 0:2], 0.0), writes=[('up', ab)])
            w_up_v = w_up.rearrange("(k p) n -> p k n", p=128)
            w_down_v = w_down.rearrange("(c p) n -> p c n", p=128)
            state_conv_v = state_conv_d.rearrange("s j n -> (s j) n")
            TG = [(0, 4), (4, 4), (8, 2)]
            pend_down = [None]
            for sbi in range(NSB):
                i = sbi % 2
                for ab in range(2):
                    S.dma('pool', wu[(ab, i)], w_up_v[:, :, 5632 * ab + SBW * sbi:5632 * ab + SBW * sbi + SBW],
                          writes=[('wu', ab, i)])
                    S.dma('sp', pvt[i][:, ab, :], state_conv_v[:, 5632 * ab + SBW * sbi:5632 * ab + SBW * sbi + SBW],
                          writes=[('pvt', 0)])
                S.dma('pool', wd[i], w_down_v[:, 2 * sbi:2 * sbi + 2, :], writes=[('wd', i)])
                for fc in range(2):
                    kch = 2 * sbi + fc
                    for ab in range(2):
                        kk = 44 * ab + kch
                        bkp = next_bank()
                        S.op('pe', lambda en, bkp=bkp, i=i, ab=ab, fc=fc: en.transpose(
                            out=banks[bkp][:, 0:32], in_=pvt[i][:, ab, 128 * fc:128 * fc + 128], identity=identf[0:32, 0:32]),
                            reads=[('pvt', 0), 'identf'], writes=[('bank', bkp)])
                        S.op('act', lambda en, bkp=bkp, ab=ab: en.copy(
                            out=us_[ab][:, :, 0:2], in_=banks[bkp][:, 0:32].rearrange("p (s j) -> p s j", j=2)),
                            reads=[('bank', bkp)], writes=[('us', ab)])
                        for (t0, nt) in TG:
                            bk = next_bank()
                            for k in range(16):
                                S.op('pe', lambda en, bk=bk, k=k, ab=ab, i=i, fc=fc, t0=t0, nt=nt: en.matmul(
                                    banks[bk][:, 0:128 * nt], lhsT=wu[(ab, i)][:, k, 128 * fc:128 * fc + 128],
                                    rhs=hnT[:, t0:t0 + nt, k, :], start=(k == 0), stop=(k == 15)),
                                    reads=[('wu', ab, i), 'hnT'], writes=[('bank', bk)])
                            if t0 == 0:
                                S.op('act', lambda en, bk=bk, ab=ab: en.activation(
                                    out=up_[ab][:, 2:130], in_=banks[bk][:, 0:128], func=AF.Copy, scale=hfl),
                                    reads=[('bank', bk), 'hfl'], writes=[('up', ab)])
                                S.op('act', lambda en, bk=bk, ab=ab: en.copy(out=up_[ab][:, 130:514], in_=banks[bk][:, 128:512]),
                                     reads=[('bank', bk)], writes=[('up', ab)])
                            elif t0 == 4:
                                S.op('act', lambda en, bk=bk, ab=ab: en.copy(out=up_[ab][:, 514:1026], in_=banks[bk]),
                                     reads=[('bank', bk)], writes=[('up', ab)])
                            else:
                                S.op('act', lambda en, bk=bk, ab=ab: en.copy(out=up_[ab][:, 1026:1154], in_=banks[bk][:, 0:128]),
                                     reads=[('bank', bk)], writes=[('up', ab)])
                                S.op('act', lambda en, bk=bk, ab=ab: en.copy(
                                    out=us_[ab][:, :, 2:10], in_=banks[bk][:, 128:256].rearrange("p (s q) -> p s q", q=8)),
                                    reads=[('bank', bk)], writes=[('us', ab)])
                        bkc = next_bank()
                        S.op('dve', lambda en, ab=ab: en.tensor_copy(
                            out=ust.rearrange("p (s j) -> p s j", j=2), in_=us_[ab][:, :, 8:10]),
                            reads=[('us', ab)], writes=['ust'])
                        S.op('pe', lambda en, bkc=bkc, ab=ab: en.transpose(
                            out=banks[bkc][0:32, 0:128], in_=ust, identity=identf),
                            reads=['ust', 'identf'], writes=[('bank', bkc)])
                        S.op('pe', lambda en, bkc=bkc, ab=ab: en.transpose(
                            out=banks[bkc][0:2, 128:256], in_=up_[ab][:, 1152:1154], identity=identf),
                            reads=[('up', ab), 'identf'], writes=[('bank', bkc)])
                        S.op('act', lambda en, bkc=bkc, ab=ab, i=i, fc=fc: en.copy(
                            out=cvo[i][0:32, ab, 128 * fc:128 * fc + 128], in_=banks[bkc][0:32, 0:128]),
                            reads=[('bank', bkc)], writes=[('cvo', 0, 0)])
                        S.op('act', lambda en, bkc=bkc, ab=ab, i=i, fc=fc: en.copy(
                            out=cvo[i][32:34, ab, 128 * fc:128 * fc + 128], in_=banks[bkc][0:2, 128:256]),
                            reads=[('bank', bkc)], writes=[('cvo', 0, 1)])
                        w0, w1_, w2_ = cw[:, 0, kk:kk + 1], cw[:, 1, kk:kk + 1], cw[:, 2, kk:kk + 1]
                        bb = cbv[:, kk:kk + 1]
                        for (u, c, ku, kc_, sl) in ((up_[ab], cp_[ab], ('up', ab), ('cp', ab),
                                                     (slice(2, 1154), slice(1, 1153), slice(0, 1152))),
                                                    (us_[ab], cs2_[ab], ('us', ab), ('cs2', ab),
                                                     (slice(2, 10), slice(1, 9), slice(0, 8)))):
                            if u is up_[ab]:
                                u2, u1, u0 = u[:, sl[0]], u[:, sl[1]], u[:, sl[2]]
                            else:
                                u2, u1, u0 = u[:, :, sl[0]], u[:, :, sl[1]], u[:, :, sl[2]]
                            S.op('act', lambda en, c=c, u2=u2, w2_=w2_, bb=bb: en.activation(
                                out=c, in_=u2, func=AF.Identity, scale=w2_, bias=bb), reads=[ku, 'cw', 'cbv'], writes=[kc_])
                            S.op('dve', lambda en, c=c, u1=u1, w1_=w1_: en.scalar_tensor_tensor(
                                out=c, in0=u1, scalar=w1_, in1=c, op0=ALU.mult, op1=ALU.add), reads=[ku, kc_, 'cw'], writes=[kc_])
                            S.op('dve', lambda en, c=c, u0=u0, w0=w0: en.scalar_tensor_tensor(
                                out=c, in0=u0, scalar=w0, in1=c, op0=ALU.mult, op1=ALU.add), reads=[ku, kc_, 'cw'], writes=[kc_])
                    S.op('act', lambda en: en.activation(out=cp_[0], in_=cp_[0], func=AF.Silu), reads=[('cp', 0)], writes=[('cp', 0)])
                    S.op('act', lambda en: en.activation(out=cs2_[0], in_=cs2_[0], func=AF.Silu), reads=[('cs2', 0)], writes=[('cs2', 0)])
                    S.op('dve', lambda en, i=i, fc=fc: en.tensor_tensor(out=actT[i][:, fc, 0:1152], in0=cp_[0], in1=cp_[1], op=ALU.mult),
                         reads=[('cp', 0), ('cp', 1)], writes=[('actT', i)])
                    S.op('dve', lambda en, i=i, fc=fc: en.tensor_tensor(
                        out=actT[i][:, fc, 1152:1280].rearrange("p (s q) -> p s q", q=8), in0=cs2_[0], in1=cs2_[1], op=ALU.mult),
                        reads=[('cs2', 0), ('cs2', 1)], writes=[('actT', i)])
                for ab in range(2):
                    S.dma('sp', conv_s_out[:, 5632 * ab + SBW * sbi:5632 * ab + SBW * sbi + SBW], cvo[i][0:32, ab, :],
                          reads=[('cvo', 0, 0)], writes=[('conv_s_out', sbi, ab)])
                    S.dma('sp', conv_p_out[:, 5632 * ab + SBW * sbi:5632 * ab + SBW * sbi + SBW], cvo[i][32:34, ab, :],
                          reads=[('cvo', 0, 1)], writes=[('conv_p_out', sbi, ab)])
                def emit_down(i=i):
                    for n_ in range(10):
                        for j in range(4):
                            bk = next_bank()
                            for fc in range(2):
                                S.op('pe', lambda en, bk=bk, fc=fc, i=i, n_=n_, j=j: en.matmul(
                                    banks[bk], lhsT=actT[i][:, fc, 128 * n_:128 * n_ + 128], rhs=wd[i][:, fc, 512 * j:512 * j + 512],
                                    start=(fc == 0), stop=(fc == 1)), reads=[('actT', i), ('wd', i)], writes=[('bank', bk)])
                            hv = hres[:, n_, 512 * j:512 * j + 512]
                            S.op('dve', lambda en, hv=hv, bk=bk: en.tensor_tensor(out=hv, in0=hv, in1=banks[bk], op=ALU.add),
                                 reads=[('bank', bk), ('hres', n_)], writes=[('hres', n_)])
                if pend_down[0] is not None:
                    pend_down[0]()
                pend_down[0] = emit_down
            pend_down[0]()
            for n_ in range(1, 10):
                S.dma('sp', y_out[128 * (n_ - 1):128 * n_, :], hres[:, n_, :], reads=[('hres', n_)], writes=[('y_out', n_)])
            S.barrier()
    if stage >= 8:
        stage_E()

    S.finish()
    with nc.allow_non_contiguous_dma(reason="small constant tables / strided layouts"):
        S.emit()
    return nc, S


def _host_inputs(inputs, c):
    b, h = c // 2, c % 2
    f = lambda a: np.ascontiguousarray(np.asarray(a, dtype=np.float32))
    xpb = np.asarray(inputs['x_prompt'][b], dtype=np.float32)
    if h == 0:
        xp = np.concatenate([np.zeros((1024, D), np.float32), xpb[:1024]], axis=0)
    else:
        xp = xpb
    pos = np.zeros((17, 128), np.float64)
    for t in range(16):
        pos[t] = 128 * t + np.arange(128) - (1024 if h == 0 else 0)
    pos[16] = 2048 + (np.arange(128) % 8)
    inv = (1.0 / (10000.0 ** np.linspace(0.0, 1.0, 128, dtype=np.float32))).astype(np.float32)
    ang = (pos.astype(np.float32)[:, :, None] * inv[None, None, :]).astype(np.float32)
    cs = np.stack([np.cos(ang), np.sin(ang)], axis=2).astype(np.float32)
    gam = 1.0 - 2.0 ** (-5.0 - np.arange(4))
    ii = np.arange(128)
    sct = np.zeros((128, 2, 2, 4), np.float32)
    sct[:, 0, 0, :] = (gam[None, :] ** (-(ii[:, None] + 1.0))) / 16.0
    sct[:, 0, 1, :] = gam[None, :] ** (ii[:, None] + 1.0)
    sct[:, 1, 0, :] = (gam[None, :] ** (-((ii[:, None] % 8) + 1.0))) / 16.0
    sct[:, 1, 1, :] = gam[None, :] ** ((ii[:, None] % 8) + 1.0)
    rmask = (ii[:, None] // 8 == np.arange(16)[None, :]).astype(np.float32)
    cmask = np.broadcast_to((np.arange(16)[:, None] == ii[None, :] // 8)[None], (128, 16, 128)).astype(ml_dtypes.bfloat16)
    mT = np.zeros((128, 2, 128), np.float32)
    mT[:, 0, :] = (ii[:, None] <= ii[None, :])
    mT[:, 1, :] = (ii[:, None] <= ii[None, :]) & (ii[:, None] // 8 == ii[None, :] // 8)
    bf = ml_dtypes.bfloat16

    def split3(a):
        a = np.asarray(a, np.float32)
        hi = a.astype(bf).astype(np.float32)
        mid = (a - hi).astype(bf).astype(np.float32)
        lo = (a - hi - mid).astype(bf).astype(np.float32)
        return hi, mid, lo
    slopes = np.exp2(-8.0 * np.arange(1, 17, dtype=np.float32) / 16).astype(np.float32)
    kpos = np.arange(2048)
    kaug = np.zeros((10, 2048), np.float32)
    kaug[0:3] = 64.0 * (kpos // 64)
    kaug[3:6] = kpos % 64
    kaug[6:9] = 1.0
    kaug[9] = (kpos < 1024) if h == 0 else 0.0
    nblk = np.arange(128)
    cend = 16 * nblk + 31
    kaugc = np.zeros((10, 128), np.float32)
    kaugc[0:3] = 64.0 * (cend // 64)
    kaugc[3:6] = cend % 64
    kaugc[6:9] = 1.0
    kaugc[9] = (nblk >= 127) | ((nblk < 64) if h == 0 else False)
    qpos = 896 + np.arange(1152)
    qaug = np.zeros((10, 16, 1152), np.float32)
    s3 = split3(slopes)
    m3 = split3(-(slopes[:, None] * qpos[None, :].astype(np.float32)))
    for r_ in range(3):
        qaug[r_] = s3[r_][:, None]
        qaug[3 + r_] = s3[r_][:, None]
        qaug[6 + r_] = m3[r_]
    qaug[9] = -30000.0
    E_tab = np.zeros((32, 16, 128), np.float32)
    for kt in range(16):
        for k_ in range(128):
            E_tab[2 * kt + k_ // 64, kt, k_] = 1.0
    cm_tab = np.zeros((128, 2, 128), np.float32)
    cm_tab[:, 0, :] = np.where(ii[:, None] <= ii[None, :], 0.0, -30000.0)
    cm_tab[:, 1, :] = np.where(ii[:, None] > ii[None, :], 0.0, -30000.0)
    cmpm = np.zeros((128, 9, 128), np.float32)
    for qi_ in range(9):
        qp = 896 + 128 * qi_ + ii
        cmpm[:, qi_, :] = np.where(cend[:, None] <= qp[None, :], 0.0, -30000.0)
    cs_ = nblk[:, None] * 16
    js_ = np.arange(32)[None, :] * 64
    cov = (np.clip(np.minimum(cs_ + 32, js_ + 64) - np.maximum(cs_, js_), 0, None) / 32.0).astype(np.float32)
    cov[127] = 0.0
    AB = np.zeros((128, 2, 9, 32), np.float32)
    j0 = 0 if h == 1 else 16
    jj = np.arange(32)
    for qi_ in range(9):
        qp = 896 + 128 * qi_ + ii
        cur = qp // 64
        valid = (jj[None, :] >= j0) & (jj[None, :] <= cur[:, None])
        forced = valid & ((jj[None, :] == j0) | (jj[None, :] == cur[:, None]) | (jj[None, :] == cur[:, None] - 1))
        AB[:, 0, qi_, :] = (valid & ~forced)
        AB[:, 1, qi_, :] = np.where(forced, 1.0e4 + jj[None, :], np.where(valid, 0.0, -1.0e4 - jj[None, :]))
    def aug_k(kp, invalid):
        a = np.zeros((10, kp.shape[0]), np.float32)
        a[0:3] = 64.0 * (kp // 64)
        a[3:6] = kp % 64
        a[6:9] = 1.0
        a[9] = invalid
        return a
    colS = np.arange(2176)
    ktS, pS = colS // 128, colS % 128
    kpS = np.where(ktS < 16, 8 * (128 * (ktS % 2) + pS) + ktS // 2, 2048 + pS)
    kaugS = aug_k(kpS, (kpS >= 2056))
    kpW = np.arange(640)
    kaugW = aug_k(kpW, (kpW >= 520))
    kaugcS = aug_k(cend, (nblk >= 127))

    def aug_q(qp):
        a = np.zeros((10, 16, 128), np.float32)
        mm = split3(-(slopes[:, None] * qp[None, :].astype(np.float32)))
        for r_ in range(3):
            a[r_] = s3[r_][:, None]
            a[3 + r_] = s3[r_][:, None]
            a[6 + r_] = mm[r_]
        a[9] = -30000.0
        return a
    qq = ii % 8
    qaugs = np.stack([aug_q(2048 + qq), aug_q(512 + qq)], axis=0)
    Es = np.zeros((33, 17, 128), np.float32)
    for kt in range(16):
        for k_ in range(128):
            pos_ = 8 * (128 * (kt % 2) + k_) + kt // 2
            Es[pos_ // 64, kt, k_] = 1.0
    Es[32, 16, :] = 1.0
    colq = ii % 8
    cmS = np.zeros((128, 2, 128), np.float32)
    cmS[:, 0, :] = np.where((ii[:, None] <= colq[None, :]) & (ii[:, None] < 8), 0.0, -30000.0)
    cmS[:, 1, :] = np.where(ii[:, None] > colq[None, :], 0.0, -30000.0)
    js33 = np.arange(33)[None, :] * 64
    covS = (np.clip(np.minimum(cs_ + 32, js33 + 64) - np.maximum(cs_, js33), 0, None) / 32.0).astype(np.float32)
    covS[127] = 0.0
    ABs = np.zeros((128, 2, 33), np.float32)
    j33 = np.arange(33)
    forcedS = (j33 == 0) | (j33 == 32) | (j33 == 31)
    ABs[:, 0, :] = (~forcedS)[None, :]
    ABs[:, 1, :] = np.where(forcedS, 1.0e4 + j33, 0.0)[None, :]
    Rsum = np.zeros((32, 16, 128), np.float32)
    for r_ in range(4):
        for q_ in range(8):
            for s_ in range(16):
                Rsum[8 * r_ + q_, s_, 8 * s_ + q_] = 1.0
    m = {
        'cache_kv': f(inputs['cache_kv'][0]).reshape(2560 * 16, 8192), 'page_table': np.ascontiguousarray(np.asarray(inputs['page_table'][16 * c:16 * c + 16], np.int32)),
        'upt_tab': (ii % 16).astype(np.float32).reshape(128, 1), 'Es_tab': Es.astype(bf), 'cmS_tab': cmS.astype(bf), 'covS_tab': covS,
        'ABs_tab': ABs, 'Rsum_tab': Rsum.astype(bf), 'kaugS_tab': kaugS.astype(bf), 'kaugW_tab': kaugW.astype(bf),
        'kaugcS_tab': kaugcS.astype(bf), 'qaugs_tab': qaugs.astype(bf),
        'w_out': f(inputs['w_out'][0]), 'g_ffn': f(inputs['g_ffn'][0]), 'w_up': f(inputs['w_up'][0]),
        'conv_w': f(inputs['conv_w'][0]), 'conv_b': f(inputs['conv_b'][0]), 'w_down': f(inputs['w_down'][0]),
        'state_conv': f(inputs['state_conv'][0, 16 * c:16 * c + 16]), 'identf': np.eye(128, dtype=np.float32),
        'hflag': np.array([float(h)], np.float32),
        'k_norm_cmp': f(inputs['k_norm_cmp'][0]), 'cmp_pe': f(inputs['cmp_pe'][0]), 'cmp_w1': f(inputs['cmp_w1'][0]),
        'cmp_b1': f(inputs['cmp_b1'][0]), 'cmp_w2': f(inputs['cmp_w2'][0]), 'cmp_b2': f(inputs['cmp_b2'][0]),
        'E_tab': E_tab.astype(bf), 'cm_tab': cm_tab.astype(bf), 'cmpm_tab': cmpm.astype(bf), 'cov_tab': cov, 'AB_tab': AB,
        'kaug_tab': kaug.astype(bf), 'kaugc_tab': kaugc.astype(bf), 'qaug_tab': qaug.astype(bf),
        'xp': np.ascontiguousarray(xp),
        'xs': f(inputs['x_sample'][16 * c:16 * c + 16]).reshape(128, D),
        'w_in': f(inputs['w_in'][0]),
        'g_attn': f(inputs['g_attn'][0]),
        'k_norm_slc': f(inputs['k_norm_slc'][0]),
        'k_norm_win': f(inputs['k_norm_win'][0]),
        'ident': np.eye(128, dtype=np.float32).astype(ml_dtypes.bfloat16),
        'cache_win': f(inputs['cache_win'][0, 16 * c:16 * c + 16]).reshape(16, 512, 512),
        'state_ret': f(inputs['state_ret'][0, 16 * c:16 * c + 16]),
        'cs_tab': cs, 'sct_tab': sct, 'rmask_tab': rmask, 'cmask_tab': np.ascontiguousarray(cmask), 'mT_tab': mT,
        'q_norm': f(inputs['q_norm'][0]), 'ret_gn': f(inputs['ret_gn'][0]),
    }
    return m


_CACHE = {}


def run_device(inputs, stage=99, trace=False):
    if stage not in _CACHE:
        _CACHE[stage] = build(stage)
    nc, S = _CACHE[stage]
    in_maps = [_host_inputs(inputs, c) for c in range(NCORES)]
    res = run_bass_kernel_spmd(nc, in_maps, core_ids=list(range(NCORES)), trace=trace)
    return res


def kernel(**inputs):
    res = run_device(inputs)
    R = res.results
    y_p = np.zeros((4, 2048, D), np.float32)
    y_s = np.zeros((128, 8, D), np.float32)
    kv_p = np.zeros((1, 4, 2048, 4, 4, 64), np.float32)
    kv_s = np.zeros((1, 128, 8, 4, 4, 64), np.float32)
    win_p = np.zeros((1, 4, 512, 2, 4, 64), np.float32)
    win_s = np.zeros((1, 128, 512, 2, 4, 64), np.float32)
    ret_p = np.zeros((1, 4, 4, 256, 256), np.float32)
    ret_s = np.zeros((1, 128, 4, 256, 256), np.float32)
    conv_p = np.zeros((1, 4, 2, 11264), np.float32)
    conv_s = np.zeros((1, 128, 2, 11264), np.float32)
    for c in range(NCORES):
        b, h = c // 2, c % 2
        r = R[c]
        kv_p[0, b, 1024 * h:1024 * h + 1024] = r['kv_out'][:1024].reshape(1024, 4, 4, 64)
        kv_s[0, 16 * c:16 * c + 16] = r['kv_out'][1024:].reshape(16, 8, 4, 4, 64)
        win_s[0, 16 * c:16 * c + 16] = r['win_s_out'].reshape(16, 512, 2, 4, 64)
        ret_s[0, 16 * c:16 * c + 16] = r['ret_s_out']
        y_p[b, 1024 * h:1024 * h + 1024] = r['y_out'][:1024]
        y_s[16 * c:16 * c + 16] = r['y_out'][1024:].reshape(16, 8, D)
        conv_s[0, 16 * c:16 * c + 16] = r['conv_s_out'].reshape(16, 2, 11264)
        if h == 1:
            conv_p[0, b] = r['conv_p_out']
            win_p[0, b] = r['win_p_out'].reshape(512, 2, 4, 64)
            ret_p[0, b] = r['ret_p_out']
    return (y_p, y_s, kv_p, kv_s, win_p, win_s, ret_p, ret_s, conv_p, conv_s)
```

```python
import numpy as np
from contextlib import ExitStack
import ml_dtypes
import concourse.bass as bass
import concourse.mybir as mybir
from concourse.bass_utils import run_bass_kernel_spmd

F32 = mybir.dt.float32
BF16 = mybir.dt.bfloat16
I32 = mybir.dt.int32
AF = mybir.ActivationFunctionType
ALU = mybir.AluOpType
AX = mybir.AxisListType

NCORES = 8
D = 2048
EPS = 1e-6
EP = 2000
NDMASEM = 24
NEG = -30000.0
WARM = 0


class Sched:
    ENG = ['pe', 'act', 'dve', 'pool', 'sp']

    def __init__(self, nc):
        self.nc = nc
        self.rec = {e: [] for e in self.ENG}
        self.n = {e: 0 for e in self.ENG}
        self.seen = {e: {f: 0 for f in self.ENG} for e in self.ENG}
        self.dseen = {e: {} for e in self.ENG}
        self.esem = {e: [] for e in self.ENG}
        self.lastw = {}
        self.readers = {}
        self.clock = {}
        self.dq = {}
        self.nwaits = 0

    def _esem(self, e, epoch):
        while len(self.esem[e]) <= epoch:
            self.esem[e].append(self.nc.alloc_semaphore(name=f"s_{e}_{len(self.esem[e])}"))
        return self.esem[e][epoch]

    def _wait(self, eng, ev, force=False):
        if ev[0] == 'eng':
            _, e2, n2 = ev
            if self.seen[eng][e2] >= n2:
                return
            if eng == 'pe' and e2 == 'pe':
                return
            epoch = (n2 - 1) // EP
            sem = self._esem(e2, epoch)
            val = n2 - epoch * EP
            self.rec[eng].append(lambda en, sem=sem, val=val: en.wait_ge(sem, val))
            self.nwaits += 1
            self.seen[eng][e2] = n2
        else:
            _, q, idx, val = ev
            key = (q, idx)
            if self.dseen[eng].get(key, 0) >= val and not force:
                return
            sem = self.dq[q]['sems'][idx]
            self.rec[eng].append(lambda en, sem=sem, val=val: en.wait_ge(sem, val))
            self.nwaits += 1
            self.dseen[eng][key] = val
        ck = self.clock.get(ev)
        if ck is not None:
            for f, v in ck[0].items():
                if self.seen[eng][f] < v:
                    self.seen[eng][f] = v
            for k, v in ck[1].items():
                if self.dseen[eng].get(k, 0) < v:
                    self.dseen[eng][k] = v

    def _deps(self, eng, reads, writes):
        evs = []
        for k in reads:
            if k in self.lastw:
                evs.append(self.lastw[k])
        for k in writes:
            if k in self.lastw:
                evs.append(self.lastw[k])
            evs.extend(self.readers.get(k, ()))
        for ev in evs:
            self._wait(eng, ev)

    def _commit(self, ev, eng, reads, writes):
        self.clock[ev] = (dict(self.seen[eng]), dict(self.dseen[eng]))
        for k in writes:
            self.lastw[k] = ev
            self.readers[k] = []
        for k in reads:
            if k in writes:
                continue
            self.readers.setdefault(k, []).append(ev)
            if len(self.readers[k]) > 64:
                self.readers[k] = self.readers[k][-64:]

    def op(self, eng, fn, reads=(), writes=()):
        self._deps(eng, reads, writes)
        self.n[eng] += 1
        n = self.n[eng]
        epoch = (n - 1) // EP
        sem = self._esem(eng, epoch)
        self.rec[eng].append(lambda en, fn=fn, sem=sem: fn(en).then_inc(sem, 1))
        self.seen[eng][eng] = n if eng == 'pe' else self.seen[eng][eng]
        ev = ('eng', eng, n)
        self._commit(ev, eng, reads, writes)
        return ev

    def dma(self, q, out, in_, reads=(), writes=(), **kw):
        if q not in self.dq:
            self.dq[q] = {'sems': [self.nc.alloc_semaphore(name=f"d_{q}_{i}") for i in range(NDMASEM)],
                          'count': 0}
        st = self.dq[q]
        i = st['count']
        st['count'] += 1
        idx = i % NDMASEM
        val = 16 * (i // NDMASEM + 1)
        if i >= NDMASEM:
            self._wait(q, ('dma', q, idx, val - 16), force=(q == 'pool'))
        self._deps(q, reads, writes)
        sem = st['sems'][idx]
        self.rec[q].append(lambda en, out=out, in_=in_, sem=sem, kw=kw:
                           en.dma_start(out=out, in_=in_, **kw).then_inc(sem, 16))
        ev = ('dma', q, idx, val)
        self._commit(ev, q, reads, writes)
        return ev

    def dma_gather(self, out, in_, idx_ap, nrows, reads=(), writes=()):
        q = 'pool'
        if q not in self.dq:
            self.dq[q] = {'sems': [self.nc.alloc_semaphore(name=f"d_{q}_{i}") for i in range(NDMASEM)], 'count': 0}
        st = self.dq[q]
        i = st['count']
        st['count'] += 1
        idx = i % NDMASEM
        val = 16 * (i // NDMASEM + 1)
        if i >= NDMASEM:
            self._wait(q, ('dma', q, idx, val - 16))
        self._deps(q, reads, writes)
        sem = st['sems'][idx]
        self.rec[q].append(lambda en, out=out, in_=in_, sem=sem, idx_ap=idx_ap: en.indirect_dma_start(
            out=out, out_offset=None, in_=in_, in_offset=bass.IndirectOffsetOnAxis(ap=idx_ap, axis=0),
            bounds_check=nrows - 1, oob_is_err=False).then_inc(sem, 16))
        ev = ('dma', q, idx, val)
        self._commit(ev, q, reads, writes)
        return ev

    def dma_dyn(self, q, out, in_fn, pt_ap, maxv, reads=(), writes=()):
        if q not in self.dq:
            self.dq[q] = {'sems': [self.nc.alloc_semaphore(name=f"d_{q}_{i}") for i in range(NDMASEM)], 'count': 0}
        st = self.dq[q]
        i = st['count']
        st['count'] += 1
        idx = i % NDMASEM
        val = 16 * (i // NDMASEM + 1)
        if i >= NDMASEM:
            self._wait(q, ('dma', q, idx, val - 16))
        self._deps(q, reads, writes)
        sem = st['sems'][idx]

        def f(en, out=out, in_fn=in_fn, sem=sem, pt_ap=pt_ap):
            pg = en.value_load(pt_ap, min_val=0, max_val=maxv)
            en.dma_start(out=out, in_=in_fn(pg)).then_inc(sem, 16)
        self.rec[q].append(f)
        ev = ('dma', q, idx, val)
        self._commit(ev, q, reads, writes)
        return ev

    def barrier(self):
        evs = []
        for e in self.ENG:
            if self.n[e] > 0:
                evs.append(('eng', e, self.n[e]))
        for q, st in self.dq.items():
            for i in range(max(0, st['count'] - NDMASEM), st['count']):
                evs.append(('dma', q, i % NDMASEM, 16 * (i // NDMASEM + 1)))
        for e in self.ENG:
            for ev in evs:
                if ev[0] == 'eng' and ev[1] == e and e == 'pe':
                    continue
                self._wait(e, ev)

    def finish(self):
        for q, st in self.dq.items():
            for i in range(max(0, st['count'] - NDMASEM), st['count']):
                self._wait('sp', ('dma', q, i % NDMASEM, 16 * (i // NDMASEM + 1)))
        for e in self.ENG:
            if e != 'sp' and self.n[e] > 0:
                self._wait('sp', ('eng', e, self.n[e]))

    def emit(self):
        nc = self.nc
        rec = self.rec
        with nc.Block() as block:
            @block.tensor
            def _(en):
                for f in rec['pe']:
                    f(en)

            @block.scalar
            def _(en):
                for f in rec['act']:
                    f(en)

            @block.vector
            def _(en):
                for f in rec['dve']:
                    f(en)

            @block.gpsimd
            def _(en):
                for f in rec['pool']:
                    f(en)

            @block.sync
            def _(en):
                for f in rec['sp']:
                    f(en)


def build(stage=99):
    nc = bass.Bass("TRN2", target_bir_lowering=False)
    S = Sched(nc)

    def din(name, shape, dt=F32):
        return nc.dram_tensor(name, list(shape), dt, kind="ExternalInput").ap()

    def dout(name, shape, dt=F32):
        return nc.dram_tensor(name, list(shape), dt, kind="ExternalOutput").ap()

    def sb(name, shape, dt=F32):
        return nc.alloc_sbuf_tensor(name, list(shape), dt).ap()

    xp = din("xp", [2048, D])
    xs = din("xs", [128, D])
    w_in = din("w_in", [D, 6704])
    g_attn = din("g_attn", [D])
    k_norm_slc = din("k_norm_slc", [64])
    k_norm_win = din("k_norm_win", [64])
    ident_d = din("ident", [128, 128], BF16)

    cache_win = din("cache_win", [16, 512, 512])
    state_ret_d = din("state_ret", [16, 4, 256, 256])
    cs_d = din("cs_tab", [17, 128, 2, 128])
    sct_d = din("sct_tab", [128, 2, 2, 4])
    rmask_d = din("rmask_tab", [128, 16])
    cmask_d = din("cmask_tab", [128, 16, 128], BF16)
    mT_d = din("mT_tab", [128, 2, 128])
    q_norm = din("q_norm", [64])
    ret_gn = din("ret_gn", [4, 256])
    k_norm_cmp = din("k_norm_cmp", [64])
    cache_kv = din("cache_kv", [2560 * 16, 8192])
    page_table_d = din("page_table", [16, 16], I32)
    upt_d = din("upt_tab", [128, 1])
    Es_d = din("Es_tab", [33, 17, 128], BF16)
    cmS_d = din("cmS_tab", [128, 2, 128], BF16)
    covS_d = din("covS_tab", [128, 33])
    ABs_d = din("ABs_tab", [128, 2, 33])
    Rsum_d = din("Rsum_tab", [32, 16, 128], BF16)
    kaugS_d = din("kaugS_tab", [10, 2176], BF16)
    kaugW_d = din("kaugW_tab", [10, 640], BF16)
    kaugcS_d = din("kaugcS_tab", [10, 128], BF16)
    qaugs_d = din("qaugs_tab", [2, 10, 16, 128], BF16)
    w_out = din("w_out", [D, D])
    g_ffn = din("g_ffn", [D])
    w_up = din("w_up", [D, 11264])
    conv_w = din("conv_w", [3, 11264])
    conv_b = din("conv_b", [11264])
    w_down = din("w_down", [5632, D])
    state_conv_d = din("state_conv", [16, 2, 11264])
    identf_d = din("identf", [128, 128])
    hflag_d = din("hflag", [1])
    y_out = dout("y_out", [1152, D])
    conv_p_out = dout("conv_p_out", [2, 11264])
    conv_s_out = dout("conv_s_out", [32, 11264])
    cmp_pe = din("cmp_pe", [2, 32, 64])
    cmp_w1 = din("cmp_w1", [2, 32, 64, 256])
    cmp_b1 = din("cmp_b1", [2, 256])
    cmp_w2 = din("cmp_w2", [2, 256, 64])
    cmp_b2 = din("cmp_b2", [2, 64])
    E_d = din("E_tab", [32, 16, 128], BF16)
    cm_d = din("cm_tab", [128, 2, 128], BF16)
    cmpm_d = din("cmpm_tab", [128, 9, 128], BF16)
    cov_d = din("cov_tab", [128, 32])
    AB_d = din("AB_tab", [128, 2, 9, 32])
    kaug_d = din("kaug_tab", [10, 2048], BF16)
    kaugc_d = din("kaugc_tab", [10, 128], BF16)
    qaug_d = din("qaug_tab", [10, 16, 1152], BF16)

    kv_out = dout("kv_out", [1152, 1024])
    win_p_out = dout("win_p_out", [512, 512])
    win_s_out = dout("win_s_out", [16, 512, 512])
    ret_p_out = dout("ret_p_out", [4, 256, 256])
    ret_s_out = dout("ret_s_out", [16, 4, 256, 256])

    ident = sb("ident_sb", [128, 128], BF16)
    S.dma('sp', ident, ident_d, writes=['ident'])
    gT = sb("gT", [128, 16])
    S.dma('sp', gT, g_attn.rearrange("(k p) -> p k", p=128), writes=['gT'])
    kn_slc = sb("kn_slc", [128, 64])
    kn_win = sb("kn_win", [128, 64])
    S.dma('sp', kn_slc, k_norm_slc.partition_broadcast(128), writes=['kn_slc'])
    S.dma('sp', kn_win, k_norm_win.partition_broadcast(128), writes=['kn_win'])

    epsb = sb("epsb", [128, 1])
    S.op('pool', lambda en: en.memset(epsb, EPS), writes=['epsb'])
    hist_all = sb("hist_all", [128, 2, 1024])
    dmy = sb("dmy", [128, 512], BF16)
    S.op('pool', lambda en: en.memset(dmy, 1.0), writes=['dmy'])

    def warm(bank_i, n=1):
        for _ in range(n * WARM):
            S.op('pe', lambda en: en.matmul(banks[bank_i], lhsT=ident, rhs=dmy, start=True, stop=True),
                 reads=['ident', 'dmy'], writes=[('bank', bank_i)])
    idx_i_perm = sb("idx_i_perm", [128, 32], I32)
    banks = [nc.alloc_psum_tensor(f"bank{i}", [128, 512], F32).ap() for i in range(8)]

    def dscr(name, shape, dt=BF16):
        return nc.dram_tensor(name, list(shape), dt, kind="Internal").ap()

    kvs_s = dscr("kvs_s", [17, 128, 1536])
    kd_s = dscr("kd_s", [17, 128, 1024])
    rv_s = dscr("rv_s", [17, 128, 1024])
    qx_s = dscr("qx_s", [17, 128, 1024])
    rg_s = dscr("rg_s", [17, 128, 1024])
    nq_s = dscr("nq_s", [17, 128, 1024])
    gate_s = dscr("gate_s", [17, 128, 48], F32)
    mix_s = dscr("mix_s", [17, 128, 2048])
    osc_s = [dscr(f"osc_s{i}", [16, 8, 4, 4, 65], F32) for i in range(3)]
    QT = list(range(7, 17))
    ALLT = list(range(17))
    w_in_v = w_in.rearrange("(k p) n -> p k n", p=128)
    bank_rr = {'n': 0}

    def next_bank(lo=2, n=6):
        b = lo + bank_rr['n'] % n
        bank_rr['n'] += 1
        return b

    GAM = [1.0 - 2.0 ** (-5 - hh) for hh in range(4)]

    with ExitStack() as esA:
        def sA(name, shape, dt=F32):
            return esA.enter_context(nc.sbuf_tensor(name, list(shape), dt)).ap()
        xnT = sA("xnT", [128, 17, 16, 128], BF16)
        wbuf = [sA("wbuf0", [128, 16, 512], BF16), sA("wbuf1", [128, 16, 512], BF16)]
        wstate = {'n': 0}

        def load_w(src_v, c0, ncols=512):
            i = wstate['n'] % 2
            wstate['n'] += 1
            S.dma('pool', wbuf[i][:, :, 0:ncols], src_v[:, :, c0:c0 + ncols], writes=[('wbuf', i)])
            return i

        def x_tile_ap(t):
            return xp[128 * t:128 * (t + 1), :] if t < 16 else xs[:, :]

        with ExitStack() as es1:
            def s1(name, shape, dt=F32):
                return es1.enter_context(nc.sbuf_tensor(name, list(shape), dt)).ap()
            xt = [s1("xt0", [128, D]), s1("xt1", [128, D])]
            xb = [s1("xb0", [128, D], BF16), s1("xb1", [128, D], BF16)]
            ss = [s1("ss0", [128, 1]), s1("ss1", [128, 1])]
            rstd = [s1("rstd0", [128, 1]), s1("rstd1", [128, 1])]
            for t in range(17):
                i = t % 2
                S.dma('sp', xt[i], x_tile_ap(t), writes=[('xt', i)])
                S.op('act', lambda en, i=i: en.activation(out=xb[i], in_=xt[i], func=AF.Square, accum_out=ss[i]),
                     reads=[('xt', i)], writes=[('xb', i), ('ss', i)])
                S.op('act', lambda en, i=i: en.activation(out=ss[i], in_=ss[i], func=AF.Sqrt, scale=1.0 / D, bias=epsb),
                     reads=[('ss', i), 'epsb'], writes=[('ss', i)])
                S.op('dve', lambda en, i=i: en.reciprocal(out=rstd[i], in_=ss[i]),
                     reads=[('ss', i)], writes=[('rstd', i)])
                S.op('act', lambda en, i=i: en.activation(out=xb[i], in_=xt[i], func=AF.Copy, scale=rstd[i]),
                     reads=[('xt', i), ('rstd', i)], writes=[('xb', i)])
                for half in range(2):
                    pb = banks[half].bitcast(BF16)
                    for kk in range(8):
                        k = half * 8 + kk
                        S.op('pe', lambda en, pb=pb, kk=kk, k=k, i=i: en.transpose(
                            out=pb[:, 128 * kk:128 * (kk + 1)], in_=xb[i][:, 128 * k:128 * (k + 1)], identity=ident),
                            reads=[('xb', i), 'ident'], writes=[('bank', half)])
                    S.op('dve', lambda en, pb=pb, half=half, t=t: en.tensor_tensor(
                        out=xnT[:, t, 8 * half:8 * half + 8, :],
                        in0=pb.rearrange("p (k n) -> p k n", k=8),
                        in1=gT[:, 8 * half:8 * half + 8].rearrange("p (k o) -> p k o", o=1).broadcast_to([128, 8, 128]),
                        op=ALU.mult),
                        reads=[('bank', half), 'gT'], writes=[('xnT', t)])
            S.barrier()

        def project_tm(t, wi, bk, ncols=512):
            for k in range(16):
                S.op('pe', lambda en, bk=bk, t=t, k=k, wi=wi: en.matmul(
                    banks[bk][:, 0:ncols], lhsT=xnT[:, t, k, :], rhs=wbuf[wi][:, k, 0:ncols],
                    start=(k == 0), stop=(k == 15)),
                    reads=[('xnT', t), ('wbuf', wi)], writes=[('bank', bk)])

        with ExitStack() as es2:
            def s2(name, shape, dt=F32):
                return es2.enter_context(nc.sbuf_tensor(name, list(shape), dt)).ap()
            rows = [s2("rows0", [128, 512]), s2("rows1", [128, 512])]
            stg = [s2("stg0", [128, 512], BF16), s2("stg1", [128, 512], BF16), s2("stg2", [128, 512], BF16)]
            gst = [s2("gst0", [128, 48]), s2("gst1", [128, 48])]
            sq, ssk, krot, ktmp = s2("sq", [128, 512]), s2("ssk", [128, 8]), s2("krot", [128, 512]), s2("ktmp", [128, 256])
            cs_all, sct, qn_t = s2("cs_all", [128, 2, 2, 128]), s2("sct", [128, 2, 2, 4]), s2("qn_t", [128, 64])
            S.dma('sp', sct, sct_d, writes=['sct'])
            S.dma('sp', qn_t, q_norm.partition_broadcast(128), writes=['qn_t'])
            S.op('dve', lambda en: en.tensor_scalar(out=qn_t, in0=qn_t, scalar1=0.125, scalar2=None, op0=ALU.mult),
                 reads=['qn_t'], writes=['qn_t'])
            cntr = {'r': 0, 's': 0, 'g': 0}

            def rms_heads(src, nh, kn, knk, key):
                v3 = src.rearrange("p (g d) -> p g d", d=64)
                sq3 = sq[:, 0:nh * 64].rearrange("p (g d) -> p g d", d=64)
                S.op('dve', lambda en: en.tensor_tensor(out=sq3, in0=v3, in1=v3, op=ALU.mult),
                     reads=[key], writes=['sq'])
                S.op('dve', lambda en: en.tensor_reduce(out=ssk[:, 0:nh], in_=sq3, axis=AX.X, op=ALU.add),
                     reads=['sq'], writes=['ssk'])
                S.op('act', lambda en: en.activation(out=ssk[:, 0:nh], in_=ssk[:, 0:nh], func=AF.Sqrt,
                                                     scale=1.0 / 64, bias=epsb),
                     reads=['ssk', 'epsb'], writes=['ssk'])
                S.op('dve', lambda en: en.reciprocal(out=ssk[:, 0:nh], in_=ssk[:, 0:nh]), reads=['ssk'], writes=['ssk'])
                S.op('dve', lambda en: en.tensor_tensor(
                    out=v3, in0=v3, in1=ssk[:, 0:nh].rearrange("p (g o) -> p g o", o=1).broadcast_to([128, nh, 64]),
                    op=ALU.mult), reads=['ssk', key], writes=[key])
                S.op('dve', lambda en: en.tensor_tensor(
                    out=v3, in0=v3, in1=kn.rearrange("p (o d) -> p o d", o=1).broadcast_to([128, nh, 64]),
                    op=ALU.mult), reads=[knk, key], writes=[key])

            def to_scratch_bf16(src, dst, keys_r, key_w, eng='act'):
                si = cntr['s'] % 3
                cntr['s'] += 1
                if eng == 'act':
                    S.op('act', lambda en, si=si: en.copy(out=stg[si], in_=src), reads=keys_r, writes=[('stg', si)])
                else:
                    S.op('dve', lambda en, si=si: en.tensor_copy(out=stg[si], in_=src), reads=keys_r, writes=[('stg', si)])
                S.dma('sp', dst, stg[si], reads=[('stg', si)], writes=[key_w])

            for j in range(3):
                wi = load_w(w_in_v, 5120 + 512 * j)
                for t in ALLT:
                    i = cntr['r'] % 2
                    cntr['r'] += 1
                    bk = next_bank()
                    project_tm(t, wi, bk)
                    S.op('act', lambda en, bk=bk, i=i: en.copy(out=rows[i], in_=banks[bk]),
                         reads=[('bank', bk)], writes=[('rows', i)])
                    if j >= 1:
                        kn, knk = (kn_slc, 'kn_slc') if j == 1 else (kn_win, 'kn_win')
                        rms_heads(rows[i][:, 0:256], 4, kn, knk, ('rows', i))
                    to_scratch_bf16(rows[i], kvs_s[t, :, 512 * j:512 * (j + 1)], [('rows', i)], ('kvs_s', t, j), eng='dve')
                    if t >= 8:
                        if j < 2:
                            r0 = 128 * (t - 8)
                            S.dma('sp', kv_out[r0:r0 + 128, 512 * j:512 * (j + 1)], rows[i],
                                  reads=[('rows', i)], writes=[('kv_out', t, j)])
                        elif 12 <= t < 16:
                            r0 = 128 * (t - 12)
                            S.dma('sp', win_p_out[r0:r0 + 128, :], rows[i], reads=[('rows', i)], writes=[('win_p', t)])
                        elif t == 16:
                            for sq_i in range(16):
                                S.dma('sp', win_s_out[sq_i, 504:512, :], rows[i][8 * sq_i:8 * sq_i + 8, :],
                                      reads=[('rows', i)], writes=[('win_s', 1, sq_i)])
            for sq_i in range(16):
                S.dma('sp', win_s_out[sq_i, 0:504, :], cache_win[sq_i, 8:512, :], writes=[('win_s', 0, sq_i)])

            def rot_group(c0, tiles, which, dst_s, dkey):
                for j in range(2):
                    wi = load_w(w_in_v, c0 + 512 * j)
                    for t in tiles:
                        bk = next_bank()
                        project_tm(t, wi, bk)
                        x = banks[bk].rearrange("p (h c f) -> p h c f", h=2, c=2)
                        x1, x2 = x[:, :, 0, :], x[:, :, 1, :]
                        ci = cntr['g'] % 2
                        cntr['g'] += 1
                        csk = ('cs_all', ci)
                        S.dma('sp', cs_all[:, ci], cs_d[t], writes=[csk])
                        cosb = cs_all[:, ci, 0:1, :].broadcast_to([128, 2, 128])
                        sinb = cs_all[:, ci, 1:2, :].broadcast_to([128, 2, 128])
                        r = krot.rearrange("p (h c f) -> p h c f", h=2, c=2)
                        r1, r2 = r[:, :, 0, :], r[:, :, 1, :]
                        tm = ktmp.rearrange("p (h f) -> p h f", h=2)
                        bkk = ('bank', bk)
                        S.op('dve', lambda en, x1=x1, cosb=cosb, r1=r1: en.tensor_tensor(out=r1, in0=x1, in1=cosb, op=ALU.mult),
                             reads=[bkk, csk], writes=['krot'])
                        S.op('dve', lambda en, x2=x2, sinb=sinb, tm=tm: en.tensor_tensor(out=tm, in0=x2, in1=sinb, op=ALU.mult),
                             reads=[bkk, csk], writes=['ktmp'])
                        S.op('dve', lambda en, r1=r1, tm=tm: en.tensor_tensor(out=r1, in0=r1, in1=tm, op=ALU.subtract),
                             reads=['krot', 'ktmp'], writes=['krot'])
                        S.op('dve', lambda en, x2=x2, cosb=cosb, r2=r2: en.tensor_tensor(out=r2, in0=x2, in1=cosb, op=ALU.mult),
                             reads=[bkk, csk], writes=['krot'])
                        S.op('dve', lambda en, x1=x1, sinb=sinb, tm=tm: en.tensor_tensor(out=tm, in0=x1, in1=sinb, op=ALU.mult),
                             reads=[bkk, csk, 'krot'], writes=['ktmp'])
                        S.op('dve', lambda en, r2=r2, tm=tm: en.tensor_tensor(out=r2, in0=r2, in1=tm, op=ALU.add),
                             reads=['krot', 'ktmp'], writes=['krot'])
                        ti = 0 if t < 16 else 1
                        si = cntr['s'] % 3
                        cntr['s'] += 1
                        S.op('dve', lambda en, j=j, ti=ti, si=si: en.tensor_tensor(
                            out=stg[si].rearrange("p (h f) -> p h f", h=2),
                            in0=krot.rearrange("p (h f) -> p h f", h=2),
                            in1=sct[:, ti, which, 2 * j:2 * j + 2].rearrange("p (h o) -> p h o", o=1).broadcast_to([128, 2, 256]),
                            op=ALU.mult), reads=['krot', 'sct'], writes=[('stg', si)])
                        S.dma('sp', dst_s[t, :, 512 * j:512 * (j + 1)], stg[si], reads=[('stg', si)], writes=[(dkey, t, j)])

            rot_group(1024, ALLT, 0, kd_s, 'kd_s')
            rot_group(0, QT, 1, qx_s, 'qx_s')
            for j in range(2):
                wi = load_w(w_in_v, 2048 + 512 * j)
                for t in ALLT:
                    bk = next_bank()
                    project_tm(t, wi, bk)
                    to_scratch_bf16(banks[bk], rv_s[t, :, 512 * j:512 * (j + 1)], [('bank', bk)], ('rv_s', t, j))
            for j in range(2):
                wi = load_w(w_in_v, 3072 + 512 * j)
                for t in QT:
                    bk = next_bank()
                    project_tm(t, wi, bk)
                    si = cntr['s'] % 3
                    cntr['s'] += 1
                    S.op('act', lambda en, bk=bk, si=si: en.activation(out=stg[si], in_=banks[bk], func=AF.Silu),
                         reads=[('bank', bk)], writes=[('stg', si)])
                    S.dma('sp', rg_s[t, :, 512 * j:512 * (j + 1)], stg[si], reads=[('stg', si)], writes=[('rg_s', t, j)])
            for j in range(2):
                wi = load_w(w_in_v, 4096 + 512 * j)
                for t in QT:
                    i = cntr['r'] % 2
                    cntr['r'] += 1
                    bk = next_bank()
                    project_tm(t, wi, bk)
                    S.op('act', lambda en, bk=bk, i=i: en.copy(out=rows[i], in_=banks[bk]),
                         reads=[('bank', bk)], writes=[('rows', i)])
                    rms_heads(rows[i], 8, qn_t, 'qn_t', ('rows', i))
                    to_scratch_bf16(rows[i], nq_s[t, :, 512 * j:512 * (j + 1)], [('rows', i)], ('nq_s', t, j), eng='dve')
            wi = load_w(w_in_v, 6656, ncols=48)
            for t in QT:
                bk = next_bank()
                project_tm(t, wi, bk, ncols=48)
                gi = t % 2
                S.op('act', lambda en, bk=bk, gi=gi: en.activation(out=gst[gi], in_=banks[bk][:, 0:48], func=AF.Sigmoid),
                     reads=[('bank', bk)], writes=[('gst', gi)])
                S.dma('sp', gate_s[t], gst[gi], reads=[('gst', gi)], writes=[('gate_s', t)])
            S.barrier()

    with ExitStack() as esB:
        def sB(name, shape, dt=F32):
            return esB.enter_context(nc.sbuf_tensor(name, list(shape), dt)).ap()
        S32, Sbf = sB("S32", [128, 8, 256]), sB("Sbf", [128, 8, 256], BF16)
        kdb = [sB("kd0", [128, 1024], BF16), sB("kd1", [128, 1024], BF16)]
        rvb = [sB("rv0", [128, 1024], BF16), sB("rv1", [128, 1024], BF16)]
        qxb = [sB("qx0", [128, 1024], BF16), sB("qx1", [128, 1024], BF16)]
        rgb = [sB("rg0", [128, 1024], BF16), sB("rg1", [128, 1024], BF16)]
        kdT, qxT = sB("kdT", [128, 8, 128], BF16), sB("qxT", [128, 8, 128], BF16)
        AT, osb, osq, oss = sB("AT", [128, 4, 128], BF16), sB("osb", [128, 4, 256]), sB("osq", [128, 4, 256]), sB("oss", [128, 4])
        retb = [sB("retb0", [128, 1024], BF16), sB("retb1", [128, 1024], BF16)]
        gn_t, mT, rmask, cmask = sB("gn_t", [128, 1024]), sB("mT", [128, 2, 128]), sB("rmask", [128, 16]), sB("cmask", [128, 16, 128], BF16)
        kdm = [sB("kdm0", [128, 1024], BF16), sB("kdm1", [128, 1024], BF16)]
        qxm = [sB("qxm0", [128, 8, 128], BF16), sB("qxm1", [128, 8, 128], BF16)]
        S0 = [sB("S0a", [128, 8, 256]), sB("S0b", [128, 8, 256])]
        S0bf = [sB("S0bfa", [128, 8, 256], BF16), sB("S0bfb", [128, 8, 256], BF16)]
        S.dma('sp', gn_t, ret_gn.rearrange("h e -> (h e)").partition_broadcast(128), writes=['gn_t'])
        S.dma('sp', mT, mT_d, writes=['mT'])
        S.dma('sp', rmask, rmask_d, writes=['rmask'])
        S.dma('sp', cmask, cmask_d, writes=['cmask'])
        S.op('pool', lambda en: en.memset(S32, 0.0), writes=['S32'])
        S.op('pool', lambda en: en.memset(Sbf, 0.0), writes=['Sbf'])

        def transpose8(src, dst, bank_i, rkey, wkey):
            pb = banks[bank_i].bitcast(BF16)
            for k in range(8):
                S.op('pe', lambda en, k=k: en.transpose(out=pb[:, 128 * k:128 * (k + 1)],
                                                        in_=src[:, 128 * k:128 * (k + 1)], identity=ident),
                     reads=[rkey, 'ident'], writes=[('bank', bank_i)])
            S.op('act', lambda en: en.copy(out=dst, in_=pb.rearrange("p (k n) -> p k n", k=8)),
                 reads=[('bank', bank_i)], writes=[wkey])

        def intra(i, mi):
            for hh in range(4):
                for c in range(2):
                    S.op('pe', lambda en, hh=hh, c=c: en.matmul(
                        banks[4][:, 128 * hh:128 * hh + 128], lhsT=kdT[:, 2 * hh + c, :], rhs=qxT[:, 2 * hh + c, :],
                        start=(c == 0), stop=(c == 1)), reads=['kdT', 'qxT'], writes=[('bank', 4)])
            S.op('dve', lambda en: en.tensor_tensor(
                out=AT, in0=banks[4].rearrange("p (h n) -> p h n", h=4),
                in1=mT[:, mi:mi + 1, :].broadcast_to([128, 4, 128]), op=ALU.mult),
                reads=[('bank', 4), 'mT'], writes=['AT'])

        def gate_out(t, i):
            S.op('act', lambda en: en.copy(out=osb[:, 0:2, :], in_=banks[2].rearrange("p (h e) -> p h e", h=2)),
                 reads=[('bank', 2)], writes=['osb'])
            S.op('act', lambda en: en.copy(out=osb[:, 2:4, :], in_=banks[3].rearrange("p (h e) -> p h e", h=2)),
                 reads=[('bank', 3)], writes=['osb'])
            S.op('dve', lambda en: en.tensor_tensor(out=osq, in0=osb, in1=osb, op=ALU.mult), reads=['osb'], writes=['osq'])
            S.op('dve', lambda en: en.tensor_reduce(out=oss, in_=osq, axis=AX.X, op=ALU.add), reads=['osq'], writes=['oss'])
            S.op('act', lambda en: en.activation(out=oss, in_=oss, func=AF.Sqrt, scale=1.0 / 256, bias=epsb),
                 reads=['oss', 'epsb'], writes=['oss'])
            S.op('dve', lambda en: en.reciprocal(out=oss, in_=oss), reads=['oss'], writes=['oss'])
            S.op('dve', lambda en: en.tensor_tensor(
                out=osb, in0=osb, in1=oss.rearrange("p (h o) -> p h o", o=1).broadcast_to([128, 4, 256]), op=ALU.mult),
                reads=['oss', 'osb'], writes=['osb'])
            S.op('dve', lambda en: en.tensor_tensor(out=osb, in0=osb, in1=gn_t.rearrange("p (h e) -> p h e", h=4), op=ALU.mult),
                 reads=['gn_t', 'osb'], writes=['osb'])
            ri = t % 2
            S.op('dve', lambda en, ri=ri, i=i: en.tensor_tensor(
                out=retb[ri].rearrange("p (h e) -> p h e", h=4), in0=osb,
                in1=rgb[i].rearrange("p (h e) -> p h e", h=4), op=ALU.mult),
                reads=['osb', ('rgb', i)], writes=[('retb', ri)])
            S.dma('sp', mix_s[t, :, 0:1024], retb[ri], reads=[('retb', ri)], writes=[('mix_s', t, 0)])

        def state_psum(lhs, lkeys, rvt, rkey):
            for hh in range(4):
                bk = (5, 6, 7, 4)[hh]
                for c in range(2):
                    S.op('pe', lambda en, hh=hh, c=c, bk=bk: en.matmul(
                        banks[bk][:, 256 * c:256 * c + 256], lhsT=lhs[:, 256 * hh + 128 * c:256 * hh + 128 * c + 128],
                        rhs=rvt[:, 256 * hh:256 * hh + 256], start=True, stop=True),
                        reads=lkeys + [rkey], writes=[('bank', bk)])

        for t in range(16):
            i = t % 2
            S.dma('sp', kdb[i], kd_s[t], reads=[('kd_s', t, 0), ('kd_s', t, 1)], writes=[('kdb', i)])
            S.dma('sp', rvb[i], rv_s[t], reads=[('rv_s', t, 0), ('rv_s', t, 1)], writes=[('rvb', i)])
            if t >= 7:
                S.dma('sp', qxb[i], qx_s[t], reads=[('qx_s', t, 0), ('qx_s', t, 1)], writes=[('qxb', i)])
                S.dma('sp', rgb[i], rg_s[t], reads=[('rg_s', t, 0), ('rg_s', t, 1)], writes=[('rgb', i)])
                transpose8(kdb[i], kdT, 0, ('kdb', i), 'kdT')
                transpose8(qxb[i], qxT, 1, ('qxb', i), 'qxT')
                intra(i, 0)
                for hh in range(4):
                    ob = banks[2 + hh // 2][:, 256 * (hh % 2):256 * (hh % 2) + 256]
                    S.op('pe', lambda en, hh=hh, ob=ob, i=i: en.matmul(ob, lhsT=AT[:, hh, :], rhs=rvb[i][:, 256 * hh:256 * hh + 256],
                                                                 start=True, stop=False),
                         reads=['AT', ('rvb', i)], writes=[('bank', 2 + hh // 2)])
                    for c in range(2):
                        S.op('pe', lambda en, hh=hh, c=c, ob=ob: en.matmul(ob, lhsT=qxT[:, 2 * hh + c, :], rhs=Sbf[:, 2 * hh + c, :],
                                                                        start=False, stop=(c == 1)),
                             reads=['qxT', 'Sbf'], writes=[('bank', 2 + hh // 2)])
                gate_out(t, i)
            state_psum(kdb[i], [('kdb', i)], rvb[i], ('rvb', i))
            for hh in range(4):
                bk = (5, 6, 7, 4)[hh]
                sv = S32[:, 2 * hh:2 * hh + 2, :]
                S.op('dve', lambda en, sv=sv, bk=bk: en.tensor_tensor(
                    out=sv, in0=sv, in1=banks[bk].rearrange("p (c e) -> p c e", c=2), op=ALU.add),
                    reads=[('bank', bk), 'S32'], writes=['S32'])
                S.op('act', lambda en, sv=sv, hh=hh: en.activation(out=sv, in_=sv, func=AF.Copy, scale=float(GAM[hh] ** 128)),
                     reads=['S32'], writes=['S32'])
            S.op('act', lambda en: en.copy(out=Sbf, in_=S32), reads=['S32'], writes=['Sbf'])
        S.dma('sp', ret_p_out.rearrange("h (c p) e -> p h c e", p=128),
              S32.rearrange("p (h c) e -> p h c e", c=2), reads=['S32'], writes=['ret_p_out'])

        t = 16
        S.dma('sp', kdb[0], kd_s[t], reads=[('kd_s', t, 0), ('kd_s', t, 1)], writes=[('kdb', 0)])
        S.dma('sp', rvb[0], rv_s[t], reads=[('rv_s', t, 0), ('rv_s', t, 1)], writes=[('rvb', 0)])
        S.dma('sp', qxb[0], qx_s[t], reads=[('qx_s', t, 0), ('qx_s', t, 1)], writes=[('qxb', 0)])
        S.dma('sp', rgb[0], rg_s[t], reads=[('rg_s', t, 0), ('rg_s', t, 1)], writes=[('rgb', 0)])
        transpose8(kdb[0], kdT, 0, ('kdb', 0), 'kdT')
        transpose8(qxb[0], qxT, 1, ('qxb', 0), 'qxT')
        intra(0, 1)
        for hh in range(4):
            ob = banks[2 + hh // 2][:, 256 * (hh % 2):256 * (hh % 2) + 256]
            S.op('pe', lambda en, hh=hh, ob=ob: en.matmul(ob, lhsT=AT[:, hh, :], rhs=rvb[0][:, 256 * hh:256 * hh + 256],
                                                         start=(hh % 2 == 0), stop=False, skip_group_check=True),
                 reads=['AT', ('rvb', 0)], writes=[('bank', 2 + hh // 2)])
        for sq_i in range(16):
            i = sq_i % 2
            S.dma('sp', S0[i].rearrange("p (h c) e -> p h c e", c=2),
                  state_ret_d[sq_i].rearrange("h (c p) e -> p h c e", p=128), writes=[('S0', i)])
            S.op('act', lambda en, i=i: en.copy(out=S0bf[i], in_=S0[i]), reads=[('S0', i)], writes=[('S0bf', i)])
            S.op('dve', lambda en, i=i, sq_i=sq_i: en.tensor_tensor(
                out=qxm[i], in0=qxT, in1=cmask[:, sq_i:sq_i + 1, :].broadcast_to([128, 8, 128]), op=ALU.mult),
                reads=['qxT', 'cmask'], writes=[('qxm', i)])
            for hh in range(4):
                ob = banks[2 + hh // 2][:, 256 * (hh % 2):256 * (hh % 2) + 256]
                for c in range(2):
                    S.op('pe', lambda en, hh=hh, c=c, ob=ob, i=i, sq_i=sq_i: en.matmul(
                        ob, lhsT=qxm[i][:, 2 * hh + c, :], rhs=S0bf[i][:, 2 * hh + c, :],
                        start=False, stop=(sq_i == 15 and c == 1), skip_group_check=True),
                        reads=[('qxm', i), ('S0bf', i)], writes=[('bank', 2 + hh // 2)])
            S.op('dve', lambda en, i=i, sq_i=sq_i: en.tensor_scalar(
                out=kdm[i], in0=kdb[0], scalar1=rmask[:, sq_i:sq_i + 1], scalar2=None, op0=ALU.mult),
                reads=[('kdb', 0), 'rmask'], writes=[('kdm', i)])
            state_psum(kdm[i], [('kdm', i)], rvb[0], ('rvb', 0))
            for hh in range(4):
                bk = (5, 6, 7, 4)[hh]
                sv = S0[i][:, 2 * hh:2 * hh + 2, :]
                S.op('dve', lambda en, sv=sv, bk=bk: en.tensor_tensor(
                    out=sv, in0=sv, in1=banks[bk].rearrange("p (c e) -> p c e", c=2), op=ALU.add),
                    reads=[('bank', bk), ('S0', i)], writes=[('S0', i)])
                S.op('act', lambda en, sv=sv, hh=hh: en.activation(out=sv, in_=sv, func=AF.Copy, scale=float(GAM[hh] ** 8)),
                     reads=[('S0', i)], writes=[('S0', i)])
            S.dma('sp', ret_s_out[sq_i].rearrange("h (c p) e -> p h c e", p=128),
                  S0[i].rearrange("p (h c) e -> p h c e", c=2), reads=[('S0', i)], writes=[('ret_s_out', sq_i)])
        gate_out(16, 0)
        S.barrier()

    def stage_C():
        NEG = -30000.0
        with ExitStack() as esC:
            def sC(name, shape, dt=F32):
                return esC.enter_context(nc.sbuf_tensor(name, list(shape), dt)).ap()
            KTc = {c: sC(f"KT{c}", [128, 4, 2048], BF16) for c in (0, 1, 2, 4)}
            Vt = {c: sC(f"V{c}", [128, 16, 4, 65], BF16) for c in (3, 5)}
            kcT = sC("kcT", [128, 4, 128], BF16)
            Vc = sC("Vc", [128, 4, 97], BF16)
            hidT = sC("hidT", [128, 16, 128], BF16)
            E_all = sC("E_all", [32, 16, 128], BF16)
            cm = sC("cm", [128, 2, 128], BF16)
            cmpm = sC("cmpm", [128, 9, 128], BF16)
            covt = sC("covt", [128, 32], F32)
            ABt = sC("ABt", [128, 2, 9, 32], F32)
            ones64 = sC("ones64", [64, 64], BF16)
            kncmp = sC("kncmp", [64, 1], F32)
            b2k = sC("b2k", [64, 1], F32)
            b2v = sC("b2v", [128, 64], F32)
            b1e = sC("b1e", [128, 4], F32)
            kvt = [sC("kvt0", [128, 1536], BF16), sC("kvt1", [128, 1536], BF16)]
            for c in (0, 1, 2, 4):
                S.op('pool', lambda en, c=c: en.memset(KTc[c], 0.0), writes=[('KT', c)])
            for c in (3, 5):
                S.op('pool', lambda en, c=c: en.memset(Vt[c], 1.0), writes=[('V', c)])
            S.op('pool', lambda en: en.memset(kcT, 0.0), writes=['kcT'])
            S.op('pool', lambda en: en.memset(Vc, 0.0), writes=['Vc'])
            S.op('pool', lambda en: en.memset(ones64, 1.0), writes=['ones64'])
            S.dma('sp', E_all, E_d, writes=['E_all'])
            S.dma('sp', cm, cm_d, writes=['cm'])
            S.dma('sp', cmpm, cmpm_d, writes=['cmpm'])
            S.dma('sp', covt, cov_d, writes=['covt'])
            S.dma('sp', ABt, AB_d, writes=['ABt'])
            S.dma('sp', kncmp, k_norm_cmp.rearrange("(d o) -> d o", o=1), writes=['kncmp'])
            S.dma('sp', b2k, cmp_b2[0].rearrange("(d o) -> d o", o=1), writes=['b2k'])
            S.dma('sp', b2v, cmp_b2[1].partition_broadcast(128), writes=['b2v'])
            for g in range(4):
                S.dma('sp', KTc[2][64:74, g, :], kaug_d, writes=[('KT', 2)])
                S.dma('sp', KTc[4][64:74, g, :], kaug_d, writes=[('KT', 4)])
                S.dma('sp', kcT[64:74, g, :], kaugc_d, writes=['kcT'])
            for t in range(16):
                i = t % 2
                S.dma('sp', kvt[i], kvs_s[t], reads=[('kvs_s', t, 0), ('kvs_s', t, 1), ('kvs_s', t, 2)], writes=[('kvt', i)])
                for ci, c in enumerate((0, 1, 2, 4)):
                    bk = ci % 2
                    pb = banks[bk].bitcast(BF16)
                    for g in range(4):
                        S.op('pe', lambda en, pb=pb, g=g, c=c, i=i: en.transpose(
                            out=pb[0:64, 128 * g:128 * g + 128], in_=kvt[i][:, 256 * c + 64 * g:256 * c + 64 * g + 64],
                            identity=ident), reads=[('kvt', i), 'ident'], writes=[('bank', bk)])
                    S.op('act', lambda en, pb=pb, c=c, t=t: en.copy(
                        out=KTc[c][0:64, :, 128 * t:128 * t + 128], in_=pb[0:64, 0:512].rearrange("p (g n) -> p g n", g=4)),
                        reads=[('bank', bk)], writes=[('KT', c)])
                for c in (3, 5):
                    S.op('dve', lambda en, c=c, t=t, i=i: en.tensor_copy(
                        out=Vt[c][:, t, :, 0:64], in_=kvt[i][:, 256 * c:256 * c + 256].rearrange("p (g d) -> p g d", g=4)),
                        reads=[('kvt', i)], writes=[('V', c)])
            if stage < 4:
                return
            with ExitStack() as esW:
                w1 = esW.enter_context(nc.sbuf_tensor("w1", [64, 2, 32, 256], BF16)).ap()
                peT = esW.enter_context(nc.sbuf_tensor("peT", [64, 2, 32], BF16)).ap()
                w2s = esW.enter_context(nc.sbuf_tensor("w2s", [128, 2, 2, 64], BF16)).ap()
                kc32 = esW.enter_context(nc.sbuf_tensor("kc32", [64, 128], F32)).ap()
                kcsq = esW.enter_context(nc.sbuf_tensor("kcsq", [64, 128], BF16)).ap()
                krs = esW.enter_context(nc.sbuf_tensor("krs", [64, 128], F32)).ap()
                b1t = esW.enter_context(nc.sbuf_tensor("b1t", [128, 4], F32)).ap()
                S.op('pool', lambda en: en.memset(kc32, 0.0), writes=['kc32'])
                for c in range(2):
                    S.dma('pool', w1[:, c], cmp_w1[c].rearrange("l d h -> d l h"), writes=['w1'])
                S.dma('pool', peT, cmp_pe.rearrange("c l d -> d c l"), writes=['peT'])
                S.dma('pool', w2s, cmp_w2.rearrange("c (k p) d -> p c k d", p=128), writes=['w2s'])
                S.dma('sp', b1t, cmp_b1.rearrange("c (k p) -> p (c k)", p=128), writes=['b1t'])
                for c in range(2):
                    for hc in range(2):
                        for l in range(32):
                            S.op('pe', lambda en, c=c, hc=hc, l=l: en.matmul(
                                banks[7][:, 2 * c + hc:2 * c + hc + 1], lhsT=w1[:, c, l, 128 * hc:128 * hc + 128],
                                rhs=peT[:, c, l:l + 1], start=(l == 0), stop=(l == 31)),
                                reads=['w1', 'peT'], writes=[('bank', 7)])
                S.op('dve', lambda en: en.tensor_tensor(out=b1e, in0=banks[7][:, 0:4], in1=b1t, op=ALU.add),
                     reads=[('bank', 7), 'b1t'], writes=['b1e'])
                for c in range(2):
                    XT = KTc[c]
                    for g in range(4):
                        Xv = XT[0:64, g, :].rearrange("p (n l) -> p l n", l=16)
                        for hc in range(2):
                            bk = 2 + (4 * c + g) % 4
                            for hf in range(2):
                                for l in range(16):
                                    first = (hf == 0 and l == 0)
                                    last = (hf == 1 and l == 15)
                                    S.op('pe', lambda en, c=c, hc=hc, hf=hf, l=l, bk=bk, Xv=Xv, first=first, last=last: en.matmul(
                                        banks[bk][:, 128 * hc:128 * hc + 127], lhsT=w1[:, c, 16 * hf + l, 128 * hc:128 * hc + 128],
                                        rhs=Xv[:, l, hf:hf + 127], start=first, stop=last),
                                        reads=['w1', ('KT', c)], writes=[('bank', bk)])
                            S.op('act', lambda en, c=c, g=g, hc=hc, bk=bk: en.activation(
                                out=hidT[:, 8 * c + 2 * g + hc, 0:127], in_=banks[bk][:, 128 * hc:128 * hc + 127], func=AF.Silu,
                                bias=b1e[:, 2 * c + hc:2 * c + hc + 1]), reads=[('bank', bk), 'b1e'], writes=['hidT'])
                for g in range(4):
                    for hc in range(2):
                        S.op('pe', lambda en, g=g, hc=hc: en.matmul(
                            banks[6][0:64, 0:127], lhsT=w2s[:, 0, hc, :], rhs=hidT[:, 2 * g + hc, 0:127],
                            start=(hc == 0), stop=(hc == 1)), reads=['w2s', 'hidT'], writes=[('bank', 6)])
                    S.op('act', lambda en: en.activation(out=kc32[:, 0:127], in_=banks[6][0:64, 0:127], func=AF.Identity, bias=b2k),
                         reads=[('bank', 6), 'b2k'], writes=['kc32'])
                    S.op('dve', lambda en: en.tensor_tensor(out=kcsq, in0=kc32, in1=kc32, op=ALU.mult), reads=['kc32'], writes=['kcsq'])
                    S.op('pe', lambda en: en.matmul(banks[7][0:64, 0:128], lhsT=ones64, rhs=kcsq, start=True, stop=True),
                         reads=['ones64', 'kcsq'], writes=[('bank', 7)])
                    S.op('act', lambda en: en.activation(out=krs, in_=banks[7][0:64, 0:128], func=AF.Sqrt, scale=1.0 / 64, bias=epsb[0:64]),
                         reads=[('bank', 7), 'epsb'], writes=['krs'])
                    S.op('dve', lambda en: en.reciprocal(out=krs, in_=krs), reads=['krs'], writes=['krs'])
                    S.op('dve', lambda en: en.tensor_tensor(out=kc32, in0=kc32, in1=krs, op=ALU.mult), reads=['kc32', 'krs'], writes=['kc32'])
                    S.op('dve', lambda en, g=g: en.tensor_scalar(out=kcT[0:64, g, 0:127], in0=kc32[:, 0:127], scalar1=kncmp, scalar2=None,
                                                                 op0=ALU.mult), reads=['kc32', 'kncmp'], writes=['kcT'])
                    for hc in range(2):
                        S.op('pe', lambda en, g=g, hc=hc: en.matmul(
                            banks[5][0:127, 0:64], lhsT=hidT[:, 8 + 2 * g + hc, 0:127], rhs=w2s[:, 1, hc, :],
                            start=(hc == 0), stop=(hc == 1)), reads=['w2s', 'hidT'], writes=[('bank', 5)])
                    S.op('dve', lambda en, g=g: en.tensor_tensor(out=Vc[0:127, g, 0:64], in0=banks[5][0:127, 0:64], in1=b2v[0:127],
                                                                 op=ALU.add), reads=[('bank', 5), 'b2v'], writes=['Vc'])
                    S.op('pool', lambda en, g=g: en.memset(Vc[:, g, 64:65], 1.0), writes=['Vc'])
                    S.op('dve', lambda en, g=g: en.tensor_copy(out=Vc[:, g, 65:97], in_=covt), reads=['covt'], writes=['Vc'])
                S.barrier()

            if stage < 5:
                return
            qT = sC("qT", [128, 16, 128], BF16)
            nqt = [sC("nqt0", [128, 1024], BF16), sC("nqt1", [128, 1024], BF16)]
            PT = [sC(f"PT{i}", [128, 4, 128], BF16) for i in range(3)]
            ob = {br: sC(f"ob{br}", [128, 16, 97], F32) for br in range(3)}
            gt = sC("gt", [128, 48], F32)
            imp = sC("imp", [128, 16, 32], F32)
            sc = sC("sc", [128, 4, 32], F32)
            sc2 = sC("sc2", [128, 32], F32)
            mx8 = sC("mx8", [128, 8], F32)
            thr = sC("thr", [128, 1], F32)
            selm = sC("selm", [128, 4, 32], F32)
            negb = sC("negb", [128, 4, 32], BF16)
            negT = sC("negT", [32, 4, 128], BF16)
            wgt = sC("wgt", [128, 16], F32)
            nsa = sC("nsa", [128, 16, 64], F32)
            ntmp = sC("ntmp", [128, 16, 64], F32)
            nsab = [sC("nsab0", [128, 1024], BF16), sC("nsab1", [128, 1024], BF16)]
            S.op('pool', lambda en: en.memset(qT, 0.0), writes=['qT'])
            cst = {'s': 0, 'p': 0, 'o': 0}

            def attend(br, g, qi, qt, kts, Kt, kkey, Vap_fn, vkey, ncol):
                obk = 5 + cst['o'] % 2
                cst['o'] += 1
                pending = [None]
                for n_, kt in enumerate(kts):
                    sb_ = 2 + cst['s'] % 3
                    cst['s'] += 1
                    extra = []
                    if br == 0:
                        extra.append((ident, cmpm[:, qi:qi + 1, :].broadcast_to([128, 4, 128]), ['ident', 'cmpm']))
                    else:
                        if br == 1:
                            extra.append((E_all[:, kt, :], negT[:, g:g + 1, :].broadcast_to([32, 4, 128]), ['E_all', 'negT']))
                        if kt == qt:
                            extra.append((ident, cm[:, 0:1, :].broadcast_to([128, 4, 128]), ['ident', 'cm']))
                        if br == 2 and kt == qt - 4:
                            extra.append((ident, cm[:, 1:2, :].broadcast_to([128, 4, 128]), ['ident', 'cm']))
                    kcols = slice(128 * kt, 128 * kt + 128) if br > 0 else slice(0, 128)
                    warm(sb_)
                    S.op('pe', lambda en, sb_=sb_, g=g, kcols=kcols, last=(len(extra) == 0): en.matmul(
                        banks[sb_], lhsT=Kt[0:74, g, kcols], rhs=qT[0:74, 4 * g:4 * g + 4, :], start=True, stop=last),
                        reads=[kkey, 'qT'], writes=[('bank', sb_)])
                    for ei, (l_, r_, ks) in enumerate(extra):
                        S.op('pe', lambda en, sb_=sb_, l_=l_, r_=r_, last=(ei == len(extra) - 1): en.matmul(
                            banks[sb_], lhsT=l_, rhs=r_, start=False, stop=last), reads=ks, writes=[('bank', sb_)])
                    pi = cst['p'] % 3
                    cst['p'] += 1
                    S.op('act', lambda en, pi=pi, sb_=sb_: en.activation(
                        out=PT[pi], in_=banks[sb_].rearrange("p (r n) -> p r n", r=4), func=AF.Exp),
                        reads=[('bank', sb_)], writes=[('PT', pi)])
                    def emit_pv(pi=pi, kt=kt, n_=n_):
                        for r in range(4):
                            S.op('pe', lambda en, r=r, pi=pi, obk=obk, kt=kt, first=(n_ == 0), last=(n_ == len(kts) - 1): en.matmul(
                                banks[obk][:, ncol * r:ncol * r + ncol], lhsT=PT[pi][:, r, :], rhs=Vap_fn(kt),
                                start=(first and r == 0), stop=last, skip_group_check=True),
                                reads=[('PT', pi), vkey], writes=[('bank', obk)])
                    if pending[0] is not None:
                        pending[0]()
                    pending[0] = emit_pv
                pending[0]()
                pending[0] = None
                S.op('act', lambda en, obk=obk, g=g, br=br: en.copy(
                    out=ob[br][:, 4 * g:4 * g + 4, 0:ncol], in_=banks[obk][:, 0:4 * ncol].rearrange("p (r c) -> p r c", r=4)),
                    reads=[('bank', obk)], writes=[('ob', br)])

            import os
            for qt in range(7, int(os.environ.get('QTL', 16))):
                qi = qt - 7
                i = qt % 2
                S.dma('sp', nqt[i], nq_s[qt], reads=[('nq_s', qt, 0), ('nq_s', qt, 1)], writes=[('nqt', i)])
                S.dma('sp', gt, gate_s[qt], reads=[('gate_s', qt)], writes=['gt'])
                S.dma('sp', qT[64:74, :, :], qaug_d[:, :, 128 * qi:128 * qi + 128], writes=['qT'])
                for half in range(2):
                    pb = banks[half].bitcast(BF16)
                    for hh in range(8):
                        hd = 8 * half + hh
                        S.op('pe', lambda en, pb=pb, hh=hh, hd=hd, i=i: en.transpose(
                            out=pb[0:64, 128 * hh:128 * hh + 128], in_=nqt[i][:, 64 * hd:64 * hd + 64], identity=ident),
                            reads=[('nqt', i), 'ident'], writes=[('bank', half)])
                    S.op('act', lambda en, pb=pb, half=half: en.copy(
                        out=qT[0:64, 8 * half:8 * half + 8, :], in_=pb[0:64, :].rearrange("p (h n) -> p h n", h=8)),
                        reads=[('bank', half)], writes=['qT'])
                for g in range(4):
                    attend(0, g, qi, qt, [0], kcT, 'kcT', lambda kt, g=g: Vc[:, g, :], 'Vc', 97)
                S.op('dve', lambda en: en.tensor_scalar(out=wgt, in0=ob[0][:, :, 64], scalar1=1e-30, scalar2=None, op0=ALU.add),
                     reads=[('ob', 0)], writes=['wgt'])
                S.op('dve', lambda en: en.reciprocal(out=wgt, in_=wgt), reads=['wgt'], writes=['wgt'])
                S.op('dve', lambda en: en.tensor_tensor(
                    out=imp, in0=ob[0][:, :, 65:97], in1=wgt.rearrange("p (h o) -> p h o", o=1).broadcast_to([128, 16, 32]),
                    op=ALU.mult), reads=[('ob', 0), 'wgt'], writes=['imp'])
                S.op('dve', lambda en: en.tensor_reduce(
                    out=sc, in_=imp.rearrange("p (g r) j -> p g j r", g=4), axis=AX.X, op=ALU.add), reads=['imp'], writes=['sc'])
                S.op('dve', lambda en, qi=qi: en.tensor_tensor(
                    out=sc, in0=sc, in1=ABt[:, 0, qi:qi + 1, :].broadcast_to([128, 4, 32]), op=ALU.mult),
                    reads=['sc', 'ABt'], writes=['sc'])
                S.op('dve', lambda en, qi=qi: en.tensor_tensor(
                    out=sc, in0=sc, in1=ABt[:, 1, qi:qi + 1, :].broadcast_to([128, 4, 32]), op=ALU.add),
                    reads=['sc', 'ABt'], writes=['sc'])
                for g in range(4):
                    S.op('dve', lambda en, g=g: en.max(out=mx8, in_=sc[:, g, :]), reads=['sc'], writes=['mx8'])
                    S.op('dve', lambda en, g=g: en.match_replace(out=sc2, in_to_replace=mx8, in_values=sc[:, g, :], imm_value=-3.0e4),
                         reads=['sc', 'mx8'], writes=['sc2'])
                    S.op('dve', lambda en: en.max(out=mx8, in_=sc2), reads=['sc2'], writes=['mx8'])
                    S.op('dve', lambda en: en.tensor_reduce(out=thr, in_=mx8, axis=AX.X, op=ALU.min), reads=['mx8'], writes=['thr'])
                    S.op('dve', lambda en, g=g: en.tensor_scalar(out=selm[:, g, :], in0=sc[:, g, :], scalar1=thr, scalar2=None,
                                                                 op0=ALU.is_ge), reads=['sc', 'thr'], writes=['selm'])
                S.op('dve', lambda en: en.tensor_scalar(out=sc, in0=sc, scalar1=-5000.0, scalar2=None, op0=ALU.is_gt),
                     reads=['sc'], writes=['sc'])
                S.op('dve', lambda en: en.tensor_tensor(out=selm, in0=selm, in1=sc, op=ALU.mult), reads=['sc', 'selm'], writes=['selm'])
                S.op('dve', lambda en: en.tensor_scalar(out=negb, in0=selm, scalar1=-1.0, scalar2=-NEG, op0=ALU.add, op1=ALU.mult),
                     reads=['selm'], writes=['negb'])
                pb7 = banks[7].bitcast(BF16)
                for g in range(4):
                    S.op('pe', lambda en, g=g: en.transpose(out=pb7[0:32, 128 * g:128 * g + 128], in_=negb[:, g, :], identity=ident),
                         reads=['negb', 'ident'], writes=[('bank', 7)])
                S.op('act', lambda en: en.copy(out=negT, in_=pb7[0:32, 0:512].rearrange("p (g n) -> p g n", g=4)),
                     reads=[('bank', 7)], writes=['negT'])
                if stage < 6:
                    continue
                for g in range(4):
                    attend(1, g, qi, qt, list(range(0, qt + 1)), KTc[2], ('KT', 2), lambda kt, g=g: Vt[3][:, kt, g, :], ('V', 3), 65)
                for g in range(4):
                    attend(2, g, qi, qt, list(range(max(0, qt - 4), qt + 1)), KTc[4], ('KT', 4),
                           lambda kt, g=g: Vt[5][:, kt, g, :], ('V', 5), 65)
                for br in range(3):
                    S.op('dve', lambda en, br=br: en.tensor_scalar(out=wgt, in0=ob[br][:, :, 64], scalar1=1e-30, scalar2=None, op0=ALU.add),
                         reads=[('ob', br)], writes=['wgt'])
                    S.op('dve', lambda en: en.reciprocal(out=wgt, in_=wgt), reads=['wgt'], writes=['wgt'])
                    S.op('dve', lambda en, br=br: en.tensor_tensor(
                        out=wgt, in0=wgt, in1=gt.rearrange("p (h b) -> p h b", b=3)[:, :, br], op=ALU.mult),
                        reads=['wgt', 'gt'], writes=['wgt'])
                    dst = nsa if br == 0 else ntmp
                    S.op('dve', lambda en, br=br, dst=dst: en.tensor_tensor(
                        out=dst, in0=ob[br][:, :, 0:64], in1=wgt.rearrange("p (h o) -> p h o", o=1).broadcast_to([128, 16, 64]),
                        op=ALU.mult), reads=[('ob', br), 'wgt'], writes=['nsa' if br == 0 else 'ntmp'])
                    if br > 0:
                        S.op('dve', lambda en: en.tensor_tensor(out=nsa, in0=nsa, in1=ntmp, op=ALU.add),
                             reads=['nsa', 'ntmp'], writes=['nsa'])
                S.op('act', lambda en, i=i: en.copy(out=nsab[i], in_=nsa.rearrange("p h d -> p (h d)")),
                     reads=['nsa'], writes=[('nsab', i)])
                S.dma('sp', mix_s[qt, :, 1024:2048], nsab[i], reads=[('nsab', i)], writes=[('mix_s', qt, 1)])
            S.barrier()


    import os
    if stage >= 3 and not os.environ.get('SKIPC'):
        stage_C()
    def stage_D():
        ckvu = cache_kv
        with ExitStack() as esD:
            def sD(name, shape, dt=F32):
                return esD.enter_context(nc.sbuf_tensor(name, list(shape), dt)).ap()
            ptb_i = sD("ptb_i", [128, 32], I32)
            ptf = sD("ptf", [128, 32])
            idx_i = idx_i_perm
            upt = sD("upt", [128, 1])
            for hf in range(2):
                srcap = page_table_d[:, 8 * hf:8 * hf + 8].rearrange("s (k o) -> k o s", o=1).broadcast_to([8, 16, 16])
                for k_ in range(8):
                    S.dma('sp', ptb_i[16 * k_:16 * k_ + 16, 16 * hf:16 * hf + 16],
                          page_table_d[:, 8 * hf + k_:8 * hf + k_ + 1].rearrange("s o -> o s").broadcast_to([16, 16]),
                          writes=['ptb_i'])
            S.dma('sp', upt, upt_d, writes=['upt'])
            S.op('dve', lambda en: en.tensor_copy(out=ptf, in_=ptb_i), reads=['ptb_i'], writes=['ptf'])
            S.op('dve', lambda en: en.tensor_scalar(out=ptf, in0=ptf, scalar1=16.0, scalar2=upt, op0=ALU.mult, op1=ALU.add),
                 reads=['ptf', 'upt'], writes=['ptf'])
            S.op('dve', lambda en: en.tensor_copy(out=idx_i[:, 0:32], in_=ptf), reads=['ptf'], writes=['idx_i'])
            E_s = sD("E_s", [33, 17, 128], BF16)
            cmS = sD("cmS", [128, 2, 128], BF16)
            covS = sD("covS", [128, 33])
            ABs = sD("ABs", [128, 2, 33])
            Rsum = sD("Rsum", [32, 16, 128], BF16)
            ones64 = sD("ones64d", [64, 64], BF16)
            kncmp, b2k, b2v, b1e = sD("kncmpd", [64, 1]), sD("b2kd", [64, 1]), sD("b2vd", [128, 64]), sD("b1ed", [128, 4])
            S.dma('sp', E_s, Es_d, writes=['E_s'])
            S.dma('sp', cmS, cmS_d, writes=['cmS'])
            S.dma('sp', covS, covS_d, writes=['covS'])
            S.dma('sp', ABs, ABs_d, writes=['ABs'])
            S.dma('sp', Rsum, Rsum_d, writes=['Rsum'])
            S.op('pool', lambda en: en.memset(ones64, 1.0), writes=['ones64'])
            S.dma('sp', kncmp, k_norm_cmp.rearrange("(d o) -> d o", o=1), writes=['kncmp'])
            S.dma('sp', b2k, cmp_b2[0].rearrange("(d o) -> d o", o=1), writes=['b2k'])
            S.dma('sp', b2v, cmp_b2[1].partition_broadcast(128), writes=['b2v'])
            qTs = [sD("qTs0", [128, 16, 128], BF16), sD("qTs1", [128, 16, 128], BF16)]
            nqt = sD("nqtd", [128, 1024], BF16)
            S.dma('sp', nqt, nq_s[16], reads=[('nq_s', 16, 0), ('nq_s', 16, 1)], writes=['nqt'])
            for v in range(2):
                S.op('pool', lambda en, v=v: en.memset(qTs[v], 0.0), writes=[('qTs', v)])
                S.dma('sp', qTs[v][64:74, :, :], qaugs_d[v], writes=[('qTs', v)])
            for half in range(2):
                pb = banks[half].bitcast(BF16)
                for hh in range(8):
                    hd = 8 * half + hh
                    S.op('pe', lambda en, pb=pb, hh=hh, hd=hd: en.transpose(
                        out=pb[0:64, 128 * hh:128 * hh + 128], in_=nqt[:, 64 * hd:64 * hd + 64], identity=ident),
                        reads=['nqt', 'ident'], writes=[('bank', half)])
                for v in range(2):
                    S.op('act', lambda en, pb=pb, half=half, v=v: en.copy(
                        out=qTs[v][0:64, 8 * half:8 * half + 8, :], in_=pb[0:64, :].rearrange("p (h n) -> p h n", h=8)),
                        reads=[('bank', half)], writes=[('qTs', v)])
            G = [sD("G0", [128, 8, 1024], BF16), sD("G1", [128, 8, 1024], BF16)]
            hb16 = [sD(f"hb16{i}", [128, 1024], BF16) for i in range(2)]
            cwt = hist_all
            XT = [sD("XTk", [64, 4, 2048], BF16), sD("XTv", [64, 4, 2048], BF16)]
            w1 = sD("w1d", [64, 2, 32, 256], BF16)
            peT = sD("peTd", [64, 2, 32], BF16)
            w2s = sD("w2sd", [128, 2, 2, 64], BF16)
            b1t = sD("b1td", [128, 4])
            hidT = sD("hidTd", [128, 4, 512], BF16)
            kcT = sD("kcTd", [128, 4, 128], BF16)
            Vc = sD("Vcd", [128, 4, 98], BF16)
            kc32 = sD("kc32d", [64, 512])
            kcsq = sD("kcsqd", [64, 512], BF16)
            krs = sD("krsd", [64, 512])
            rden = sD("rdend", [32, 4])
            Unb = sD("Unbd", [32, 4, 33], BF16)
            KS = sD("KS", [128, 4, 2176], BF16)
            KW = sD("KW", [128, 4, 640], BF16)
            VS = sD("VS", [128, 17, 4, 65], BF16)
            VW = sD("VW", [128, 5, 4, 65], BF16)
            newt = sD("newt", [128, 1536], BF16)
            obr = [sD("ocsd", [32, 4, 98]), sD("ossd", [32, 4, 65]), sD("owsd", [32, 4, 65])]
            PT = [sD(f"PTd{i}", [128, 128], BF16) for i in range(3)]
            sc = sD("scd", [128, 4, 33])
            sc2 = sD("sc2d", [128, 33])
            mx8 = sD("mx8d", [128, 8])
            thr = sD("thrd", [128, 1])
            selm = sD("selmd", [128, 4, 33])
            negb = sD("negbd", [128, 4, 33], BF16)
            negT = sD("negTd", [33, 4, 128], BF16)
            negTx = sD("negTxd", [33, 4, 4, 8], BF16)
            cst = {'h': 0, 's': 0, 'p': 0}
            for c in range(2):
                S.dma('pool', w1[:, c], cmp_w1[c].rearrange("l d h -> d l h"), writes=['w1'])
            S.dma('pool', peT, cmp_pe.rearrange("c l d -> d c l"), writes=['peT'])
            S.dma('pool', w2s, cmp_w2.rearrange("c (k p) d -> p c k d", p=128), writes=['w2s'])
            S.dma('sp', b1t, cmp_b1.rearrange("c (k p) -> p (c k)", p=128), writes=['b1t'])
            for buf, key, val in ((kcT, 'kcT', 0.0), (Vc, 'Vc', 0.0), (kc32, 'kc32', 0.0), (hidT, 'hidT', 0.0), (KS, 'KS', 0.0),
                                  (KW, 'KW', 0.0), (VS, 'VS', 1.0), (VW, 'VW', 1.0), (newt, 'newt', 0.0)):
                S.op('pool', lambda en, buf=buf, val=val: en.memset(buf, val), writes=[key])
            for g in range(4):
                S.dma('sp', kcT[64:74, g, :], kaugcS_d, writes=['kcT'])
                S.op('pool', lambda en, g=g: en.memset(Vc[:, g, 64:65], 1.0), writes=['Vc'])
                S.op('dve', lambda en, g=g: en.tensor_copy(out=Vc[:, g, 65:98], in_=covS), reads=['covS'], writes=['Vc'])
                S.dma('sp', KS[64:74, g, :], kaugS_d, writes=['KS'])
                S.dma('sp', KW[64:74, g, :], kaugW_d, writes=['KW'])
            for c in range(2):
                for hc in range(2):
                    for l in range(32):
                        S.op('pe', lambda en, c=c, hc=hc, l=l: en.matmul(
                            banks[7][:, 2 * c + hc:2 * c + hc + 1], lhsT=w1[:, c, l, 128 * hc:128 * hc + 128],
                            rhs=peT[:, c, l:l + 1], start=(l == 0), stop=(l == 31)),
                            reads=['w1', 'peT'], writes=[('bank', 7)])
            S.op('dve', lambda en: en.tensor_tensor(out=b1e, in0=banks[7][:, 0:4], in1=b1t, op=ALU.add),
                 reads=[('bank', 7), 'b1t'], writes=['b1e'])

            def transpose_g(src, skey, c0, dst_fn, dkey):
                bk = cst['s'] % 5
                cst['s'] += 1
                warm(bk)
                pb = banks[bk].bitcast(BF16)
                for g in range(4):
                    S.op('pe', lambda en, pb=pb, g=g: en.transpose(
                        out=pb[0:64, 128 * g:128 * g + 128], in_=src[:, c0 + 64 * g:c0 + 64 * g + 64], identity=ident),
                        reads=[skey, 'ident'], writes=[('bank', bk)])
                S.op('act', lambda en, pb=pb: en.copy(out=dst_fn(), in_=pb[0:64, 0:512].rearrange("p (g n) -> p g n", g=4)),
                     reads=[('bank', bk)], writes=[dkey])

            def scores(bank_i, Kt, kkey, kcols, qv, sq_i, extra):
                warm(bank_i)
                for g in range(4):
                    S.op('pe', lambda en, g=g, last=(g == 3 and not extra): en.matmul(
                        banks[bank_i][:, 32 * g:32 * g + 32], lhsT=Kt[0:74, g, kcols],
                        rhs=qTs[qv][0:74, 4 * g:4 * g + 4, 8 * sq_i:8 * sq_i + 8],
                        start=(g == 0), stop=last, skip_group_check=True),
                        reads=[kkey, ('qTs', qv)], writes=[('bank', bank_i)])
                for ei, (l_, r_, ks) in enumerate(extra):
                    S.op('pe', lambda en, l_=l_, r_=r_, last=(ei == len(extra) - 1): en.matmul(
                        banks[bank_i][:, 0:128], lhsT=l_, rhs=r_, start=False, stop=last, skip_group_check=True),
                        reads=ks, writes=[('bank', bank_i)])
                pi = cst['p'] % 3
                cst['p'] += 1
                S.op('act', lambda en, pi=pi: en.activation(out=PT[pi], in_=banks[bank_i][:, 0:128], func=AF.Exp),
                     reads=[('bank', bank_i)], writes=[('PT', pi)])
                return pi

            def pv(obk, pi, Vfn, vkey, ncol, first, last):
                for g in range(4):
                    S.op('pe', lambda en, g=g: en.matmul(
                        banks[obk][0:32, ncol * g:ncol * g + ncol], lhsT=PT[pi][:, 32 * g:32 * g + 32], rhs=Vfn(g),
                        start=(first and g == 0), stop=last, skip_group_check=True),
                        reads=[('PT', pi), vkey], writes=[('bank', obk)])

            def store_branch(br, sq_i, obk, ncol):
                S.op('act', lambda en: en.copy(out=obr[br], in_=banks[obk][0:32, 0:4 * ncol].rearrange("p (g c) -> p g c", g=4)),
                     reads=[('bank', obk)], writes=[('obr', br)])
                for r in range(4):
                    S.dma('sp', osc_s[br][sq_i, :, :, r, :], obr[br][8 * r:8 * r + 8, :, 0:65],
                          reads=[('obr', br)], writes=[('osc_s', br)])

            for sq_i in range(int(os.environ.get('SEQL', 16))):
                for hf in range(2):
                    S.dma_gather(G[hf].rearrange("p j c -> p (j c)"), ckvu, idx_i[:, 16 * hf + sq_i:16 * hf + sq_i + 1], 2560 * 16,
                                 reads=['idx_i'], writes=[('G', hf)])
                    for j in range(8):
                        kt = 2 * j + hf
                        Gj = G[hf][:, j, :]
                        for c in range(2):
                            transpose_g(Gj, ('G', hf), 256 * c,
                                        lambda c=c, kt=kt: XT[c][0:64, :, 128 * kt:128 * kt + 128], ('XT', c))
                        transpose_g(Gj, ('G', hf), 512, lambda kt=kt: KS[0:64, :, 128 * kt:128 * kt + 128], 'KS')
                        S.op('dve', lambda en, Gj=Gj, kt=kt: en.tensor_copy(
                            out=VS[:, kt, :, 0:64], in_=Gj[:, 768:1024].rearrange("p (g d) -> p g d", g=4)),
                            reads=[('G', hf)], writes=['VS'])
                for k in range(4):
                    S.dma('sp', cwt[:, k % 2, 0:512], cache_win[sq_i, 128 * k:128 * k + 128, :], writes=[('cwt', k % 2)])
                    hi = cst['h'] % 2
                    cst['h'] += 1
                    S.op('dve', lambda en, hi=hi, k=k: en.tensor_copy(out=hb16[hi][:, 0:512], in_=cwt[:, k % 2, 0:512]),
                         reads=[('cwt', k % 2)], writes=[('hb16', hi)])
                    transpose_g(hb16[hi], ('hb16', hi), 0, lambda k=k: KW[0:64, :, 128 * k:128 * k + 128], 'KW')
                    S.op('dve', lambda en, hi=hi, k=k: en.tensor_copy(
                        out=VW[:, k, :, 0:64], in_=hb16[hi][:, 256:512].rearrange("p (g d) -> p g d", g=4)),
                        reads=[('hb16', hi)], writes=['VW'])
                S.dma('sp', newt[0:8, :], kvs_s[16, 8 * sq_i:8 * sq_i + 8, :],
                      reads=[('kvs_s', 16, 0), ('kvs_s', 16, 1), ('kvs_s', 16, 2)], writes=['newt'])
                transpose_g(newt, 'newt', 512, lambda: KS[0:64, :, 2048:2176], 'KS')
                transpose_g(newt, 'newt', 1024, lambda: KW[0:64, :, 512:640], 'KW')
                S.op('dve', lambda en: en.tensor_copy(out=VS[:, 16, :, 0:64], in_=newt[:, 768:1024].rearrange("p (g d) -> p g d", g=4)),
                     reads=['newt'], writes=['VS'])
                S.op('dve', lambda en: en.tensor_copy(out=VW[:, 4, :, 0:64], in_=newt[:, 1280:1536].rearrange("p (g d) -> p g d", g=4)),
                     reads=['newt'], writes=['VW'])
                for c in range(2):
                    Xv = XT[c].rearrange("p g (j n two) -> p g j two n", j=8, two=2)
                    for hc in range(2):
                        bk = 2 + (2 * c + hc) % 3
                        for hf in range(2):
                            for l in range(16):
                                S.op('pe', lambda en, c=c, hc=hc, hf=hf, l=l, bk=bk, Xv=Xv: en.matmul(
                                    banks[bk][:, 0:508].rearrange("p (g n) -> p g n", g=4),
                                    lhsT=w1[:, c, 16 * hf + l, 128 * hc:128 * hc + 128],
                                    rhs=Xv[:, :, l % 8, l // 8, hf:hf + 127], start=(hf == 0 and l == 0), stop=(hf == 1 and l == 15)),
                                    reads=['w1', ('XT', c)], writes=[('bank', bk)])
                        S.op('act', lambda en, c=c, hc=hc, bk=bk: en.activation(
                            out=hidT[:, 2 * c + hc, 0:508], in_=banks[bk][:, 0:508], func=AF.Silu,
                            bias=b1e[:, 2 * c + hc:2 * c + hc + 1]), reads=[('bank', bk), 'b1e'], writes=['hidT'])
                for hc in range(2):
                    S.op('pe', lambda en, hc=hc: en.matmul(banks[6][0:64, 0:508], lhsT=w2s[:, 0, hc, :], rhs=hidT[:, hc, 0:508],
                                                           start=(hc == 0), stop=(hc == 1)),
                         reads=['w2s', 'hidT'], writes=[('bank', 6)])
                S.op('act', lambda en: en.activation(out=kc32[:, 0:508], in_=banks[6][0:64, 0:508], func=AF.Identity, bias=b2k),
                     reads=[('bank', 6), 'b2k'], writes=['kc32'])
                S.op('dve', lambda en: en.tensor_tensor(out=kcsq, in0=kc32, in1=kc32, op=ALU.mult), reads=['kc32'], writes=['kcsq'])
                S.op('pe', lambda en: en.matmul(banks[7][0:64, 0:512], lhsT=ones64, rhs=kcsq, start=True, stop=True),
                     reads=['ones64', 'kcsq'], writes=[('bank', 7)])
                S.op('act', lambda en: en.activation(out=krs, in_=banks[7][0:64, 0:512], func=AF.Sqrt, scale=1.0 / 64, bias=epsb[0:64]),
                     reads=[('bank', 7), 'epsb'], writes=['krs'])
                S.op('dve', lambda en: en.reciprocal(out=krs, in_=krs), reads=['krs'], writes=['krs'])
                S.op('dve', lambda en: en.tensor_tensor(out=kc32, in0=kc32, in1=krs, op=ALU.mult), reads=['kc32', 'krs'], writes=['kc32'])
                S.op('dve', lambda en: en.tensor_scalar(
                    out=kcT[0:64, :, 0:127], in0=kc32[:, 0:508].rearrange("p (g n) -> p g n", g=4), scalar1=kncmp, scalar2=None,
                    op0=ALU.mult), reads=['kc32', 'kncmp'], writes=['kcT'])
                for g in range(4):
                    for hc in range(2):
                        S.op('pe', lambda en, g=g, hc=hc: en.matmul(
                            banks[5][0:127, 64 * g:64 * g + 64], lhsT=hidT[:, 2 + hc, 127 * g:127 * g + 127], rhs=w2s[:, 1, hc, :],
                            start=(g == 0 and hc == 0), stop=(hc == 1), skip_group_check=True),
                            reads=['w2s', 'hidT'], writes=[('bank', 5)])
                S.op('dve', lambda en: en.tensor_tensor(
                    out=Vc[0:127, :, 0:64], in0=banks[5][0:127, 0:256].rearrange("p (g d) -> p g d", g=4),
                    in1=b2v[0:127].rearrange("p (o d) -> p o d", o=1).broadcast_to([127, 4, 64]), op=ALU.add),
                    reads=[('bank', 5), 'b2v'], writes=['Vc'])
                pi = scores(2, kcT, 'kcT', slice(0, 128), 0, sq_i, [])
                pv(3, pi, lambda g: Vc[:, g, :], 'Vc', 98, True, True)
                store_branch(0, sq_i, 3, 98)
                S.op('dve', lambda en: en.tensor_scalar(out=rden, in0=obr[0][:, :, 64], scalar1=1e-30, scalar2=None, op0=ALU.add),
                     reads=[('obr', 0)], writes=['rden'])
                S.op('dve', lambda en: en.reciprocal(out=rden, in_=rden), reads=['rden'], writes=['rden'])
                S.op('dve', lambda en: en.tensor_tensor(
                    out=Unb, in0=obr[0][:, :, 65:98], in1=rden.rearrange("p (g o) -> p g o", o=1).broadcast_to([32, 4, 33]),
                    op=ALU.mult), reads=[('obr', 0), 'rden'], writes=['Unb'])
                S.op('pe', lambda en, sq_i=sq_i: en.matmul(
                    banks[4][:, 0:132], lhsT=Rsum[:, sq_i, :], rhs=Unb.rearrange("p g j -> p (g j)"), start=True, stop=True),
                    reads=['Rsum', 'Unb'], writes=[('bank', 4)])
                S.op('dve', lambda en: en.tensor_tensor(
                    out=sc, in0=banks[4][:, 0:132].rearrange("p (g j) -> p g j", g=4),
                    in1=ABs[:, 0:1, :].broadcast_to([128, 4, 33]), op=ALU.mult), reads=[('bank', 4), 'ABs'], writes=['sc'])
                S.op('dve', lambda en: en.tensor_tensor(out=sc, in0=sc, in1=ABs[:, 1:2, :].broadcast_to([128, 4, 33]), op=ALU.add),
                     reads=['sc', 'ABs'], writes=['sc'])
                for g in range(4):
                    S.op('dve', lambda en, g=g: en.max(out=mx8, in_=sc[:, g, :]), reads=['sc'], writes=['mx8'])
                    S.op('dve', lambda en, g=g: en.match_replace(out=sc2, in_to_replace=mx8, in_values=sc[:, g, :], imm_value=-3.0e4),
                         reads=['sc', 'mx8'], writes=['sc2'])
                    S.op('dve', lambda en: en.max(out=mx8, in_=sc2), reads=['sc2'], writes=['mx8'])
                    S.op('dve', lambda en: en.tensor_reduce(out=thr, in_=mx8, axis=AX.X, op=ALU.min), reads=['mx8'], writes=['thr'])
                    S.op('dve', lambda en, g=g: en.tensor_scalar(out=selm[:, g, :], in0=sc[:, g, :], scalar1=thr, scalar2=None,
                                                                 op0=ALU.is_ge), reads=['sc', 'thr'], writes=['selm'])
                S.op('dve', lambda en: en.tensor_scalar(out=negb, in0=selm, scalar1=-1.0, scalar2=-NEG, op0=ALU.add, op1=ALU.mult),
                     reads=['selm'], writes=['negb'])
                for kt in range(5):
                    extra = []
                    if kt == 0:
                        extra.append((ident, cmS[:, 1, :], ['ident', 'cmS']))
                    if kt == 4:
                        extra.append((ident, cmS[:, 0, :], ['ident', 'cmS']))
                    pi = scores(2 + kt % 3, KW, 'KW', slice(128 * kt, 128 * kt + 128), 1, sq_i, extra)
                    if kt > 0:
                        pv(6, pprev[0], lambda g, kt=kt: VW[:, kt - 1, g, :], 'VW', 65, kt - 1 == 0, False)
                    pprev = [pi]
                pv(6, pprev[0], lambda g: VW[:, 4, g, :], 'VW', 65, False, True)
                store_branch(2, sq_i, 6, 65)
                pb7 = banks[7].bitcast(BF16)
                for g in range(4):
                    S.op('pe', lambda en, g=g, pb7=pb7: en.transpose(out=pb7[0:33, 128 * g:128 * g + 128], in_=negb[:, g, :], identity=ident),
                         reads=['negb', 'ident'], writes=[('bank', 7)])
                S.op('act', lambda en, pb7=pb7: en.copy(out=negT, in_=pb7[0:33, 0:512].rearrange("p (g n) -> p g n", g=4)),
                     reads=[('bank', 7)], writes=['negT'])
                for g in range(4):
                    S.op('dve', lambda en, g=g, sq_i=sq_i: en.tensor_copy(
                        out=negTx[:, g, :, :], in_=negT[:, g:g + 1, 8 * sq_i:8 * sq_i + 8].broadcast_to([33, 4, 8])),
                        reads=['negT'], writes=['negTx'])
                for kt in range(17):
                    extra = [(E_s[:, kt, :], negTx.rearrange("p g r q -> p (g r q)"), ['E_s', 'negTx'])]
                    if kt == 16:
                        extra.append((ident, cmS[:, 0, :], ['ident', 'cmS']))
                    pi = scores(2 + kt % 3, KS, 'KS', slice(128 * kt, 128 * kt + 128), 0, sq_i, extra)
                    if kt > 0:
                        pv(5, pprev[0], lambda g, kt=kt: VS[:, kt - 1, g, :], 'VS', 65, kt - 1 == 0, False)
                    pprev = [pi]
                pv(5, pprev[0], lambda g: VS[:, 16, g, :], 'VS', 65, False, True)
                store_branch(1, sq_i, 5, 65)
            S.barrier()
        with ExitStack() as esF:
            def sF(name, shape, dt=F32):
                return esF.enter_context(nc.sbuf_tensor(name, list(shape), dt)).ap()
            ob = {}
            for br in range(3):
                ob[br] = sF(f"obd{br}", [128, 16, 65])
                S.dma('sp', ob[br], osc_s[br].rearrange("s q g r c -> (s q) (g r) c"), reads=[('osc_s', br)], writes=[('ob', br)])
            gt = sF("gtd", [128, 48])
            S.dma('sp', gt, gate_s[16], reads=[('gate_s', 16)], writes=['gt'])
            wgt = sF("wgtd", [128, 16])
            nsa = sF("nsad", [128, 16, 64])
            ntmp = sF("ntmpd", [128, 16, 64])
            nsab = sF("nsabd", [128, 1024], BF16)
            for br in range(3):
                S.op('dve', lambda en, br=br: en.tensor_scalar(out=wgt, in0=ob[br][:, :, 64], scalar1=1e-30, scalar2=None, op0=ALU.add),
                     reads=[('ob', br)], writes=['wgt'])
                S.op('dve', lambda en: en.reciprocal(out=wgt, in_=wgt), reads=['wgt'], writes=['wgt'])
                S.op('dve', lambda en, br=br: en.tensor_tensor(
                    out=wgt, in0=wgt, in1=gt.rearrange("p (h b) -> p h b", b=3)[:, :, br], op=ALU.mult),
                    reads=['wgt', 'gt'], writes=['wgt'])
                dst = nsa if br == 0 else ntmp
                S.op('dve', lambda en, br=br, dst=dst: en.tensor_tensor(
                    out=dst, in0=ob[br][:, :, 0:64], in1=wgt.rearrange("p (h o) -> p h o", o=1).broadcast_to([128, 16, 64]),
                    op=ALU.mult), reads=[('ob', br), 'wgt'], writes=['nsa' if br == 0 else 'ntmp'])
                if br > 0:
                    S.op('dve', lambda en: en.tensor_tensor(out=nsa, in0=nsa, in1=ntmp, op=ALU.add),
                         reads=['nsa', 'ntmp'], writes=['nsa'])
            S.op('act', lambda en: en.copy(out=nsab, in_=nsa.rearrange("p h d -> p (h d)")), reads=['nsa'], writes=['nsab'])
            S.dma('sp', mix_s[16, :, 1024:2048], nsab, reads=['nsab'], writes=[('mix_s', 16, 1)])
            S.barrier()
    if stage >= 7:
        stage_D()

    def stage_E():
        with ExitStack() as esE:
            def sE(name, shape, dt=F32):
                return esE.enter_context(nc.sbuf_tensor(name, list(shape), dt)).ap()
            hres = sE("hres", [128, 10, D])
            for n_, t in enumerate(QT):
                src = xp[128 * t:128 * (t + 1), :] if t < 16 else xs[:, :]
                S.dma('sp', hres[:, n_, :], src, writes=[('hres', n_)])
            with ExitStack() as es1:
                wo = es1.enter_context(nc.sbuf_tensor("wo", [128, 16, D], BF16)).ap()
                mxt = [es1.enter_context(nc.sbuf_tensor(f"mxt{i}", [128, D], BF16)).ap() for i in range(2)]
                mxT = [es1.enter_context(nc.sbuf_tensor(f"mxT{i}", [128, 16, 128], BF16)).ap() for i in range(2)]
                w_out_v = w_out.rearrange("(k p) n -> p k n", p=128)
                for j in range(4):
                    S.dma('pool', wo[:, :, 512 * j:512 * j + 512], w_out_v[:, :, 512 * j:512 * j + 512], writes=[('wo', j)])
                for n_, t in enumerate(QT):
                    i = n_ % 2
                    S.dma('sp', mxt[i], mix_s[t], reads=[('mix_s', t, 0), ('mix_s', t, 1)], writes=[('mxt', i)])
                    for half in range(2):
                        pb = banks[half].bitcast(BF16)
                        for kk in range(8):
                            k = 8 * half + kk
                            S.op('pe', lambda en, pb=pb, kk=kk, k=k, i=i: en.transpose(
                                out=pb[:, 128 * kk:128 * kk + 128], in_=mxt[i][:, 128 * k:128 * k + 128], identity=ident),
                                reads=[('mxt', i), 'ident'], writes=[('bank', half)])
                        S.op('act', lambda en, pb=pb, half=half, i=i: en.copy(
                            out=mxT[i][:, 8 * half:8 * half + 8, :], in_=pb.rearrange("p (k n) -> p k n", k=8)),
                            reads=[('bank', half)], writes=[('mxT', i)])
                    for j in range(4):
                        bk = next_bank()
                        for k in range(16):
                            S.op('pe', lambda en, bk=bk, k=k, j=j, i=i: en.matmul(
                                banks[bk], lhsT=mxT[i][:, k, :], rhs=wo[:, k, 512 * j:512 * j + 512],
                                start=(k == 0), stop=(k == 15)), reads=[('mxT', i), ('wo', j)], writes=[('bank', bk)])
                        hv = hres[:, n_, 512 * j:512 * j + 512]
                        S.op('dve', lambda en, hv=hv, bk=bk: en.tensor_tensor(out=hv, in0=hv, in1=banks[bk], op=ALU.add),
                             reads=[('bank', bk), ('hres', n_)], writes=[('hres', n_)])
                S.barrier()
            hnT = sE("hnT", [128, 10, 16, 128], BF16)
            gfT = sE("gfT", [128, 16])
            S.dma('sp', gfT, g_ffn.rearrange("(k p) -> p k", p=128), writes=['gfT'])
            with ExitStack() as es2:
                hb = [es2.enter_context(nc.sbuf_tensor(f"hb{i}", [128, D], BF16)).ap() for i in range(2)]
                hss = [es2.enter_context(nc.sbuf_tensor(f"hss{i}", [128, 1], F32)).ap() for i in range(2)]
                for n_ in range(10):
                    i = n_ % 2
                    hv = hres[:, n_, :]
                    S.op('act', lambda en, i=i, hv=hv: en.activation(out=hb[i], in_=hv, func=AF.Square, accum_out=hss[i]),
                         reads=[('hres', n_)], writes=[('hb', i), ('hss', i)])
                    S.op('act', lambda en, i=i: en.activation(out=hss[i], in_=hss[i], func=AF.Sqrt, scale=1.0 / D, bias=epsb),
                         reads=[('hss', i), 'epsb'], writes=[('hss', i)])
                    S.op('dve', lambda en, i=i: en.reciprocal(out=hss[i], in_=hss[i]), reads=[('hss', i)], writes=[('hss', i)])
                    S.op('act', lambda en, i=i, hv=hv: en.activation(out=hb[i], in_=hv, func=AF.Copy, scale=hss[i]),
                         reads=[('hres', n_), ('hss', i)], writes=[('hb', i)])
                    for half in range(2):
                        pb = banks[half].bitcast(BF16)
                        for kk in range(8):
                            k = 8 * half + kk
                            S.op('pe', lambda en, pb=pb, kk=kk, k=k, i=i: en.transpose(
                                out=pb[:, 128 * kk:128 * kk + 128], in_=hb[i][:, 128 * k:128 * k + 128], identity=ident),
                                reads=[('hb', i), 'ident'], writes=[('bank', half)])
                        S.op('dve', lambda en, pb=pb, half=half, n_=n_: en.tensor_tensor(
                            out=hnT[:, n_, 8 * half:8 * half + 8, :], in0=pb.rearrange("p (k n) -> p k n", k=8),
                            in1=gfT[:, 8 * half:8 * half + 8].rearrange("p (k o) -> p k o", o=1).broadcast_to([128, 8, 128]),
                            op=ALU.mult), reads=[('bank', half), 'gfT'], writes=['hnT'])
                S.barrier()
            SBW = 256
            NSB = 5632 // SBW
            wu = {(ab, i): sE(f"wu{ab}{i}", [128, 16, SBW], BF16) for ab in range(2) for i in range(2)}
            wd = [hist_all.rearrange("p a n -> p (a n)").bitcast(BF16).rearrange("p (c n) -> p c n", c=2), sE("wd1", [128, 2, D], BF16)]
            identf = sE("identf_sb", [128, 128])
            S.dma('sp', identf, identf_d, writes=['identf'])
            hfl = sE("hfl", [128, 1])
            ust = sE("ust", [128, 32])
            S.dma('sp', hfl, hflag_d.partition_broadcast(128), writes=['hfl'])
            cw = sE("cw", [128, 3, 88])
            cbv = sE("cbv", [128, 88])
            for j in range(3):
                S.dma('sp', cw[:, j, :], conv_w[j].rearrange("(k p) -> p k", p=128), writes=['cw'])
            S.dma('sp', cbv, conv_b.rearrange("(k p) -> p k", p=128), writes=['cbv'])
            up_ = {ab: sE(f"up{ab}", [128, 1154]) for ab in range(2)}
            us_ = {ab: sE(f"us{ab}", [128, 16, 10]) for ab in range(2)}
            cp_ = {ab: sE(f"cp{ab}", [128, 1152]) for ab in range(2)}
            cs2_ = {ab: sE(f"cs2{ab}", [128, 16, 8]) for ab in range(2)}
            actT = [sE(f"actT{i}", [128, 2, 1280], BF16) for i in range(2)]
            pvt0 = sE("pvt0", [32, 2, SBW]); pvt = [pvt0, pvt0]
            cvo0 = sE("cvo0", [34, 2, SBW]); cvo = [cvo0, cvo0]
            for ab in range(2):
                S.op('pool', lambda en, ab=ab: en.memset(up_[ab][:, 0:2], 0.0), writes=[('up', ab)])
            w_up_v = w_up.rearrange("(k p) n -> p k n", p=128)
            w_down_v = w_down.rearrange("(c p) n -> p c n", p=128)
            state_conv_v = state_conv_d.rearrange("s j n -> (s j) n")
            TG = [(0, 4), (4, 4), (8, 2)]
            pend_down = [None]
            for sbi in range(NSB):
                i = sbi % 2
                for ab in range(2):
                    S.dma('pool', wu[(ab, i)], w_up_v[:, :, 5632 * ab + SBW * sbi:5632 * ab + SBW * sbi + SBW],
                          writes=[('wu', ab, i)])
                    S.dma('sp', pvt[i][:, ab, :], state_conv_v[:, 5632 * ab + SBW * sbi:5632 * ab + SBW * sbi + SBW],
                          writes=[('pvt', 0)])
                S.dma('pool', wd[i], w_down_v[:, 2 * sbi:2 * sbi + 2, :], writes=[('wd', i)])
                for fc in range(2):
                    kch = 2 * sbi + fc
                    for ab in range(2):
                        kk = 44 * ab + kch
                        bkp = next_bank()
                        S.op('pe', lambda en, bkp=bkp, i=i, ab=ab, fc=fc: en.transpose(
                            out=banks[bkp][:, 0:32], in_=pvt[i][:, ab, 128 * fc:128 * fc + 128], identity=identf[0:32, 0:32]),
                            reads=[('pvt', 0), 'identf'], writes=[('bank', bkp)])
                        S.op('act', lambda en, bkp=bkp, ab=ab: en.copy(
                            out=us_[ab][:, :, 0:2], in_=banks[bkp][:, 0:32].rearrange("p (s j) -> p s j", j=2)),
                            reads=[('bank', bkp)], writes=[('us', ab)])
                        for (t0, nt) in TG:
                            bk = next_bank()
                            for k in range(16):
                                S.op('pe', lambda en, bk=bk, k=k, ab=ab, i=i, fc=fc, t0=t0, nt=nt: en.matmul(
                                    banks[bk][:, 0:128 * nt], lhsT=wu[(ab, i)][:, k, 128 * fc:128 * fc + 128],
                                    rhs=hnT[:, t0:t0 + nt, k, :], start=(k == 0), stop=(k == 15)),
                                    reads=[('wu', ab, i), 'hnT'], writes=[('bank', bk)])
                            if t0 == 0:
                                S.op('act', lambda en, bk=bk, ab=ab: en.activation(
                                    out=up_[ab][:, 2:130], in_=banks[bk][:, 0:128], func=AF.Copy, scale=hfl),
                                    reads=[('bank', bk), 'hfl'], writes=[('up', ab)])
                                S.op('act', lambda en, bk=bk, ab=ab: en.copy(out=up_[ab][:, 130:514], in_=banks[bk][:, 128:512]),
                                     reads=[('bank', bk)], writes=[('up', ab)])
                            elif t0 == 4:
                                S.op('act', lambda en, bk=bk, ab=ab: en.copy(out=up_[ab][:, 514:1026], in_=banks[bk]),
                                     reads=[('bank', bk)], writes=[('up', ab)])
                            else:
                                S.op('act', lambda en, bk=bk, ab=ab: en.copy(out=up_[ab][:, 1026:1154], in_=banks[bk][:, 0:128]),
                                     reads=[('bank', bk)], writes=[('up', ab)])
                                S.op('act', lambda en, bk=bk, ab=ab: en.copy(
                                    out=us_[ab][:, :, 2:10], in_=banks[bk][:, 128:256].rearrange("p (s q) -> p s q", q=8)),
                                    reads=[('bank', bk)], writes=[('us', ab)])
                        bkc = next_bank()
                        S.op('dve', lambda en, ab=ab: en.tensor_copy(
                            out=ust.rearrange("p (s j) -> p s j", j=2), in_=us_[ab][:, :, 8:10]),
                            reads=[('us', ab)], writes=['ust'])
                        S.op('pe', lambda en, bkc=bkc, ab=ab: en.transpose(
                            out=banks[bkc][0:32, 0:128], in_=ust, identity=identf),
                            reads=['ust', 'identf'], writes=[('bank', bkc)])
                        S.op('pe', lambda en, bkc=bkc, ab=ab: en.transpose(
                            out=banks[bkc][0:2, 128:256], in_=up_[ab][:, 1152:1154], identity=identf),
                            reads=[('up', ab), 'identf'], writes=[('bank', bkc)])
                        S.op('act', lambda en, bkc=bkc, ab=ab, i=i, fc=fc: en.copy(
                            out=cvo[i][0:32, ab, 128 * fc:128 * fc + 128], in_=banks[bkc][0:32, 0:128]),
                            reads=[('bank', bkc)], writes=[('cvo', 0, 0)])
                        S.op('act', lambda en, bkc=bkc, ab=ab, i=i, fc=fc: en.copy(
                            out=cvo[i][32:34, ab, 128 * fc:128 * fc + 128], in_=banks[bkc][0:2, 128:256]),
                            reads=[('bank', bkc)], writes=[('cvo', 0, 1)])
                        w0, w1_, w2_ = cw[:, 0, kk:kk + 1], cw[:, 1, kk:kk + 1], cw[:, 2, kk:kk + 1]
                        bb = cbv[:, kk:kk + 1]
                        for (u, c, ku, kc_, sl) in ((up_[ab], cp_[ab], ('up', ab), ('cp', ab),
                                                     (slice(2, 1154), slice(1, 1153), slice(0, 1152))),
                                                    (us_[ab], cs2_[ab], ('us', ab), ('cs2', ab),
                                                     (slice(2, 10), slice(1, 9), slice(0, 8)))):
                            if u is up_[ab]:
                                u2, u1, u0 = u[:, sl[0]], u[:, sl[1]], u[:, sl[2]]
                            else:
                                u2, u1, u0 = u[:, :, sl[0]], u[:, :, sl[1]], u[:, :, sl[2]]
                            S.op('act', lambda en, c=c, u2=u2, w2_=w2_, bb=bb: en.activation(
                                out=c, in_=u2, func=AF.Identity, scale=w2_, bias=bb), reads=[ku, 'cw', 'cbv'], writes=[kc_])
                            S.op('dve', lambda en, c=c, u1=u1, w1_=w1_: en.scalar_tensor_tensor(
                                out=c, in0=u1, scalar=w1_, in1=c, op0=ALU.mult, op1=ALU.add), reads=[ku, kc_, 'cw'], writes=[kc_])
                            S.op('dve', lambda en, c=c, u0=u0, w0=w0: en.scalar_tensor_tensor(
                                out=c, in0=u0, scalar=w0, in1=c, op0=ALU.mult, op1=ALU.add), reads=[ku, kc_, 'cw'], writes=[kc_])
                    S.op('act', lambda en: en.activation(out=cp_[0], in_=cp_[0], func=AF.Silu), reads=[('cp', 0)], writes=[('cp', 0)])
                    S.op('act', lambda en: en.activation(out=cs2_[0], in_=cs2_[0], func=AF.Silu), reads=[('cs2', 0)], writes=[('cs2', 0)])
                    S.op('dve', lambda en, i=i, fc=fc: en.tensor_tensor(out=actT[i][:, fc, 0:1152], in0=cp_[0], in1=cp_[1], op=ALU.mult),
                         reads=[('cp', 0), ('cp', 1)], writes=[('actT', i)])
                    S.op('dve', lambda en, i=i, fc=fc: en.tensor_tensor(
                        out=actT[i][:, fc, 1152:1280].rearrange("p (s q) -> p s q", q=8), in0=cs2_[0], in1=cs2_[1], op=ALU.mult),
                        reads=[('cs2', 0), ('cs2', 1)], writes=[('actT', i)])
                for ab in range(2):
                    S.dma('sp', conv_s_out[:, 5632 * ab + SBW * sbi:5632 * ab + SBW * sbi + SBW], cvo[i][0:32, ab, :],
                          reads=[('cvo', 0, 0)], writes=[('conv_s_out', sbi, ab)])
                    S.dma('sp', conv_p_out[:, 5632 * ab + SBW * sbi:5632 * ab + SBW * sbi + SBW], cvo[i][32:34, ab, :],
                          reads=[('cvo', 0, 1)], writes=[('conv_p_out', sbi, ab)])
                def emit_down(i=i):
                    for n_ in range(10):
                        for j in range(4):
                            bk = next_bank()
                            for fc in range(2):
                                S.op('pe', lambda en, bk=bk, fc=fc, i=i, n_=n_, j=j: en.matmul(
                                    banks[bk], lhsT=actT[i][:, fc, 128 * n_:128 * n_ + 128], rhs=wd[i][:, fc, 512 * j:512 * j + 512],
                                    start=(fc == 0), stop=(fc == 1)), reads=[('actT', i), ('wd', i)], writes=[('bank', bk)])
                            hv = hres[:, n_, 512 * j:512 * j + 512]
                            S.op('dve', lambda en, hv=hv, bk=bk: en.tensor_tensor(out=hv, in0=hv, in1=banks[bk], op=ALU.add),
                                 reads=[('bank', bk), ('hres', n_)], writes=[('hres', n_)])
                if pend_down[0] is not None:
                    pend_down[0]()
                pend_down[0] = emit_down
            pend_down[0]()
            for n_ in range(1, 10):
                S.dma('sp', y_out[128 * (n_ - 1):128 * n_, :], hres[:, n_, :], reads=[('hres', n_)], writes=[('y_out', n_)])
            S.barrier()
    if stage >= 8:
        stage_E()

    S.finish()
    with nc.allow_non_contiguous_dma(reason="small constant tables / strided layouts"):
        S.emit()
    return nc, S


def _host_inputs(inputs, c):
    b, h = c // 2, c % 2
    f = lambda a: np.ascontiguousarray(np.asarray(a, dtype=np.float32))
    xpb = np.asarray(inputs['x_prompt'][b], dtype=np.float32)
    if h == 0:
        xp = np.concatenate([np.zeros((1024, D), np.float32), xpb[:1024]], axis=0)
    else:
        xp = xpb
    pos = np.zeros((17, 128), np.float64)
    for t in range(16):
        pos[t] = 128 * t + np.arange(128) - (1024 if h == 0 else 0)
    pos[16] = 2048 + (np.arange(128) % 8)
    inv = (1.0 / (10000.0 ** np.linspace(0.0, 1.0, 128, dtype=np.float32))).astype(np.float32)
    ang = (pos.astype(np.float32)[:, :, None] * inv[None, None, :]).astype(np.float32)
    cs = np.stack([np.cos(ang), np.sin(ang)], axis=2).astype(np.float32)
    gam = 1.0 - 2.0 ** (-5.0 - np.arange(4))
    ii = np.arange(128)
    sct = np.zeros((128, 2, 2, 4), np.float32)
    sct[:, 0, 0, :] = (gam[None, :] ** (-(ii[:, None] + 1.0))) / 16.0
    sct[:, 0, 1, :] = gam[None, :] ** (ii[:, None] + 1.0)
    sct[:, 1, 0, :] = (gam[None, :] ** (-((ii[:, None] % 8) + 1.0))) / 16.0
    sct[:, 1, 1, :] = gam[None, :] ** ((ii[:, None] % 8) + 1.0)
    rmask = (ii[:, None] // 8 == np.arange(16)[None, :]).astype(np.float32)
    cmask = np.broadcast_to((np.arange(16)[:, None] == ii[None, :] // 8)[None], (128, 16, 128)).astype(ml_dtypes.bfloat16)
    mT = np.zeros((128, 2, 128), np.float32)
    mT[:, 0, :] = (ii[:, None] <= ii[None, :])
    mT[:, 1, :] = (ii[:, None] <= ii[None, :]) & (ii[:, None] // 8 == ii[None, :] // 8)
    bf = ml_dtypes.bfloat16

    def split3(a):
        a = np.asarray(a, np.float32)
        hi = a.astype(bf).astype(np.float32)
        mid = (a - hi).astype(bf).astype(np.float32)
        lo = (a - hi - mid).astype(bf).astype(np.float32)
        return hi, mid, lo
    slopes = np.exp2(-8.0 * np.arange(1, 17, dtype=np.float32) / 16).astype(np.float32)
    kpos = np.arange(2048)
    kaug = np.zeros((10, 2048), np.float32)
    kaug[0:3] = 64.0 * (kpos // 64)
    kaug[3:6] = kpos % 64
    kaug[6:9] = 1.0
    kaug[9] = (kpos < 1024) if h == 0 else 0.0
    nblk = np.arange(128)
    cend = 16 * nblk + 31
    kaugc = np.zeros((10, 128), np.float32)
    kaugc[0:3] = 64.0 * (cend // 64)
    kaugc[3:6] = cend % 64
    kaugc[6:9] = 1.0
    kaugc[9] = (nblk >= 127) | ((nblk < 64) if h == 0 else False)
    qpos = 896 + np.arange(1152)
    qaug = np.zeros((10, 16, 1152), np.float32)
    s3 = split3(slopes)
    m3 = split3(-(slopes[:, None] * qpos[None, :].astype(np.float32)))
    for r_ in range(3):
        qaug[r_] = s3[r_][:, None]
        qaug[3 + r_] = s3[r_][:, None]
        qaug[6 + r_] = m3[r_]
    qaug[9] = -30000.0
    E_tab = np.zeros((32, 16, 128), np.float32)
    for kt in range(16):
        for k_ in range(128):
            E_tab[2 * kt + k_ // 64, kt, k_] = 1.0
    cm_tab = np.zeros((128, 2, 128), np.float32)
    cm_tab[:, 0, :] = np.where(ii[:, None] <= ii[None, :], 0.0, -30000.0)
    cm_tab[:, 1, :] = np.where(ii[:, None] > ii[None, :], 0.0, -30000.0)
    cmpm = np.zeros((128, 9, 128), np.float32)
    for qi_ in range(9):
        qp = 896 + 128 * qi_ + ii
        cmpm[:, qi_, :] = np.where(cend[:, None] <= qp[None, :], 0.0, -30000.0)
    cs_ = nblk[:, None] * 16
    js_ = np.arange(32)[None, :] * 64
    cov = (np.clip(np.minimum(cs_ + 32, js_ + 64) - np.maximum(cs_, js_), 0, None) / 32.0).astype(np.float32)
    cov[127] = 0.0
    AB = np.zeros((128, 2, 9, 32), np.float32)
    j0 = 0 if h == 1 else 16
    jj = np.arange(32)
    for qi_ in range(9):
        qp = 896 + 128 * qi_ + ii
        cur = qp // 64
        valid = (jj[None, :] >= j0) & (jj[None, :] <= cur[:, None])
        forced = valid & ((jj[None, :] == j0) | (jj[None, :] == cur[:, None]) | (jj[None, :] == cur[:, None] - 1))
        AB[:, 0, qi_, :] = (valid & ~forced)
        AB[:, 1, qi_, :] = np.where(forced, 1.0e4 + jj[None, :], np.where(valid, 0.0, -1.0e4 - jj[None, :]))
    def aug_k(kp, invalid):
        a = np.zeros((10, kp.shape[0]), np.float32)
        a[0:3] = 64.0 * (kp // 64)
        a[3:6] = kp % 64
        a[6:9] = 1.0
        a[9] = invalid
        return a
    colS = np.arange(2176)
    ktS, pS = colS // 128, colS % 128
    kpS = np.where(ktS < 16, 8 * (128 * (ktS % 2) + pS) + ktS // 2, 2048 + pS)
    kaugS = aug_k(kpS, (kpS >= 2056))
    kpW = np.arange(640)
    kaugW = aug_k(kpW, (kpW >= 520))
    kaugcS = aug_k(cend, (nblk >= 127))

    def aug_q(qp):
        a = np.zeros((10, 16, 128), np.float32)
        mm = split3(-(slopes[:, None] * qp[None, :].astype(np.float32)))
        for r_ in range(3):
            a[r_] = s3[r_][:, None]
            a[3 + r_] = s3[r_][:, None]
            a[6 + r_] = mm[r_]
        a[9] = -30000.0
        return a
    qq = ii % 8
    qaugs = np.stack([aug_q(2048 + qq), aug_q(512 + qq)], axis=0)
    Es = np.zeros((33, 17, 128), np.float32)
    for kt in range(16):
        for k_ in range(128):
            pos_ = 8 * (128 * (kt % 2) + k_) + kt // 2
            Es[pos_ // 64, kt, k_] = 1.0
    Es[32, 16, :] = 1.0
    colq = ii % 8
    cmS = np.zeros((128, 2, 128), np.float32)
    cmS[:, 0, :] = np.where((ii[:, None] <= colq[None, :]) & (ii[:, None] < 8), 0.0, -30000.0)
    cmS[:, 1, :] = np.where(ii[:, None] > colq[None, :], 0.0, -30000.0)
    js33 = np.arange(33)[None, :] * 64
    covS = (np.clip(np.minimum(cs_ + 32, js33 + 64) - np.maximum(cs_, js33), 0, None) / 32.0).astype(np.float32)
    covS[127] = 0.0
    ABs = np.zeros((128, 2, 33), np.float32)
    j33 = np.arange(33)
    forcedS = (j33 == 0) | (j33 == 32) | (j33 == 31)
    ABs[:, 0, :] = (~forcedS)[None, :]
    ABs[:, 1, :] = np.where(forcedS, 1.0e4 + j33, 0.0)[None, :]
    Rsum = np.zeros((32, 16, 128), np.float32)
    for r_ in range(4):
        for q_ in range(8):
            for s_ in range(16):
                Rsum[8 * r_ + q_, s_, 8 * s_ + q_] = 1.0
    m = {
        'cache_kv': f(inputs['cache_kv'][0]).reshape(2560 * 16, 8192), 'page_table': np.ascontiguousarray(np.asarray(inputs['page_table'][16 * c:16 * c + 16], np.int32)),
        'upt_tab': (ii % 16).astype(np.float32).reshape(128, 1), 'Es_tab': Es.astype(bf), 'cmS_tab': cmS.astype(bf), 'covS_tab': covS,
        'ABs_tab': ABs, 'Rsum_tab': Rsum.astype(bf), 'kaugS_tab': kaugS.astype(bf), 'kaugW_tab': kaugW.astype(bf),
        'kaugcS_tab': kaugcS.astype(bf), 'qaugs_tab': qaugs.astype(bf),
        'w_out': f(inputs['w_out'][0]), 'g_ffn': f(inputs['g_ffn'][0]), 'w_up': f(inputs['w_up'][0]),
        'conv_w': f(inputs['conv_w'][0]), 'conv_b': f(inputs['conv_b'][0]), 'w_down': f(inputs['w_down'][0]),
        'state_conv': f(inputs['state_conv'][0, 16 * c:16 * c + 16]), 'identf': np.eye(128, dtype=np.float32),
        'hflag': np.array([float(h)], np.float32),
        'k_norm_cmp': f(inputs['k_norm_cmp'][0]), 'cmp_pe': f(inputs['cmp_pe'][0]), 'cmp_w1': f(inputs['cmp_w1'][0]),
        'cmp_b1': f(inputs['cmp_b1'][0]), 'cmp_w2': f(inputs['cmp_w2'][0]), 'cmp_b2': f(inputs['cmp_b2'][0]),
        'E_tab': E_tab.astype(bf), 'cm_tab': cm_tab.astype(bf), 'cmpm_tab': cmpm.astype(bf), 'cov_tab': cov, 'AB_tab': AB,
        'kaug_tab': kaug.astype(bf), 'kaugc_tab': kaugc.astype(bf), 'qaug_tab': qaug.astype(bf),
        'xp': np.ascontiguousarray(xp),
        'xs': f(inputs['x_sample'][16 * c:16 * c + 16]).reshape(128, D),
        'w_in': f(inputs['w_in'][0]),
        'g_attn': f(inputs['g_attn'][0]),
        'k_norm_slc': f(inputs['k_norm_slc'][0]),
        'k_norm_win': f(inputs['k_norm_win'][0]),
        'ident': np.eye(128, dtype=np.float32).astype(ml_dtypes.bfloat16),
        'cache_win': f(inputs['cache_win'][0, 16 * c:16 * c + 16]).reshape(16, 512, 512),
        'state_ret': f(inputs['state_ret'][0, 16 * c:16 * c + 16]),
        'cs_tab': cs, 'sct_tab': sct, 'rmask_tab': rmask, 'cmask_tab': np.ascontiguousarray(cmask), 'mT_tab': mT,
        'q_norm': f(inputs['q_norm'][0]), 'ret_gn': f(inputs['ret_gn'][0]),
    }
    return m


_CACHE = {}


def run_device(inputs, stage=99, trace=False):
    if stage not in _CACHE:
        _CACHE[stage] = build(stage)
    nc, S = _CACHE[stage]
    in_maps = [_host_inputs(inputs, c) for c in range(NCORES)]
    res = run_bass_kernel_spmd(nc, in_maps, core_ids=list(range(NCORES)), trace=trace)
    return res


def kernel(**inputs):
    res = run_device(inputs)
    R = res.results
    y_p = np.zeros((4, 2048, D), np.float32)
    y_s = np.zeros((128, 8, D), np.float32)
    kv_p = np.zeros((1, 4, 2048, 4, 4, 64), np.float32)
    kv_s = np.zeros((1, 128, 8, 4, 4, 64), np.float32)
    win_p = np.zeros((1, 4, 512, 2, 4, 64), np.float32)
    win_s = np.zeros((1, 128, 512, 2, 4, 64), np.float32)
    ret_p = np.zeros((1, 4, 4, 256, 256), np.float32)
    ret_s = np.zeros((1, 128, 4, 256, 256), np.float32)
    conv_p = np.zeros((1, 4, 2, 11264), np.float32)
    conv_s = np.zeros((1, 128, 2, 11264), np.float32)
    for c in range(NCORES):
        b, h = c // 2, c % 2
        r = R[c]
        kv_p[0, b, 1024 * h:1024 * h + 1024] = r['kv_out'][:1024].reshape(1024, 4, 4, 64)
        kv_s[0, 16 * c:16 * c + 16] = r['kv_out'][1024:].reshape(16, 8, 4, 4, 64)
        win_s[0, 16 * c:16 * c + 16] = r['win_s_out'].reshape(16, 512, 2, 4, 64)
        ret_s[0, 16 * c:16 * c + 16] = r['ret_s_out']
        y_p[b, 1024 * h:1024 * h + 1024] = r['y_out'][:1024]
        y_s[16 * c:16 * c + 16] = r['y_out'][1024:].reshape(16, 8, D)
        conv_s[0, 16 * c:16 * c + 16] = r['conv_s_out'].reshape(16, 2, 11264)
        if h == 1:
            conv_p[0, b] = r['conv_p_out']
            win_p[0, b] = r['win_p_out'].reshape(512, 2, 4, 64)
            ret_p[0, b] = r['ret_p_out']
    return (y_p, y_s, kv_p, kv_s, win_p, win_s, ret_p, ret_s, conv_p, conv_s)
```

```python
import numpy as np
from contextlib import ExitStack
import ml_dtypes
import concourse.bass as bass
import concourse.mybir as mybir
from concourse.bass_utils import run_bass_kernel_spmd

F32 = mybir.dt.float32
BF16 = mybir.dt.bfloat16
I32 = mybir.dt.int32
AF = mybir.ActivationFunctionType
ALU = mybir.AluOpType
AX = mybir.AxisListType

NCORES = 8
D = 2048
EPS = 1e-6
EP = 2000
NDMASEM = 24
NEG = -30000.0
WARM = 0


class Sched:
    ENG = ['pe', 'act', 'dve', 'pool', 'sp']

    def __init__(self, nc):
        self.nc = nc
        self.rec = {e: [] for e in self.ENG}
        self.n = {e: 0 for e in self.ENG}
        self.seen = {e: {f: 0 for f in self.ENG} for e in self.ENG}
        self.dseen = {e: {} for e in self.ENG}
        self.esem = {e: [] for e in self.ENG}
        self.lastw = {}
        self.readers = {}
        self.clock = {}
        self.dq = {}
        self.nwaits = 0

    def _esem(self, e, epoch):
        while len(self.esem[e]) <= epoch:
            self.esem[e].append(self.nc.alloc_semaphore(name=f"s_{e}_{len(self.esem[e])}"))
        return self.esem[e][epoch]

    def _wait(self, eng, ev, force=False):
        if ev[0] == 'eng':
            _, e2, n2 = ev
            if self.seen[eng][e2] >= n2:
                return
            if eng == 'pe' and e2 == 'pe':
                return
            epoch = (n2 - 1) // EP
            sem = self._esem(e2, epoch)
            val = n2 - epoch * EP
            self.rec[eng].append(lambda en, sem=sem, val=val: en.wait_ge(sem, val))
            self.nwaits += 1
            self.seen[eng][e2] = n2
        else:
            _, q, idx, val = ev
            key = (q, idx)
            if self.dseen[eng].get(key, 0) >= val and not force:
                return
            sem = self.dq[q]['sems'][idx]
            self.rec[eng].append(lambda en, sem=sem, val=val: en.wait_ge(sem, val))
            self.nwaits += 1
            self.dseen[eng][key] = val
        ck = self.clock.get(ev)
        if ck is not None:
            for f, v in ck[0].items():
                if self.seen[eng][f] < v:
                    self.seen[eng][f] = v
            for k, v in ck[1].items():
                if self.dseen[eng].get(k, 0) < v:
                    self.dseen[eng][k] = v

    def _deps(self, eng, reads, writes):
        evs = []
        for k in reads:
            if k in self.lastw:
                evs.append(self.lastw[k])
        for k in writes:
            if k in self.lastw:
                evs.append(self.lastw[k])
            evs.extend(self.readers.get(k, ()))
        for ev in evs:
            self._wait(eng, ev)

    def _commit(self, ev, eng, reads, writes):
        self.clock[ev] = (dict(self.seen[eng]), dict(self.dseen[eng]))
        for k in writes:
            self.lastw[k] = ev
            self.readers[k] = []
        for k in reads:
            if k in writes:
                continue
            self.readers.setdefault(k, []).append(ev)
            if len(self.readers[k]) > 64:
                self.readers[k] = self.readers[k][-64:]

    def op(self, eng, fn, reads=(), writes=()):
        self._deps(eng, reads, writes)
        self.n[eng] += 1
        n = self.n[eng]
        epoch = (n - 1) // EP
        sem = self._esem(eng, epoch)
        self.rec[eng].append(lambda en, fn=fn, sem=sem: fn(en).then_inc(sem, 1))
        self.seen[eng][eng] = n if eng == 'pe' else self.seen[eng][eng]
        ev = ('eng', eng, n)
        self._commit(ev, eng, reads, writes)
        return ev

    def dma(self, q, out, in_, reads=(), writes=(), **kw):
        if q not in self.dq:
            self.dq[q] = {'sems': [self.nc.alloc_semaphore(name=f"d_{q}_{i}") for i in range(NDMASEM)],
                          'count': 0}
        st = self.dq[q]
        i = st['count']
        st['count'] += 1
        idx = i % NDMASEM
        val = 16 * (i // NDMASEM + 1)
        if i >= NDMASEM:
            self._wait(q, ('dma', q, idx, val - 16), force=(q == 'pool'))
        self._deps(q, reads, writes)
        sem = st['sems'][idx]
        self.rec[q].append(lambda en, out=out, in_=in_, sem=sem, kw=kw:
                           en.dma_start(out=out, in_=in_, **kw).then_inc(sem, 16))
        ev = ('dma', q, idx, val)
        self._commit(ev, q, reads, writes)
        return ev

    def dma_gather(self, out, in_, idx_ap, nrows, reads=(), writes=()):
        q = 'pool'
        if q not in self.dq:
            self.dq[q] = {'sems': [self.nc.alloc_semaphore(name=f"d_{q}_{i}") for i in range(NDMASEM)], 'count': 0}
        st = self.dq[q]
        i = st['count']
        st['count'] += 1
        idx = i % NDMASEM
        val = 16 * (i // NDMASEM + 1)
        if i >= NDMASEM:
            self._wait(q, ('dma', q, idx, val - 16))
        self._deps(q, reads, writes)
        sem = st['sems'][idx]
        self.rec[q].append(lambda en, out=out, in_=in_, sem=sem, idx_ap=idx_ap: en.indirect_dma_start(
            out=out, out_offset=None, in_=in_, in_offset=bass.IndirectOffsetOnAxis(ap=idx_ap, axis=0),
            bounds_check=nrows - 1, oob_is_err=False).then_inc(sem, 16))
        ev = ('dma', q, idx, val)
        self._commit(ev, q, reads, writes)
        return ev

    def dma_dyn(self, q, out, in_fn, pt_ap, maxv, reads=(), writes=()):
        if q not in self.dq:
            self.dq[q] = {'sems': [self.nc.alloc_semaphore(name=f"d_{q}_{i}") for i in range(NDMASEM)], 'count': 0}
        st = self.dq[q]
        i = st['count']
        st['count'] += 1
        idx = i % NDMASEM
        val = 16 * (i // NDMASEM + 1)
        if i >= NDMASEM:
            self._wait(q, ('dma', q, idx, val - 16))
        self._deps(q, reads, writes)
        sem = st['sems'][idx]

        def f(en, out=out, in_fn=in_fn, sem=sem, pt_ap=pt_ap):
            pg = en.value_load(pt_ap, min_val=0, max_val=maxv)
            en.dma_start(out=out, in_=in_fn(pg)).then_inc(sem, 16)
        self.rec[q].append(f)
        ev = ('dma', q, idx, val)
        self._commit(ev, q, reads, writes)
        return ev

    def barrier(self):
        evs = []
        for e in self.ENG:
            if self.n[e] > 0:
                evs.append(('eng', e, self.n[e]))
        for q, st in self.dq.items():
            for i in range(max(0, st['count'] - NDMASEM), st['count']):
                evs.append(('dma', q, i % NDMASEM, 16 * (i // NDMASEM + 1)))
        for e in self.ENG:
            for ev in evs:
                if ev[0] == 'eng' and ev[1] == e and e == 'pe':
                    continue
                self._wait(e, ev)

    def finish(self):
        for q, st in self.dq.items():
            for i in range(max(0, st['count'] - NDMASEM), st['count']):
                self._wait('sp', ('dma', q, i % NDMASEM, 16 * (i // NDMASEM + 1)))
        for e in self.ENG:
            if e != 'sp' and self.n[e] > 0:
                self._wait('sp', ('eng', e, self.n[e]))

    def emit(self):
        nc = self.nc
        rec = self.rec
        with nc.Block() as block:
            @block.tensor
            def _(en):
                for f in rec['pe']:
                    f(en)

            @block.scalar
            def _(en):
                for f in rec['act']:
                    f(en)

            @block.vector
            def _(en):
                for f in rec['dve']:
                    f(en)

            @block.gpsimd
            def _(en):
                for f in rec['pool']:
                    f(en)

            @block.sync
            def _(en):
                for f in rec['sp']:
                    f(en)


def build(stage=99):
    nc = bass.Bass("TRN2", target_bir_lowering=False)
    S = Sched(nc)

    def din(name, shape, dt=F32):
        return nc.dram_tensor(name, list(shape), dt, kind="ExternalInput").ap()

    def dout(name, shape, dt=F32):
        return nc.dram_tensor(name, list(shape), dt, kind="ExternalOutput").ap()

    def sb(name, shape, dt=F32):
        return nc.alloc_sbuf_tensor(name, list(shape), dt).ap()

    xp = din("xp", [2048, D])
    xs = din("xs", [128, D])
    w_in = din("w_in", [D, 6704])
    g_attn = din("g_attn", [D])
    k_norm_slc = din("k_norm_slc", [64])
    k_norm_win = din("k_norm_win", [64])
    ident_d = din("ident", [128, 128], BF16)

    cache_win = din("cache_win", [16, 512, 512])
    state_ret_d = din("state_ret", [16, 4, 256, 256])
    cs_d = din("cs_tab", [17, 128, 2, 128])
    sct_d = din("sct_tab", [128, 2, 2, 4])
    rmask_d = din("rmask_tab", [128, 16])
    cmask_d = din("cmask_tab", [128, 16, 128], BF16)
    mT_d = din("mT_tab", [128, 2, 128])
    q_norm = din("q_norm", [64])
    ret_gn = din("ret_gn", [4, 256])
    k_norm_cmp = din("k_norm_cmp", [64])
    cache_kv = din("cache_kv", [2560 * 16, 8192])
    page_table_d = din("page_table", [16, 16], I32)
    upt_d = din("upt_tab", [128, 1])
    Es_d = din("Es_tab", [33, 17, 128], BF16)
    cmS_d = din("cmS_tab", [128, 2, 128], BF16)
    covS_d = din("covS_tab", [128, 33])
    ABs_d = din("ABs_tab", [128, 2, 33])
    Rsum_d = din("Rsum_tab", [32, 16, 128], BF16)
    kaugS_d = din("kaugS_tab", [10, 2176], BF16)
    kaugW_d = din("kaugW_tab", [10, 640], BF16)
    kaugcS_d = din("kaugcS_tab", [10, 128], BF16)
    qaugs_d = din("qaugs_tab", [2, 10, 16, 128], BF16)
    w_out = din("w_out", [D, D])
    g_ffn = din("g_ffn", [D])
    w_up = din("w_up", [D, 11264])
    conv_w = din("conv_w", [3, 11264])
    conv_b = din("conv_b", [11264])
    w_down = din("w_down", [5632, D])
    state_conv_d = din("state_conv", [16, 2, 11264])
    identf_d = din("identf", [128, 128])
    hflag_d = din("hflag", [1])
    y_out = dout("y_out", [1152, D])
    conv_p_out = dout("conv_p_out", [2, 11264])
    conv_s_out = dout("conv_s_out", [32, 11264])
    cmp_pe = din("cmp_pe", [2, 32, 64])
    cmp_w1 = din("cmp_w1", [2, 32, 64, 256])
    cmp_b1 = din("cmp_b1", [2, 256])
    cmp_w2 = din("cmp_w2", [2, 256, 64])
    cmp_b2 = din("cmp_b2", [2, 64])
    E_d = din("E_tab", [32, 16, 128], BF16)
    cm_d = din("cm_tab", [128, 2, 128], BF16)
    cmpm_d = din("cmpm_tab", [128, 9, 128], BF16)
    cov_d = din("cov_tab", [128, 32])
    AB_d = din("AB_tab", [128, 2, 9, 32])
    kaug_d = din("kaug_tab", [10, 2048], BF16)
    kaugc_d = din("kaugc_tab", [10, 128], BF16)
    qaug_d = din("qaug_tab", [10, 16, 1152], BF16)

    kv_out = dout("kv_out", [1152, 1024])
    win_p_out = dout("win_p_out", [512, 512])
    win_s_out = dout("win_s_out", [16, 512, 512])
    ret_p_out = dout("ret_p_out", [4, 256, 256])
    ret_s_out = dout("ret_s_out", [16, 4, 256, 256])

    ident = sb("ident_sb", [128, 128], BF16)
    S.dma('sp', ident, ident_d, writes=['ident'])
    gT = sb("gT", [128, 16])
    S.dma('sp', gT, g_attn.rearrange("(k p) -> p k", p=128), writes=['gT'])
    kn_slc = sb("kn_slc", [128, 64])
    kn_win = sb("kn_win", [128, 64])
    S.dma('sp', kn_slc, k_norm_slc.partition_broadcast(128), writes=['kn_slc'])
    S.dma('sp', kn_win, k_norm_win.partition_broadcast(128), writes=['kn_win'])

    epsb = sb("epsb", [128, 1])
    S.op('pool', lambda en: en.memset(epsb, EPS), writes=['epsb'])
    hist_all = sb("hist_all", [128, 2, 1024])
    dmy = sb("dmy", [128, 512], BF16)
    S.op('pool', lambda en: en.memset(dmy, 1.0), writes=['dmy'])

    def warm(bank_i, n=1):
        for _ in range(n * WARM):
            S.op('pe', lambda en: en.matmul(banks[bank_i], lhsT=ident, rhs=dmy, start=True, stop=True),
                 reads=['ident', 'dmy'], writes=[('bank', bank_i)])
    idx_i_perm = sb("idx_i_perm", [128, 32], I32)
    banks = [nc.alloc_psum_tensor(f"bank{i}", [128, 512], F32).ap() for i in range(8)]

    def dscr(name, shape, dt=BF16):
        return nc.dram_tensor(name, list(shape), dt, kind="Internal").ap()

    kvs_s = dscr("kvs_s", [17, 128, 1536])
    kd_s = dscr("kd_s", [17, 128, 1024])
    rv_s = dscr("rv_s", [17, 128, 1024])
    qx_s = dscr("qx_s", [17, 128, 1024])
    rg_s = dscr("rg_s", [17, 128, 1024])
    nq_s = dscr("nq_s", [17, 128, 1024])
    gate_s = dscr("gate_s", [17, 128, 48], F32)
    mix_s = dscr("mix_s", [17, 128, 2048])
    osc_s = [dscr(f"osc_s{i}", [16, 8, 4, 4, 65], F32) for i in range(3)]
    QT = list(range(7, 17))
    ALLT = list(range(17))
    w_in_v = w_in.rearrange("(k p) n -> p k n", p=128)
    bank_rr = {'n': 0}

    def next_bank(lo=2, n=6):
        b = lo + bank_rr['n'] % n
        bank_rr['n'] += 1
        return b

    GAM = [1.0 - 2.0 ** (-5 - hh) for hh in range(4)]

    with ExitStack() as esA:
        def sA(name, shape, dt=F32):
            return esA.enter_context(nc.sbuf_tensor(name, list(shape), dt)).ap()
        xnT = sA("xnT", [128, 17, 16, 128], BF16)
        wbuf = [sA("wbuf0", [128, 16, 512], BF16), sA("wbuf1", [128, 16, 512], BF16)]
        wstate = {'n': 0}

        def load_w(src_v, c0, ncols=512):
            i = wstate['n'] % 2
            wstate['n'] += 1
            S.dma('pool', wbuf[i][:, :, 0:ncols], src_v[:, :, c0:c0 + ncols], writes=[('wbuf', i)])
            return i

        def x_tile_ap(t):
            return xp[128 * t:128 * (t + 1), :] if t < 16 else xs[:, :]

        with ExitStack() as es1:
            def s1(name, shape, dt=F32):
                return es1.enter_context(nc.sbuf_tensor(name, list(shape), dt)).ap()
            xt = [s1("xt0", [128, D]), s1("xt1", [128, D])]
            xb = [s1("xb0", [128, D], BF16), s1("xb1", [128, D], BF16)]
            ss = [s1("ss0", [128, 1]), s1("ss1", [128, 1])]
            rstd = [s1("rstd0", [128, 1]), s1("rstd1", [128, 1])]
            for t in range(17):
                i = t % 2
                S.dma('sp', xt[i], x_tile_ap(t), writes=[('xt', i)])
                S.op('act', lambda en, i=i: en.activation(out=xb[i], in_=xt[i], func=AF.Square, accum_out=ss[i]),
                     reads=[('xt', i)], writes=[('xb', i), ('ss', i)])
                S.op('act', lambda en, i=i: en.activation(out=ss[i], in_=ss[i], func=AF.Sqrt, scale=1.0 / D, bias=epsb),
                     reads=[('ss', i), 'epsb'], writes=[('ss', i)])
                S.op('dve', lambda en, i=i: en.reciprocal(out=rstd[i], in_=ss[i]),
                     reads=[('ss', i)], writes=[('rstd', i)])
                S.op('act', lambda en, i=i: en.activation(out=xb[i], in_=xt[i], func=AF.Copy, scale=rstd[i]),
                     reads=[('xt', i), ('rstd', i)], writes=[('xb', i)])
                for half in range(2):
                    pb = banks[half].bitcast(BF16)
                    for kk in range(8):
                        k = half * 8 + kk
                        S.op('pe', lambda en, pb=pb, kk=kk, k=k, i=i: en.transpose(
                            out=pb[:, 128 * kk:128 * (kk + 1)], in_=xb[i][:, 128 * k:128 * (k + 1)], identity=ident),
                            reads=[('xb', i), 'ident'], writes=[('bank', half)])
                    S.op('dve', lambda en, pb=pb, half=half, t=t: en.tensor_tensor(
                        out=xnT[:, t, 8 * half:8 * half + 8, :],
                        in0=pb.rearrange("p (k n) -> p k n", k=8),
                        in1=gT[:, 8 * half:8 * half + 8].rearrange("p (k o) -> p k o", o=1).broadcast_to([128, 8, 128]),
                        op=ALU.mult),
                        reads=[('bank', half), 'gT'], writes=[('xnT', t)])
            S.barrier()

        def project_tm(t, wi, bk, ncols=512):
            for k in range(16):
                S.op('pe', lambda en, bk=bk, t=t, k=k, wi=wi: en.matmul(
                    banks[bk][:, 0:ncols], lhsT=xnT[:, t, k, :], rhs=wbuf[wi][:, k, 0:ncols],
                    start=(k == 0), stop=(k == 15)),
                    reads=[('xnT', t), ('wbuf', wi)], writes=[('bank', bk)])

        with ExitStack() as es2:
            def s2(name, shape, dt=F32):
                return es2.enter_context(nc.sbuf_tensor(name, list(shape), dt)).ap()
            rows = [s2("rows0", [128, 512]), s2("rows1", [128, 512])]
            stg = [s2("stg0", [128, 512], BF16), s2("stg1", [128, 512], BF16), s2("stg2", [128, 512], BF16)]
            gst = [s2("gst0", [128, 48]), s2("gst1", [128, 48])]
            sq, ssk, krot, ktmp = s2("sq", [128, 512]), s2("ssk", [128, 8]), s2("krot", [128, 512]), s2("ktmp", [128, 256])
            cs_all, sct, qn_t = s2("cs_all", [128, 2, 2, 128]), s2("sct", [128, 2, 2, 4]), s2("qn_t", [128, 64])
            S.dma('sp', sct, sct_d, writes=['sct'])
            S.dma('sp', qn_t, q_norm.partition_broadcast(128), writes=['qn_t'])
            S.op('dve', lambda en: en.tensor_scalar(out=qn_t, in0=qn_t, scalar1=0.125, scalar2=None, op0=ALU.mult),
                 reads=['qn_t'], writes=['qn_t'])
            cntr = {'r': 0, 's': 0, 'g': 0}

            def rms_heads(src, nh, kn, knk, key):
                v3 = src.rearrange("p (g d) -> p g d", d=64)
                sq3 = sq[:, 0:nh * 64].rearrange("p (g d) -> p g d", d=64)
                S.op('dve', lambda en: en.tensor_tensor(out=sq3, in0=v3, in1=v3, op=ALU.mult),
                     reads=[key], writes=['sq'])
                S.op('dve', lambda en: en.tensor_reduce(out=ssk[:, 0:nh], in_=sq3, axis=AX.X, op=ALU.add),
                     reads=['sq'], writes=['ssk'])
                S.op('act', lambda en: en.activation(out=ssk[:, 0:nh], in_=ssk[:, 0:nh], func=AF.Sqrt,
                                                     scale=1.0 / 64, bias=epsb),
                     reads=['ssk', 'epsb'], writes=['ssk'])
                S.op('dve', lambda en: en.reciprocal(out=ssk[:, 0:nh], in_=ssk[:, 0:nh]), reads=['ssk'], writes=['ssk'])
                S.op('dve', lambda en: en.tensor_tensor(
                    out=v3, in0=v3, in1=ssk[:, 0:nh].rearrange("p (g o) -> p g o", o=1).broadcast_to([128, nh, 64]),
                    op=ALU.mult), reads=['ssk', key], writes=[key])
                S.op('dve', lambda en: en.tensor_tensor(
                    out=v3, in0=v3, in1=kn.rearrange("p (o d) -> p o d", o=1).broadcast_to([128, nh, 64]),
                    op=ALU.mult), reads=[knk, key], writes=[key])

            def to_scratch_bf16(src, dst, keys_r, key_w, eng='act'):
                si = cntr['s'] % 3
                cntr['s'] += 1
                if eng == 'act':
                    S.op('act', lambda en, si=si: en.copy(out=stg[si], in_=src), reads=keys_r, writes=[('stg', si)])
                else:
                    S.op('dve', lambda en, si=si: en.tensor_copy(out=stg[si], in_=src), reads=keys_r, writes=[('stg', si)])
                S.dma('sp', dst, stg[si], reads=[('stg', si)], writes=[key_w])

            for j in range(3):
                wi = load_w(w_in_v, 5120 + 512 * j)
                for t in ALLT:
                    i = cntr['r'] % 2
                    cntr['r'] += 1
                    bk = next_bank()
                    project_tm(t, wi, bk)
                    S.op('act', lambda en, bk=bk, i=i: en.copy(out=rows[i], in_=banks[bk]),
                         reads=[('bank', bk)], writes=[('rows', i)])
                    if j >= 1:
                        kn, knk = (kn_slc, 'kn_slc') if j == 1 else (kn_win, 'kn_win')
                        rms_heads(rows[i][:, 0:256], 4, kn, knk, ('rows', i))
                    to_scratch_bf16(rows[i], kvs_s[t, :, 512 * j:512 * (j + 1)], [('rows', i)], ('kvs_s', t, j), eng='dve')
                    if t >= 8:
                        if j < 2:
                            r0 = 128 * (t - 8)
                            S.dma('sp', kv_out[r0:r0 + 128, 512 * j:512 * (j + 1)], rows[i],
                                  reads=[('rows', i)], writes=[('kv_out', t, j)])
                        elif 12 <= t < 16:
                            r0 = 128 * (t - 12)
                            S.dma('sp', win_p_out[r0:r0 + 128, :], rows[i], reads=[('rows', i)], writes=[('win_p', t)])
                        elif t == 16:
                            for sq_i in range(16):
                                S.dma('sp', win_s_out[sq_i, 504:512, :], rows[i][8 * sq_i:8 * sq_i + 8, :],
                                      reads=[('rows', i)], writes=[('win_s', 1, sq_i)])
            for sq_i in range(16):
                S.dma('sp', win_s_out[sq_i, 0:504, :], cache_win[sq_i, 8:512, :], writes=[('win_s', 0, sq_i)])

            def rot_group(c0, tiles, which, dst_s, dkey):
                for j in range(2):
                    wi = load_w(w_in_v, c0 + 512 * j)
                    for t in tiles:
                        bk = next_bank()
                        project_tm(t, wi, bk)
                        x = banks[bk].rearrange("p (h c f) -> p h c f", h=2, c=2)
                        x1, x2 = x[:, :, 0, :], x[:, :, 1, :]
                        ci = cntr['g'] % 2
                        cntr['g'] += 1
                        csk = ('cs_all', ci)
                        S.dma('sp', cs_all[:, ci], cs_d[t], writes=[csk])
                        cosb = cs_all[:, ci, 0:1, :].broadcast_to([128, 2, 128])
                        sinb = cs_all[:, ci, 1:2, :].broadcast_to([128, 2, 128])
                        r = krot.rearrange("p (h c f) -> p h c f", h=2, c=2)
                        r1, r2 = r[:, :, 0, :], r[:, :, 1, :]
                        tm = ktmp.rearrange("p (h f) -> p h f", h=2)
                        bkk = ('bank', bk)
                        S.op('dve', lambda en, x1=x1, cosb=cosb, r1=r1: en.tensor_tensor(out=r1, in0=x1, in1=cosb, op=ALU.mult),
                             reads=[bkk, csk], writes=['krot'])
                        S.op('dve', lambda en, x2=x2, sinb=sinb, tm=tm: en.tensor_tensor(out=tm, in0=x2, in1=sinb, op=ALU.mult),
                             reads=[bkk, csk], writes=['ktmp'])
                        S.op('dve', lambda en, r1=r1, tm=tm: en.tensor_tensor(out=r1, in0=r1, in1=tm, op=ALU.subtract),
                             reads=['krot', 'ktmp'], writes=['krot'])
                        S.op('dve', lambda en, x2=x2, cosb=cosb, r2=r2: en.tensor_tensor(out=r2, in0=x2, in1=cosb, op=ALU.mult),
                             reads=[bkk, csk], writes=['krot'])
                        S.op('dve', lambda en, x1=x1, sinb=sinb, tm=tm: en.tensor_tensor(out=tm, in0=x1, in1=sinb, op=ALU.mult),
                             reads=[bkk, csk, 'krot'], writes=['ktmp'])
                        S.op('dve', lambda en, r2=r2, tm=tm: en.tensor_tensor(out=r2, in0=r2, in1=tm, op=ALU.add),
                             reads=['krot', 'ktmp'], writes=['krot'])
                        ti = 0 if t < 16 else 1
                        si = cntr['s'] % 3
                        cntr['s'] += 1
                        S.op('dve', lambda en, j=j, ti=ti, si=si: en.tensor_tensor(
                            out=stg[si].rearrange("p (h f) -> p h f", h=2),
                            in0=krot.rearrange("p (h f) -> p h f", h=2),
                            in1=sct[:, ti, which, 2 * j:2 * j + 2].rearrange("p (h o) -> p h o", o=1).broadcast_to([128, 2, 256]),
                            op=ALU.mult), reads=['krot', 'sct'], writes=[('stg', si)])
                        S.dma('sp', dst_s[t, :, 512 * j:512 * (j + 1)], stg[si], reads=[('stg', si)], writes=[(dkey, t, j)])

            rot_group(1024, ALLT, 0, kd_s, 'kd_s')
            rot_group(0, QT, 1, qx_s, 'qx_s')
            for j in range(2):
                wi = load_w(w_in_v, 2048 + 512 * j)
                for t in ALLT:
                    bk = next_bank()
                    project_tm(t, wi, bk)
                    to_scratch_bf16(banks[bk], rv_s[t, :, 512 * j:512 * (j + 1)], [('bank', bk)], ('rv_s', t, j))
            for j in range(2):
                wi = load_w(w_in_v, 3072 + 512 * j)
                for t in QT:
                    bk = next_bank()
                    project_tm(t, wi, bk)
                    si = cntr['s'] % 3
                    cntr['s'] += 1
                    S.op('act', lambda en, bk=bk, si=si: en.activation(out=stg[si], in_=banks[bk], func=AF.Silu),
                         reads=[('bank', bk)], writes=[('stg', si)])
                    S.dma('sp', rg_s[t, :, 512 * j:512 * (j + 1)], stg[si], reads=[('stg', si)], writes=[('rg_s', t, j)])
            for j in range(2):
                wi = load_w(w_in_v, 4096 + 512 * j)
                for t in QT:
                    i = cntr['r'] % 2
                    cntr['r'] += 1
                    bk = next_bank()
                    project_tm(t, wi, bk)
                    S.op('act', lambda en, bk=bk, i=i: en.copy(out=rows[i], in_=banks[bk]),
                         reads=[('bank', bk)], writes=[('rows', i)])
                    rms_heads(rows[i], 8, qn_t, 'qn_t', ('rows', i))
                    to_scratch_bf16(rows[i], nq_s[t, :, 512 * j:512 * (j + 1)], [('rows', i)], ('nq_s', t, j), eng='dve')
            wi = load_w(w_in_v, 6656, ncols=48)
            for t in QT:
                bk = next_bank()
                project_tm(t, wi, bk, ncols=48)
                gi = t % 2
                S.op('act', lambda en, bk=bk, gi=gi: en.activation(out=gst[gi], in_=banks[bk][:, 0:48], func=AF.Sigmoid),
                     reads=[('bank', bk)], writes=[('gst', gi)])
                S.dma('sp', gate_s[t], gst[gi], reads=[('gst', gi)], writes=[('gate_s', t)])
            S.barrier()

    with ExitStack() as esB:
        def sB(name, shape, dt=F32):
            return esB.enter_context(nc.sbuf_tensor(name, list(shape), dt)).ap()
        S32, Sbf = sB("S32", [128, 8, 256]), sB("Sbf", [128, 8, 256], BF16)
        kdb = [sB("kd0", [128, 1024], BF16), sB("kd1", [128, 1024], BF16)]
        rvb = [sB("rv0", [128, 1024], BF16), sB("rv1", [128, 1024], BF16)]
        qxb = [sB("qx0", [128, 1024], BF16), sB("qx1", [128, 1024], BF16)]
        rgb = [sB("rg0", [128, 1024], BF16), sB("rg1", [128, 1024], BF16)]
        kdT, qxT = sB("kdT", [128, 8, 128], BF16), sB("qxT", [128, 8, 128], BF16)
        AT, osb, osq, oss = sB("AT", [128, 4, 128], BF16), sB("osb", [128, 4, 256]), sB("osq", [128, 4, 256]), sB("oss", [128, 4])
        retb = [sB("retb0", [128, 1024], BF16), sB("retb1", [128, 1024], BF16)]
        gn_t, mT, rmask, cmask = sB("gn_t", [128, 1024]), sB("mT", [128, 2, 128]), sB("rmask", [128, 16]), sB("cmask", [128, 16, 128], BF16)
        kdm = [sB("kdm0", [128, 1024], BF16), sB("kdm1", [128, 1024], BF16)]
        qxm = [sB("qxm0", [128, 8, 128], BF16), sB("qxm1", [128, 8, 128], BF16)]
        S0 = [sB("S0a", [128, 8, 256]), sB("S0b", [128, 8, 256])]
        S0bf = [sB("S0bfa", [128, 8, 256], BF16), sB("S0bfb", [128, 8, 256], BF16)]
        S.dma('sp', gn_t, ret_gn.rearrange("h e -> (h e)").partition_broadcast(128), writes=['gn_t'])
        S.dma('sp', mT, mT_d, writes=['mT'])
        S.dma('sp', rmask, rmask_d, writes=['rmask'])
        S.dma('sp', cmask, cmask_d, writes=['cmask'])
        S.op('pool', lambda en: en.memset(S32, 0.0), writes=['S32'])
        S.op('pool', lambda en: en.memset(Sbf, 0.0), writes=['Sbf'])

        def transpose8(src, dst, bank_i, rkey, wkey):
            pb = banks[bank_i].bitcast(BF16)
            for k in range(8):
                S.op('pe', lambda en, k=k: en.transpose(out=pb[:, 128 * k:128 * (k + 1)],
                                                        in_=src[:, 128 * k:128 * (k + 1)], identity=ident),
                     reads=[rkey, 'ident'], writes=[('bank', bank_i)])
            S.op('act', lambda en: en.copy(out=dst, in_=pb.rearrange("p (k n) -> p k n", k=8)),
                 reads=[('bank', bank_i)], writes=[wkey])

        def intra(i, mi):
            for hh in range(4):
                for c in range(2):
                    S.op('pe', lambda en, hh=hh, c=c: en.matmul(
                        banks[4][:, 128 * hh:128 * hh + 128], lhsT=kdT[:, 2 * hh + c, :], rhs=qxT[:, 2 * hh + c, :],
                        start=(c == 0), stop=(c == 1)), reads=['kdT', 'qxT'], writes=[('bank', 4)])
            S.op('dve', lambda en: en.tensor_tensor(
                out=AT, in0=banks[4].rearrange("p (h n) -> p h n", h=4),
                in1=mT[:, mi:mi + 1, :].broadcast_to([128, 4, 128]), op=ALU.mult),
                reads=[('bank', 4), 'mT'], writes=['AT'])

        def gate_out(t, i):
            S.op('act', lambda en: en.copy(out=osb[:, 0:2, :], in_=banks[2].rearrange("p (h e) -> p h e", h=2)),
                 reads=[('bank', 2)], writes=['osb'])
            S.op('act', lambda en: en.copy(out=osb[:, 2:4, :], in_=banks[3].rearrange("p (h e) -> p h e", h=2)),
                 reads=[('bank', 3)], writes=['osb'])
            S.op('dve', lambda en: en.tensor_tensor(out=osq, in0=osb, in1=osb, op=ALU.mult), reads=['osb'], writes=['osq'])
            S.op('dve', lambda en: en.tensor_reduce(out=oss, in_=osq, axis=AX.X, op=ALU.add), reads=['osq'], writes=['oss'])
            S.op('act', lambda en: en.activation(out=oss, in_=oss, func=AF.Sqrt, scale=1.0 / 256, bias=epsb),
                 reads=['oss', 'epsb'], writes=['oss'])
            S.op('dve', lambda en: en.reciprocal(out=oss, in_=oss), reads=['oss'], writes=['oss'])
            S.op('dve', lambda en: en.tensor_tensor(
                out=osb, in0=osb, in1=oss.rearrange("p (h o) -> p h o", o=1).broadcast_to([128, 4, 256]), op=ALU.mult),
                reads=['oss', 'osb'], writes=['osb'])
            S.op('dve', lambda en: en.tensor_tensor(out=osb, in0=osb, in1=gn_t.rearrange("p (h e) -> p h e", h=4), op=ALU.mult),
                 reads=['gn_t', 'osb'], writes=['osb'])
            ri = t % 2
            S.op('dve', lambda en, ri=ri, i=i: en.tensor_tensor(
                out=retb[ri].rearrange("p (h e) -> p h e", h=4), in0=osb,
                in1=rgb[i].rearrange("p (h e) -> p h e", h=4), op=ALU.mult),
                reads=['osb', ('rgb', i)], writes=[('retb', ri)])
            S.dma('sp', mix_s[t, :, 0:1024], retb[ri], reads=[('retb', ri)], writes=[('mix_s', t, 0)])

        def state_psum(lhs, lkeys, rvt, rkey):
            for hh in range(4):
                bk = (5, 6, 7, 4)[hh]
                for c in range(2):
                    S.op('pe', lambda en, hh=hh, c=c, bk=bk: en.matmul(
                        banks[bk][:, 256 * c:256 * c + 256], lhsT=lhs[:, 256 * hh + 128 * c:256 * hh + 128 * c + 128],
                        rhs=rvt[:, 256 * hh:256 * hh + 256], start=True, stop=True),
                        reads=lkeys + [rkey], writes=[('bank', bk)])

        for t in range(16):
            i = t % 2
            S.dma('sp', kdb[i], kd_s[t], reads=[('kd_s', t, 0), ('kd_s', t, 1)], writes=[('kdb', i)])
            S.dma('sp', rvb[i], rv_s[t], reads=[('rv_s', t, 0), ('rv_s', t, 1)], writes=[('rvb', i)])
            if t >= 7:
                S.dma('sp', qxb[i], qx_s[t], reads=[('qx_s', t, 0), ('qx_s', t, 1)], writes=[('qxb', i)])
                S.dma('sp', rgb[i], rg_s[t], reads=[('rg_s', t, 0), ('rg_s', t, 1)], writes=[('rgb', i)])
                transpose8(kdb[i], kdT, 0, ('kdb', i), 'kdT')
                transpose8(qxb[i], qxT, 1, ('qxb', i), 'qxT')
                intra(i, 0)
                for hh in range(4):
                    ob = banks[2 + hh // 2][:, 256 * (hh % 2):256 * (hh % 2) + 256]
                    S.op('pe', lambda en, hh=hh, ob=ob, i=i: en.matmul(ob, lhsT=AT[:, hh, :], rhs=rvb[i][:, 256 * hh:256 * hh + 256],
                                                                 start=True, stop=False),
                         reads=['AT', ('rvb', i)], writes=[('bank', 2 + hh // 2)])
                    for c in range(2):
                        S.op('pe', lambda en, hh=hh, c=c, ob=ob: en.matmul(ob, lhsT=qxT[:, 2 * hh + c, :], rhs=Sbf[:, 2 * hh + c, :],
                                                                        start=False, stop=(c == 1)),
                             reads=['qxT', 'Sbf'], writes=[('bank', 2 + hh // 2)])
                gate_out(t, i)
            state_psum(kdb[i], [('kdb', i)], rvb[i], ('rvb', i))
            for hh in range(4):
                bk = (5, 6, 7, 4)[hh]
                sv = S32[:, 2 * hh:2 * hh + 2, :]
                S.op('dve', lambda en, sv=sv, bk=bk: en.tensor_tensor(
                    out=sv, in0=sv, in1=banks[bk].rearrange("p (c e) -> p c e", c=2), op=ALU.add),
                    reads=[('bank', bk), 'S32'], writes=['S32'])
                S.op('act', lambda en, sv=sv, hh=hh: en.activation(out=sv, in_=sv, func=AF.Copy, scale=float(GAM[hh] ** 128)),
                     reads=['S32'], writes=['S32'])
            S.op('act', lambda en: en.copy(out=Sbf, in_=S32), reads=['S32'], writes=['Sbf'])
        S.dma('sp', ret_p_out.rearrange("h (c p) e -> p h c e", p=128),
              S32.rearrange("p (h c) e -> p h c e", c=2), reads=['S32'], writes=['ret_p_out'])

        t = 16
        S.dma('sp', kdb[0], kd_s[t], reads=[('kd_s', t, 0), ('kd_s', t, 1)], writes=[('kdb', 0)])
        S.dma('sp', rvb[0], rv_s[t], reads=[('rv_s', t, 0), ('rv_s', t, 1)], writes=[('rvb', 0)])
        S.dma('sp', qxb[0], qx_s[t], reads=[('qx_s', t, 0), ('qx_s', t, 1)], writes=[('qxb', 0)])
        S.dma('sp', rgb[0], rg_s[t], reads=[('rg_s', t, 0), ('rg_s', t, 1)], writes=[('rgb', 0)])
        transpose8(kdb[0], kdT, 0, ('kdb', 0), 'kdT')
        transpose8(qxb[0], qxT, 1, ('qxb', 0), 'qxT')
        intra(0, 1)
        for hh in range(4):
            ob = banks[2 + hh // 2][:, 256 * (hh % 2):256 * (hh % 2) + 256]
            S.op('pe', lambda en, hh=hh, ob=ob: en.matmul(ob, lhsT=AT[:, hh, :], rhs=rvb[0][:, 256 * hh:256 * hh + 256],
                                                         start=(hh % 2 == 0), stop=False, skip_group_check=True),
                 reads=['AT', ('rvb', 0)], writes=[('bank', 2 + hh // 2)])
        for sq_i in range(16):
            i = sq_i % 2
            S.dma('sp', S0[i].rearrange("p (h c) e -> p h c e", c=2),
                  state_ret_d[sq_i].rearrange("h (c p) e -> p h c e", p=128), writes=[('S0', i)])
            S.op('act', lambda en, i=i: en.copy(out=S0bf[i], in_=S0[i]), reads=[('S0', i)], writes=[('S0bf', i)])
            S.op('dve', lambda en, i=i, sq_i=sq_i: en.tensor_tensor(
                out=qxm[i], in0=qxT, in1=cmask[:, sq_i:sq_i + 1, :].broadcast_to([128, 8, 128]), op=ALU.mult),
                reads=['qxT', 'cmask'], writes=[('qxm', i)])
            for hh in range(4):
                ob = banks[2 + hh // 2][:, 256 * (hh % 2):256 * (hh % 2) + 256]
                for c in range(2):
                    S.op('pe', lambda en, hh=hh, c=c, ob=ob, i=i, sq_i=sq_i: en.matmul(
                        ob, lhsT=qxm[i][:, 2 * hh + c, :], rhs=S0bf[i][:, 2 * hh + c, :],
                        start=False, stop=(sq_i == 15 and c == 1), skip_group_check=True),
                        reads=[('qxm', i), ('S0bf', i)], writes=[('bank', 2 + hh // 2)])
            S.op('dve', lambda en, i=i, sq_i=sq_i: en.tensor_scalar(
                out=kdm[i], in0=kdb[0], scalar1=rmask[:, sq_i:sq_i + 1], scalar2=None, op0=ALU.mult),
                reads=[('kdb', 0), 'rmask'], writes=[('kdm', i)])
            state_psum(kdm[i], [('kdm', i)], rvb[0], ('rvb', 0))
            for hh in range(4):
                bk = (5, 6, 7, 4)[hh]
                sv = S0[i][:, 2 * hh:2 * hh + 2, :]
                S.op('dve', lambda en, sv=sv, bk=bk: en.tensor_tensor(
                    out=sv, in0=sv, in1=banks[bk].rearrange("p (c e) -> p c e", c=2), op=ALU.add),
                    reads=[('bank', bk), ('S0', i)], writes=[('S0', i)])
                S.op('act', lambda en, sv=sv, hh=hh: en.activation(out=sv, in_=sv, func=AF.Copy, scale=float(GAM[hh] ** 8)),
                     reads=[('S0', i)], writes=[('S0', i)])
            S.dma('sp', ret_s_out[sq_i].rearrange("h (c p) e -> p h c e", p=128),
                  S0[i].rearrange("p (h c) e -> p h c e", c=2), reads=[('S0', i)], writes=[('ret_s_out', sq_i)])
        gate_out(16, 0)
        S.barrier()

    def stage_C():
        NEG = -30000.0
        with ExitStack() as esC:
            def sC(name, shape, dt=F32):
                return esC.enter_context(nc.sbuf_tensor(name, list(shape), dt)).ap()
            KTc = {c: sC(f"KT{c}", [128, 4, 2048], BF16) for c in (0, 1, 2, 4)}
            Vt = {c: sC(f"V{c}", [128, 16, 4, 65], BF16) for c in (3, 5)}
            kcT = sC("kcT", [128, 4, 128], BF16)
            Vc = sC("Vc", [128, 4, 97], BF16)
            hidT = sC("hidT", [128, 16, 128], BF16)
            E_all = sC("E_all", [32, 16, 128], BF16)
            cm = sC("cm", [128, 2, 128], BF16)
            cmpm = sC("cmpm", [128, 9, 128], BF16)
            covt = sC("covt", [128, 32], F32)
            ABt = sC("ABt", [128, 2, 9, 32], F32)
            ones64 = sC("ones64", [64, 64], BF16)
            kncmp = sC("kncmp", [64, 1], F32)
            b2k = sC("b2k", [64, 1], F32)
            b2v = sC("b2v", [128, 64], F32)
            b1e = sC("b1e", [128, 4], F32)
            kvt = [sC("kvt0", [128, 1536], BF16), sC("kvt1", [128, 1536], BF16)]
            for c in (0, 1, 2, 4):
                S.op('pool', lambda en, c=c: en.memset(KTc[c], 0.0), writes=[('KT', c)])
            for c in (3, 5):
                S.op('pool', lambda en, c=c: en.memset(Vt[c], 1.0), writes=[('V', c)])
            S.op('pool', lambda en: en.memset(kcT, 0.0), writes=['kcT'])
            S.op('pool', lambda en: en.memset(Vc, 0.0), writes=['Vc'])
            S.op('pool', lambda en: en.memset(ones64, 1.0), writes=['ones64'])
            S.dma('sp', E_all, E_d, writes=['E_all'])
            S.dma('sp', cm, cm_d, writes=['cm'])
            S.dma('sp', cmpm, cmpm_d, writes=['cmpm'])
            S.dma('sp', covt, cov_d, writes=['covt'])
            S.dma('sp', ABt, AB_d, writes=['ABt'])
            S.dma('sp', kncmp, k_norm_cmp.rearrange("(d o) -> d o", o=1), writes=['kncmp'])
            S.dma('sp', b2k, cmp_b2[0].rearrange("(d o) -> d o", o=1), writes=['b2k'])
            S.dma('sp', b2v, cmp_b2[1].partition_broadcast(128), writes=['b2v'])
            for g in range(4):
                S.dma('sp', KTc[2][64:74, g, :], kaug_d, writes=[('KT', 2)])
                S.dma('sp', KTc[4][64:74, g, :], kaug_d, writes=[('KT', 4)])
                S.dma('sp', kcT[64:74, g, :], kaugc_d, writes=['kcT'])
            for t in range(16):
                i = t % 2
                S.dma('sp', kvt[i], kvs_s[t], reads=[('kvs_s', t, 0), ('kvs_s', t, 1), ('kvs_s', t, 2)], writes=[('kvt', i)])
                for ci, c in enumerate((0, 1, 2, 4)):
                    bk = ci % 2
                    pb = banks[bk].bitcast(BF16)
                    for g in range(4):
                        S.op('pe', lambda en, pb=pb, g=g, c=c, i=i: en.transpose(
                            out=pb[0:64, 128 * g:128 * g + 128], in_=kvt[i][:, 256 * c + 64 * g:256 * c + 64 * g + 64],
                            identity=ident), reads=[('kvt', i), 'ident'], writes=[('bank', bk)])
                    S.op('act', lambda en, pb=pb, c=c, t=t: en.copy(
                        out=KTc[c][0:64, :, 128 * t:128 * t + 128], in_=pb[0:64, 0:512].rearrange("p (g n) -> p g n", g=4)),
                        reads=[('bank', bk)], writes=[('KT', c)])
                for c in (3, 5):
                    S.op('dve', lambda en, c=c, t=t, i=i: en.tensor_copy(
                        out=Vt[c][:, t, :, 0:64], in_=kvt[i][:, 256 * c:256 * c + 256].rearrange("p (g d) -> p g d", g=4)),
                        reads=[('kvt', i)], writes=[('V', c)])
            if stage < 4:
                return
            with ExitStack() as esW:
                w1 = esW.enter_context(nc.sbuf_tensor("w1", [64, 2, 32, 256], BF16)).ap()
                peT = esW.enter_context(nc.sbuf_tensor("peT", [64, 2, 32], BF16)).ap()
                w2s = esW.enter_context(nc.sbuf_tensor("w2s", [128, 2, 2, 64], BF16)).ap()
                kc32 = esW.enter_context(nc.sbuf_tensor("kc32", [64, 128], F32)).ap()
                kcsq = esW.enter_context(nc.sbuf_tensor("kcsq", [64, 128], BF16)).ap()
                krs = esW.enter_context(nc.sbuf_tensor("krs", [64, 128], F32)).ap()
                b1t = esW.enter_context(nc.sbuf_tensor("b1t", [128, 4], F32)).ap()
                S.op('pool', lambda en: en.memset(kc32, 0.0), writes=['kc32'])
                for c in range(2):
                    S.dma('pool', w1[:, c], cmp_w1[c].rearrange("l d h -> d l h"), writes=['w1'])
                S.dma('pool', peT, cmp_pe.rearrange("c l d -> d c l"), writes=['peT'])
                S.dma('pool', w2s, cmp_w2.rearrange("c (k p) d -> p c k d", p=128), writes=['w2s'])
                S.dma('sp', b1t, cmp_b1.rearrange("c (k p) -> p (c k)", p=128), writes=['b1t'])
                for c in range(2):
                    for hc in range(2):
                        for l in range(32):
                            S.op('pe', lambda en, c=c, hc=hc, l=l: en.matmul(
                                banks[7][:, 2 * c + hc:2 * c + hc + 1], lhsT=w1[:, c, l, 128 * hc:128 * hc + 128],
                                rhs=peT[:, c, l:l + 1], start=(l == 0), stop=(l == 31)),
                                reads=['w1', 'peT'], writes=[('bank', 7)])
                S.op('dve', lambda en: en.tensor_tensor(out=b1e, in0=banks[7][:, 0:4], in1=b1t, op=ALU.add),
                     reads=[('bank', 7), 'b1t'], writes=['b1e'])
                for c in range(2):
                    XT = KTc[c]
                    for g in range(4):
                        Xv = XT[0:64, g, :].rearrange("p (n l) -> p l n", l=16)
                        for hc in range(2):
                            bk = 2 + (4 * c + g) % 4
                            for hf in range(2):
                                for l in range(16):
                                    first = (hf == 0 and l == 0)
                                    last = (hf == 1 and l == 15)
                                    S.op('pe', lambda en, c=c, hc=hc, hf=hf, l=l, bk=bk, Xv=Xv, first=first, last=last: en.matmul(
                                        banks[bk][:, 128 * hc:128 * hc + 127], lhsT=w1[:, c, 16 * hf + l, 128 * hc:128 * hc + 128],
                                        rhs=Xv[:, l, hf:hf + 127], start=first, stop=last),
                                        reads=['w1', ('KT', c)], writes=[('bank', bk)])
                            S.op('act', lambda en, c=c, g=g, hc=hc, bk=bk: en.activation(
                                out=hidT[:, 8 * c + 2 * g + hc, 0:127], in_=banks[bk][:, 128 * hc:128 * hc + 127], func=AF.Silu,
                                bias=b1e[:, 2 * c + hc:2 * c + hc + 1]), reads=[('bank', bk), 'b1e'], writes=['hidT'])
                for g in range(4):
                    for hc in range(2):
                        S.op('pe', lambda en, g=g, hc=hc: en.matmul(
                            banks[6][0:64, 0:127], lhsT=w2s[:, 0, hc, :], rhs=hidT[:, 2 * g + hc, 0:127],
                            start=(hc == 0), stop=(hc == 1)), reads=['w2s', 'hidT'], writes=[('bank', 6)])
                    S.op('act', lambda en: en.activation(out=kc32[:, 0:127], in_=banks[6][0:64, 0:127], func=AF.Identity, bias=b2k),
                         reads=[('bank', 6), 'b2k'], writes=['kc32'])
                    S.op('dve', lambda en: en.tensor_tensor(out=kcsq, in0=kc32, in1=kc32, op=ALU.mult), reads=['kc32'], writes=['kcsq'])
                    S.op('pe', lambda en: en.matmul(banks[7][0:64, 0:128], lhsT=ones64, rhs=kcsq, start=True, stop=True),
                         reads=['ones64', 'kcsq'], writes=[('bank', 7)])
                    S.op('act', lambda en: en.activation(out=krs, in_=banks[7][0:64, 0:128], func=AF.Sqrt, scale=1.0 / 64, bias=epsb[0:64]),
                         reads=[('bank', 7), 'epsb'], writes=['krs'])
                    S.op('dve', lambda en: en.reciprocal(out=krs, in_=krs), reads=['krs'], writes=['krs'])
                    S.op('dve', lambda en: en.tensor_tensor(out=kc32, in0=kc32, in1=krs, op=ALU.mult), reads=['kc32', 'krs'], writes=['kc32'])
                    S.op('dve', lambda en, g=g: en.tensor_scalar(out=kcT[0:64, g, 0:127], in0=kc32[:, 0:127], scalar1=kncmp, scalar2=None,
                                                                 op0=ALU.mult), reads=['kc32', 'kncmp'], writes=['kcT'])
                    for hc in range(2):
                        S.op('pe', lambda en, g=g, hc=hc: en.matmul(
                            banks[5][0:127, 0:64], lhsT=hidT[:, 8 + 2 * g + hc, 0:127], rhs=w2s[:, 1, hc, :],
                            start=(hc == 0), stop=(hc == 1)), reads=['w2s', 'hidT'], writes=[('bank', 5)])
                    S.op('dve', lambda en, g=g: en.tensor_tensor(out=Vc[0:127, g, 0:64], in0=banks[5][0:127, 0:64], in1=b2v[0:127],
                                                                 op=ALU.add), reads=[('bank', 5), 'b2v'], writes=['Vc'])
                    S.op('pool', lambda en, g=g: en.memset(Vc[:, g, 64:65], 1.0), writes=['Vc'])
                    S.op('dve', lambda en, g=g: en.tensor_copy(out=Vc[:, g, 65:97], in_=covt), reads=['covt'], writes=['Vc'])
                S.barrier()

            if stage < 5:
                return
            qT = sC("qT", [128, 16, 128], BF16)
            nqt = [sC("nqt0", [128, 1024], BF16), sC("nqt1", [128, 1024], BF16)]
            PT = [sC(f"PT{i}", [128, 4, 128], BF16) for i in range(3)]
            ob = {br: sC(f"ob{br}", [128, 16, 97], F32) for br in range(3)}
            gt = sC("gt", [128, 48], F32)
            imp = sC("imp", [128, 16, 32], F32)
            sc = sC("sc", [128, 4, 32], F32)
            sc2 = sC("sc2", [128, 32], F32)
            mx8 = sC("mx8", [128, 8], F32)
            thr = sC("thr", [128, 1], F32)
            selm = sC("selm", [128, 4, 32], F32)
            negb = sC("negb", [128, 4, 32], BF16)
            negT = sC("negT", [32, 4, 128], BF16)
            wgt = sC("wgt", [128, 16], F32)
            nsa = sC("nsa", [128, 16, 64], F32)
            ntmp = sC("ntmp", [128, 16, 64], F32)
            nsab = [sC("nsab0", [128, 1024], BF16), sC("nsab1", [128, 1024], BF16)]
            S.op('pool', lambda en: en.memset(qT, 0.0), writes=['qT'])
            cst = {'s': 0, 'p': 0, 'o': 0}

            def attend(br, g, qi, qt, kts, Kt, kkey, Vap_fn, vkey, ncol):
                obk = 5 + cst['o'] % 2
                cst['o'] += 1
                pending = [None]
                for n_, kt in enumerate(kts):
                    sb_ = 2 + cst['s'] % 3
                    cst['s'] += 1
                    extra = []
                    if br == 0:
                        extra.append((ident, cmpm[:, qi:qi + 1, :].broadcast_to([128, 4, 128]), ['ident', 'cmpm']))
                    else:
                        if br == 1:
                            extra.append((E_all[:, kt, :], negT[:, g:g + 1, :].broadcast_to([32, 4, 128]), ['E_all', 'negT']))
                        if kt == qt:
                            extra.append((ident, cm[:, 0:1, :].broadcast_to([128, 4, 128]), ['ident', 'cm']))
                        if br == 2 and kt == qt - 4:
                            extra.append((ident, cm[:, 1:2, :].broadcast_to([128, 4, 128]), ['ident', 'cm']))
                    kcols = slice(128 * kt, 128 * kt + 128) if br > 0 else slice(0, 128)
                    warm(sb_)
                    S.op('pe', lambda en, sb_=sb_, g=g, kcols=kcols, last=(len(extra) == 0): en.matmul(
                        banks[sb_], lhsT=Kt[0:74, g, kcols], rhs=qT[0:74, 4 * g:4 * g + 4, :], start=True, stop=last),
                        reads=[kkey, 'qT'], writes=[('bank', sb_)])
                    for ei, (l_, r_, ks) in enumerate(extra):
                        S.op('pe', lambda en, sb_=sb_, l_=l_, r_=r_, last=(ei == len(extra) - 1): en.matmul(
                            banks[sb_], lhsT=l_, rhs=r_, start=False, stop=last), reads=ks, writes=[('bank', sb_)])
                    pi = cst['p'] % 3
                    cst['p'] += 1
                    S.op('act', lambda en, pi=pi, sb_=sb_: en.activation(
                        out=PT[pi], in_=banks[sb_].rearrange("p (r n) -> p r n", r=4), func=AF.Exp),
                        reads=[('bank', sb_)], writes=[('PT', pi)])
                    def emit_pv(pi=pi, kt=kt, n_=n_):
                        for r in range(4):
                            S.op('pe', lambda en, r=r, pi=pi, obk=obk, kt=kt, first=(n_ == 0), last=(n_ == len(kts) - 1): en.matmul(
                                banks[obk][:, ncol * r:ncol * r + ncol], lhsT=PT[pi][:, r, :], rhs=Vap_fn(kt),
                                start=(first and r == 0), stop=last, skip_group_check=True),
                                reads=[('PT', pi), vkey], writes=[('bank', obk)])
                    if pending[0] is not None:
                        pending[0]()
                    pending[0] = emit_pv
                pending[0]()
                pending[0] = None
                S.op('act', lambda en, obk=obk, g=g, br=br: en.copy(
                    out=ob[br][:, 4 * g:4 * g + 4, 0:ncol], in_=banks[obk][:, 0:4 * ncol].rearrange("p (r c) -> p r c", r=4)),
                    reads=[('bank', obk)], writes=[('ob', br)])

            import os
            for qt in range(7, int(os.environ.get('QTL', 16))):
                qi = qt - 7
                i = qt % 2
                S.dma('sp', nqt[i], nq_s[qt], reads=[('nq_s', qt, 0), ('nq_s', qt, 1)], writes=[('nqt', i)])
                S.dma('sp', gt, gate_s[qt], reads=[('gate_s', qt)], writes=['gt'])
                S.dma('sp', qT[64:74, :, :], qaug_d[:, :, 128 * qi:128 * qi + 128], writes=['qT'])
                for half in range(2):
                    pb = banks[half].bitcast(BF16)
                    for hh in range(8):
                        hd = 8 * half + hh
                        S.op('pe', lambda en, pb=pb, hh=hh, hd=hd, i=i: en.transpose(
                            out=pb[0:64, 128 * hh:128 * hh + 128], in_=nqt[i][:, 64 * hd:64 * hd + 64], identity=ident),
                            reads=[('nqt', i), 'ident'], writes=[('bank', half)])
                    S.op('act', lambda en, pb=pb, half=half: en.copy(
                        out=qT[0:64, 8 * half:8 * half + 8, :], in_=pb[0:64, :].rearrange("p (h n) -> p h n", h=8)),
                        reads=[('bank', half)], writes=['qT'])
                for g in range(4):
                    attend(0, g, qi, qt, [0], kcT, 'kcT', lambda kt, g=g: Vc[:, g, :], 'Vc', 97)
                S.op('dve', lambda en: en.tensor_scalar(out=wgt, in0=ob[0][:, :, 64], scalar1=1e-30, scalar2=None, op0=ALU.add),
                     reads=[('ob', 0)], writes=['wgt'])
                S.op('dve', lambda en: en.reciprocal(out=wgt, in_=wgt), reads=['wgt'], writes=['wgt'])
                S.op('dve', lambda en: en.tensor_tensor(
                    out=imp, in0=ob[0][:, :, 65:97], in1=wgt.rearrange("p (h o) -> p h o", o=1).broadcast_to([128, 16, 32]),
                    op=ALU.mult), reads=[('ob', 0), 'wgt'], writes=['imp'])
                S.op('dve', lambda en: en.tensor_reduce(
                    out=sc, in_=imp.rearrange("p (g r) j -> p g j r", g=4), axis=AX.X, op=ALU.add), reads=['imp'], writes=['sc'])
                S.op('dve', lambda en, qi=qi: en.tensor_tensor(
                    out=sc, in0=sc, in1=ABt[:, 0, qi:qi + 1, :].broadcast_to([128, 4, 32]), op=ALU.mult),
                    reads=['sc', 'ABt'], writes=['sc'])
                S.op('dve', lambda en, qi=qi: en.tensor_tensor(
                    out=sc, in0=sc, in1=ABt[:, 1, qi:qi + 1, :].broadcast_to([128, 4, 32]), op=ALU.add),
                    reads=['sc', 'ABt'], writes=['sc'])
                for g in range(4):
                    S.op('dve', lambda en, g=g: en.max(out=mx8, in_=sc[:, g, :]), reads=['sc'], writes=['mx8'])
                    S.op('dve', lambda en, g=g: en.match_replace(out=sc2, in_to_replace=mx8, in_values=sc[:, g, :], imm_value=-3.0e4),
                         reads=['sc', 'mx8'], writes=['sc2'])
                    S.op('dve', lambda en: en.max(out=mx8, in_=sc2), reads=['sc2'], writes=['mx8'])
                    S.op('dve', lambda en: en.tensor_reduce(out=thr, in_=mx8, axis=AX.X, op=ALU.min), reads=['mx8'], writes=['thr'])
                    S.op('dve', lambda en, g=g: en.tensor_scalar(out=selm[:, g, :], in0=sc[:, g, :], scalar1=thr, scalar2=None,
                                                                 op0=ALU.is_ge), reads=['sc', 'thr'], writes=['selm'])
                S.op('dve', lambda en: en.tensor_scalar(out=sc, in0=sc, scalar1=-5000.0, scalar2=None, op0=ALU.is_gt),
                     reads=['sc'], writes=['sc'])
                S.op('dve', lambda en: en.tensor_tensor(out=selm, in0=selm, in1=sc, op=ALU.mult), reads=['sc', 'selm'], writes=['selm'])
                S.op('dve', lambda en: en.tensor_scalar(out=negb, in0=selm, scalar1=-1.0, scalar2=-NEG, op0=ALU.add, op1=ALU.mult),
                     reads=['selm'], writes=['negb'])
                for g in range(4):
                    attend(2, g, qi, qt, list(range(max(0, qt - 4), qt + 1)), KTc[4], ('KT', 4),
                           lambda kt, g=g: Vt[5][:, kt, g, :], ('V', 5), 65)
                pb7 = banks[7].bitcast(BF16)
                for g in range(4):
                    S.op('pe', lambda en, g=g: en.transpose(out=pb7[0:32, 128 * g:128 * g + 128], in_=negb[:, g, :], identity=ident),
                         reads=['negb', 'ident'], writes=[('bank', 7)])
                S.op('act', lambda en: en.copy(out=negT, in_=pb7[0:32, 0:512].rearrange("p (g n) -> p g n", g=4)),
                     reads=[('bank', 7)], writes=['negT'])
                if stage < 6:
                    continue
                for g in range(4):
                    attend(1, g, qi, qt, list(range(0, qt + 1)), KTc[2], ('KT', 2), lambda kt, g=g: Vt[3][:, kt, g, :], ('V', 3), 65)
                for br in range(3):
                    S.op('dve', lambda en, br=br: en.tensor_scalar(out=wgt, in0=ob[br][:, :, 64], scalar1=1e-30, scalar2=None, op0=ALU.add),
                         reads=[('ob', br)], writes=['wgt'])
                    S.op('dve', lambda en: en.reciprocal(out=wgt, in_=wgt), reads=['wgt'], writes=['wgt'])
                    S.op('dve', lambda en, br=br: en.tensor_tensor(
                        out=wgt, in0=wgt, in1=gt.rearrange("p (h b) -> p h b", b=3)[:, :, br], op=ALU.mult),
                        reads=['wgt', 'gt'], writes=['wgt'])
                    dst = nsa if br == 0 else ntmp
                    S.op('dve', lambda en, br=br, dst=dst: en.tensor_tensor(
                        out=dst, in0=ob[br][:, :, 0:64], in1=wgt.rearrange("p (h o) -> p h o", o=1).broadcast_to([128, 16, 64]),
                        op=ALU.mult), reads=[('ob', br), 'wgt'], writes=['nsa' if br == 0 else 'ntmp'])
                    if br > 0:
                        S.op('dve', lambda en: en.tensor_tensor(out=nsa, in0=nsa, in1=ntmp, op=ALU.add),
                             reads=['nsa', 'ntmp'], writes=['nsa'])
                S.op('act', lambda en, i=i: en.copy(out=nsab[i], in_=nsa.rearrange("p h d -> p (h d)")),
                     reads=['nsa'], writes=[('nsab', i)])
                S.dma('sp', mix_s[qt, :, 1024:2048], nsab[i], reads=[('nsab', i)], writes=[('mix_s', qt, 1)])
            S.barrier()


    import os
    if stage >= 3 and not os.environ.get('SKIPC'):
        stage_C()
    def stage_D():
        ckvu = cache_kv
        with ExitStack() as esD:
            def sD(name, shape, dt=F32):
                return esD.enter_context(nc.sbuf_tensor(name, list(shape), dt)).ap()
            ptb_i = sD("ptb_i", [128, 32], I32)
            ptf = sD("ptf", [128, 32])
            idx_i = idx_i_perm
            upt = sD("upt", [128, 1])
            for hf in range(2):
                srcap = page_table_d[:, 8 * hf:8 * hf + 8].rearrange("s (k o) -> k o s", o=1).broadcast_to([8, 16, 16])
                for k_ in range(8):
                    S.dma('sp', ptb_i[16 * k_:16 * k_ + 16, 16 * hf:16 * hf + 16],
                          page_table_d[:, 8 * hf + k_:8 * hf + k_ + 1].rearrange("s o -> o s").broadcast_to([16, 16]),
                          writes=['ptb_i'])
            S.dma('sp', upt, upt_d, writes=['upt'])
            S.op('dve', lambda en: en.tensor_copy(out=ptf, in_=ptb_i), reads=['ptb_i'], writes=['ptf'])
            S.op('dve', lambda en: en.tensor_scalar(out=ptf, in0=ptf, scalar1=16.0, scalar2=upt, op0=ALU.mult, op1=ALU.add),
                 reads=['ptf', 'upt'], writes=['ptf'])
            S.op('dve', lambda en: en.tensor_copy(out=idx_i[:, 0:32], in_=ptf), reads=['ptf'], writes=['idx_i'])
            E_s = sD("E_s", [33, 17, 128], BF16)
            cmS = sD("cmS", [128, 2, 128], BF16)
            covS = sD("covS", [128, 33])
            ABs = sD("ABs", [128, 2, 33])
            Rsum = sD("Rsum", [32, 16, 128], BF16)
            ones64 = sD("ones64d", [64, 64], BF16)
            kncmp, b2k, b2v, b1e = sD("kncmpd", [64, 1]), sD("b2kd", [64, 1]), sD("b2vd", [128, 64]), sD("b1ed", [128, 4])
            S.dma('sp', E_s, Es_d, writes=['E_s'])
            S.dma('sp', cmS, cmS_d, writes=['cmS'])
            S.dma('sp', covS, covS_d, writes=['covS'])
            S.dma('sp', ABs, ABs_d, writes=['ABs'])
            S.dma('sp', Rsum, Rsum_d, writes=['Rsum'])
            S.op('pool', lambda en: en.memset(ones64, 1.0), writes=['ones64'])
            S.dma('sp', kncmp, k_norm_cmp.rearrange("(d o) -> d o", o=1), writes=['kncmp'])
            S.dma('sp', b2k, cmp_b2[0].rearrange("(d o) -> d o", o=1), writes=['b2k'])
            S.dma('sp', b2v, cmp_b2[1].partition_broadcast(128), writes=['b2v'])
            qTs = [sD("qTs0", [128, 16, 128], BF16), sD("qTs1", [128, 16, 128], BF16)]
            nqt = sD("nqtd", [128, 1024], BF16)
            S.dma('sp', nqt, nq_s[16], reads=[('nq_s', 16, 0), ('nq_s', 16, 1)], writes=['nqt'])
            for v in range(2):
                S.op('pool', lambda en, v=v: en.memset(qTs[v], 0.0), writes=[('qTs', v)])
                S.dma('sp', qTs[v][64:74, :, :], qaugs_d[v], writes=[('qTs', v)])
            for half in range(2):
                pb = banks[half].bitcast(BF16)
                for hh in range(8):
                    hd = 8 * half + hh
                    S.op('pe', lambda en, pb=pb, hh=hh, hd=hd: en.transpose(
                        out=pb[0:64, 128 * hh:128 * hh + 128], in_=nqt[:, 64 * hd:64 * hd + 64], identity=ident),
                        reads=['nqt', 'ident'], writes=[('bank', half)])
                for v in range(2):
                    S.op('act', lambda en, pb=pb, half=half, v=v: en.copy(
                        out=qTs[v][0:64, 8 * half:8 * half + 8, :], in_=pb[0:64, :].rearrange("p (h n) -> p h n", h=8)),
                        reads=[('bank', half)], writes=[('qTs', v)])
            G = [sD("G0", [128, 8, 1024], BF16), sD("G1", [128, 8, 1024], BF16)]
            hb16 = [sD(f"hb16{i}", [128, 1024], BF16) for i in range(2)]
            cwt = hist_all
            XT = [sD("XTk", [64, 4, 2048], BF16), sD("XTv", [64, 4, 2048], BF16)]
            w1 = sD("w1d", [64, 2, 32, 256], BF16)
            peT = sD("peTd", [64, 2, 32], BF16)
            w2s = sD("w2sd", [128, 2, 2, 64], BF16)
            b1t = sD("b1td", [128, 4])
            hidT = sD("hidTd", [128, 4, 512], BF16)
            kcT = sD("kcTd", [128, 4, 128], BF16)
            Vc = sD("Vcd", [128, 4, 98], BF16)
            kc32 = sD("kc32d", [64, 512])
            kcsq = sD("kcsqd", [64, 512], BF16)
            krs = sD("krsd", [64, 512])
            rden = sD("rdend", [32, 4])
            Unb = sD("Unbd", [32, 4, 33], BF16)
            KS = sD("KS", [128, 4, 2176], BF16)
            KW = sD("KW", [128, 4, 640], BF16)
            VS = sD("VS", [128, 17, 4, 65], BF16)
            VW = sD("VW", [128, 5, 4, 65], BF16)
            newt = sD("newt", [128, 1536], BF16)
            obr = [sD("ocsd", [32, 4, 98]), sD("ossd", [32, 4, 65]), sD("owsd", [32, 4, 65])]
            PT = [sD(f"PTd{i}", [128, 128], BF16) for i in range(3)]
            sc = sD("scd", [128, 4, 33])
            sc2 = sD("sc2d", [128, 33])
            mx8 = sD("mx8d", [128, 8])
            thr = sD("thrd", [128, 1])
            selm = sD("selmd", [128, 4, 33])
            negb = sD("negbd", [128, 4, 33], BF16)
            negT = sD("negTd", [33, 4, 128], BF16)
            negTx = sD("negTxd", [33, 4, 4, 8], BF16)
            cst = {'h': 0, 's': 0, 'p': 0}
            for c in range(2):
                S.dma('pool', w1[:, c], cmp_w1[c].rearrange("l d h -> d l h"), writes=['w1'])
            S.dma('pool', peT, cmp_pe.rearrange("c l d -> d c l"), writes=['peT'])
            S.dma('pool', w2s, cmp_w2.rearrange("c (k p) d -> p c k d", p=128), writes=['w2s'])
            S.dma('sp', b1t, cmp_b1.rearrange("c (k p) -> p (c k)", p=128), writes=['b1t'])
            for buf, key, val in ((kcT, 'kcT', 0.0), (Vc, 'Vc', 0.0), (kc32, 'kc32', 0.0), (hidT, 'hidT', 0.0), (KS, 'KS', 0.0),
                                  (KW, 'KW', 0.0), (VS, 'VS', 1.0), (VW, 'VW', 1.0), (newt, 'newt', 0.0)):
                S.op('pool', lambda en, buf=buf, val=val: en.memset(buf, val), writes=[key])
            for g in range(4):
                S.dma('sp', kcT[64:74, g, :], kaugcS_d, writes=['kcT'])
                S.op('pool', lambda en, g=g: en.memset(Vc[:, g, 64:65], 1.0), writes=['Vc'])
                S.op('dve', lambda en, g=g: en.tensor_copy(out=Vc[:, g, 65:98], in_=covS), reads=['covS'], writes=['Vc'])
                S.dma('sp', KS[64:74, g, :], kaugS_d, writes=['KS'])
                S.dma('sp', KW[64:74, g, :], kaugW_d, writes=['KW'])
            for c in range(2):
                for hc in range(2):
                    for l in range(32):
                        S.op('pe', lambda en, c=c, hc=hc, l=l: en.matmul(
                            banks[7][:, 2 * c + hc:2 * c + hc + 1], lhsT=w1[:, c, l, 128 * hc:128 * hc + 128],
                            rhs=peT[:, c, l:l + 1], start=(l == 0), stop=(l == 31)),
                            reads=['w1', 'peT'], writes=[('bank', 7)])
            S.op('dve', lambda en: en.tensor_tensor(out=b1e, in0=banks[7][:, 0:4], in1=b1t, op=ALU.add),
                 reads=[('bank', 7), 'b1t'], writes=['b1e'])

            def transpose_g(src, skey, c0, dst_fn, dkey):
                bk = cst['s'] % 5
                cst['s'] += 1
                warm(bk)
                pb = banks[bk].bitcast(BF16)
                for g in range(4):
                    S.op('pe', lambda en, pb=pb, g=g: en.transpose(
                        out=pb[0:64, 128 * g:128 * g + 128], in_=src[:, c0 + 64 * g:c0 + 64 * g + 64], identity=ident),
                        reads=[skey, 'ident'], writes=[('bank', bk)])
                S.op('act', lambda en, pb=pb: en.copy(out=dst_fn(), in_=pb[0:64, 0:512].rearrange("p (g n) -> p g n", g=4)),
                     reads=[('bank', bk)], writes=[dkey])

            def scores(bank_i, Kt, kkey, kcols, qv, sq_i, extra):
                warm(bank_i)
                for g in range(4):
                    S.op('pe', lambda en, g=g, last=(g == 3 and not extra): en.matmul(
                        banks[bank_i][:, 32 * g:32 * g + 32], lhsT=Kt[0:74, g, kcols],
                        rhs=qTs[qv][0:74, 4 * g:4 * g + 4, 8 * sq_i:8 * sq_i + 8],
                        start=(g == 0), stop=last, skip_group_check=True),
                        reads=[kkey, ('qTs', qv)], writes=[('bank', bank_i)])
                for ei, (l_, r_, ks) in enumerate(extra):
                    S.op('pe', lambda en, l_=l_, r_=r_, last=(ei == len(extra) - 1): en.matmul(
                        banks[bank_i][:, 0:128], lhsT=l_, rhs=r_, start=False, stop=last, skip_group_check=True),
                        reads=ks, writes=[('bank', bank_i)])
                pi = cst['p'] % 3
                cst['p'] += 1
                S.op('act', lambda en, pi=pi: en.activation(out=PT[pi], in_=banks[bank_i][:, 0:128], func=AF.Exp),
                     reads=[('bank', bank_i)], writes=[('PT', pi)])
                return pi

            def pv(obk, pi, Vfn, vkey, ncol, first, last):
                for g in range(4):
                    S.op('pe', lambda en, g=g: en.matmul(
                        banks[obk][0:32, ncol * g:ncol * g + ncol], lhsT=PT[pi][:, 32 * g:32 * g + 32], rhs=Vfn(g),
                        start=(first and g == 0), stop=last, skip_group_check=True),
                        reads=[('PT', pi), vkey], writes=[('bank', obk)])

            def store_branch(br, sq_i, obk, ncol):
                S.op('act', lambda en: en.copy(out=obr[br], in_=banks[obk][0:32, 0:4 * ncol].rearrange("p (g c) -> p g c", g=4)),
                     reads=[('bank', obk)], writes=[('obr', br)])
                for r in range(4):
                    S.dma('sp', osc_s[br][sq_i, :, :, r, :], obr[br][8 * r:8 * r + 8, :, 0:65],
                          reads=[('obr', br)], writes=[('osc_s', br)])

            for sq_i in range(int(os.environ.get('SEQL', 16))):
                for hf in range(2):
                    S.dma_gather(G[hf].rearrange("p j c -> p (j c)"), ckvu, idx_i[:, 16 * hf + sq_i:16 * hf + sq_i + 1], 2560 * 16,
                                 reads=['idx_i'], writes=[('G', hf)])
                    for j in range(8):
                        kt = 2 * j + hf
                        Gj = G[hf][:, j, :]
                        for c in range(2):
                            transpose_g(Gj, ('G', hf), 256 * c,
                                        lambda c=c, kt=kt: XT[c][0:64, :, 128 * kt:128 * kt + 128], ('XT', c))
                        transpose_g(Gj, ('G', hf), 512, lambda kt=kt: KS[0:64, :, 128 * kt:128 * kt + 128], 'KS')
                        S.op('dve', lambda en, Gj=Gj, kt=kt: en.tensor_copy(
                            out=VS[:, kt, :, 0:64], in_=Gj[:, 768:1024].rearrange("p (g d) -> p g d", g=4)),
                            reads=[('G', hf)], writes=['VS'])
                for k in range(4):
                    S.dma('sp', cwt[:, k % 2, 0:512], cache_win[sq_i, 128 * k:128 * k + 128, :], writes=[('cwt', k % 2)])
                    hi = cst['h'] % 2
                    cst['h'] += 1
                    S.op('dve', lambda en, hi=hi, k=k: en.tensor_copy(out=hb16[hi][:, 0:512], in_=cwt[:, k % 2, 0:512]),
                         reads=[('cwt', k % 2)], writes=[('hb16', hi)])
                    transpose_g(hb16[hi], ('hb16', hi), 0, lambda k=k: KW[0:64, :, 128 * k:128 * k + 128], 'KW')
                    S.op('dve', lambda en, hi=hi, k=k: en.tensor_copy(
                        out=VW[:, k, :, 0:64], in_=hb16[hi][:, 256:512].rearrange("p (g d) -> p g d", g=4)),
                        reads=[('hb16', hi)], writes=['VW'])
                S.dma('sp', newt[0:8, :], kvs_s[16, 8 * sq_i:8 * sq_i + 8, :],
                      reads=[('kvs_s', 16, 0), ('kvs_s', 16, 1), ('kvs_s', 16, 2)], writes=['newt'])
                transpose_g(newt, 'newt', 512, lambda: KS[0:64, :, 2048:2176], 'KS')
                transpose_g(newt, 'newt', 1024, lambda: KW[0:64, :, 512:640], 'KW')
                S.op('dve', lambda en: en.tensor_copy(out=VS[:, 16, :, 0:64], in_=newt[:, 768:1024].rearrange("p (g d) -> p g d", g=4)),
                     reads=['newt'], writes=['VS'])
                S.op('dve', lambda en: en.tensor_copy(out=VW[:, 4, :, 0:64], in_=newt[:, 1280:1536].rearrange("p (g d) -> p g d", g=4)),
                     reads=['newt'], writes=['VW'])
                for c in range(2):
                    Xv = XT[c].rearrange("p g (j n two) -> p g j two n", j=8, two=2)
                    for hc in range(2):
                        bk = 2 + (2 * c + hc) % 3
                        for hf in range(2):
                            for l in range(16):
                                S.op('pe', lambda en, c=c, hc=hc, hf=hf, l=l, bk=bk, Xv=Xv: en.matmul(
                                    banks[bk][:, 0:508].rearrange("p (g n) -> p g n", g=4),
                                    lhsT=w1[:, c, 16 * hf + l, 128 * hc:128 * hc + 128],
                                    rhs=Xv[:, :, l % 8, l // 8, hf:hf + 127], start=(hf == 0 and l == 0), stop=(hf == 1 and l == 15)),
                                    reads=['w1', ('XT', c)], writes=[('bank', bk)])
                        S.op('act', lambda en, c=c, hc=hc, bk=bk: en.activation(
                            out=hidT[:, 2 * c + hc, 0:508], in_=banks[bk][:, 0:508], func=AF.Silu,
                            bias=b1e[:, 2 * c + hc:2 * c + hc + 1]), reads=[('bank', bk), 'b1e'], writes=['hidT'])
                for hc in range(2):
                    S.op('pe', lambda en, hc=hc: en.matmul(banks[6][0:64, 0:508], lhsT=w2s[:, 0, hc, :], rhs=hidT[:, hc, 0:508],
                                                           start=(hc == 0), stop=(hc == 1)),
                         reads=['w2s', 'hidT'], writes=[('bank', 6)])
                S.op('act', lambda en: en.activation(out=kc32[:, 0:508], in_=banks[6][0:64, 0:508], func=AF.Identity, bias=b2k),
                     reads=[('bank', 6), 'b2k'], writes=['kc32'])
                S.op('dve', lambda en: en.tensor_tensor(out=kcsq, in0=kc32, in1=kc32, op=ALU.mult), reads=['kc32'], writes=['kcsq'])
                S.op('pe', lambda en: en.matmul(banks[7][0:64, 0:512], lhsT=ones64, rhs=kcsq, start=True, stop=True),
                     reads=['ones64', 'kcsq'], writes=[('bank', 7)])
                S.op('act', lambda en: en.activation(out=krs, in_=banks[7][0:64, 0:512], func=AF.Sqrt, scale=1.0 / 64, bias=epsb[0:64]),
                     reads=[('bank', 7), 'epsb'], writes=['krs'])
                S.op('dve', lambda en: en.reciprocal(out=krs, in_=krs), reads=['krs'], writes=['krs'])
                S.op('dve', lambda en: en.tensor_tensor(out=kc32, in0=kc32, in1=krs, op=ALU.mult), reads=['kc32', 'krs'], writes=['kc32'])
                S.op('dve', lambda en: en.tensor_scalar(
                    out=kcT[0:64, :, 0:127], in0=kc32[:, 0:508].rearrange("p (g n) -> p g n", g=4), scalar1=kncmp, scalar2=None,
                    op0=ALU.mult), reads=['kc32', 'kncmp'], writes=['kcT'])
                for g in range(4):
                    for hc in range(2):
                        S.op('pe', lambda en, g=g, hc=hc: en.matmul(
                            banks[5][0:127, 64 * g:64 * g + 64], lhsT=hidT[:, 2 + hc, 127 * g:127 * g + 127], rhs=w2s[:, 1, hc, :],
                            start=(g == 0 and hc == 0), stop=(hc == 1), skip_group_check=True),
                            reads=['w2s', 'hidT'], writes=[('bank', 5)])
                S.op('dve', lambda en: en.tensor_tensor(
                    out=Vc[0:127, :, 0:64], in0=banks[5][0:127, 0:256].rearrange("p (g d) -> p g d", g=4),
                    in1=b2v[0:127].rearrange("p (o d) -> p o d", o=1).broadcast_to([127, 4, 64]), op=ALU.add),
                    reads=[('bank', 5), 'b2v'], writes=['Vc'])
                pi = scores(2, kcT, 'kcT', slice(0, 128), 0, sq_i, [])
                pv(3, pi, lambda g: Vc[:, g, :], 'Vc', 98, True, True)
                store_branch(0, sq_i, 3, 98)
                S.op('dve', lambda en: en.tensor_scalar(out=rden, in0=obr[0][:, :, 64], scalar1=1e-30, scalar2=None, op0=ALU.add),
                     reads=[('obr', 0)], writes=['rden'])
                S.op('dve', lambda en: en.reciprocal(out=rden, in_=rden), reads=['rden'], writes=['rden'])
                S.op('dve', lambda en: en.tensor_tensor(
                    out=Unb, in0=obr[0][:, :, 65:98], in1=rden.rearrange("p (g o) -> p g o", o=1).broadcast_to([32, 4, 33]),
                    op=ALU.mult), reads=[('obr', 0), 'rden'], writes=['Unb'])
                S.op('pe', lambda en, sq_i=sq_i: en.matmul(
                    banks[4][:, 0:132], lhsT=Rsum[:, sq_i, :], rhs=Unb.rearrange("p g j -> p (g j)"), start=True, stop=True),
                    reads=['Rsum', 'Unb'], writes=[('bank', 4)])
                S.op('dve', lambda en: en.tensor_tensor(
                    out=sc, in0=banks[4][:, 0:132].rearrange("p (g j) -> p g j", g=4),
                    in1=ABs[:, 0:1, :].broadcast_to([128, 4, 33]), op=ALU.mult), reads=[('bank', 4), 'ABs'], writes=['sc'])
                S.op('dve', lambda en: en.tensor_tensor(out=sc, in0=sc, in1=ABs[:, 1:2, :].broadcast_to([128, 4, 33]), op=ALU.add),
                     reads=['sc', 'ABs'], writes=['sc'])
                for g in range(4):
                    S.op('dve', lambda en, g=g: en.max(out=mx8, in_=sc[:, g, :]), reads=['sc'], writes=['mx8'])
                    S.op('dve', lambda en, g=g: en.match_replace(out=sc2, in_to_replace=mx8, in_values=sc[:, g, :], imm_value=-3.0e4),
                         reads=['sc', 'mx8'], writes=['sc2'])
                    S.op('dve', lambda en: en.max(out=mx8, in_=sc2), reads=['sc2'], writes=['mx8'])
                    S.op('dve', lambda en: en.tensor_reduce(out=thr, in_=mx8, axis=AX.X, op=ALU.min), reads=['mx8'], writes=['thr'])
                    S.op('dve', lambda en, g=g: en.tensor_scalar(out=selm[:, g, :], in0=sc[:, g, :], scalar1=thr, scalar2=None,
                                                                 op0=ALU.is_ge), reads=['sc', 'thr'], writes=['selm'])
                S.op('dve', lambda en: en.tensor_scalar(out=negb, in0=selm, scalar1=-1.0, scalar2=-NEG, op0=ALU.add, op1=ALU.mult),
                     reads=['selm'], writes=['negb'])
                for kt in range(5):
                    extra = []
                    if kt == 0:
                        extra.append((ident, cmS[:, 1, :], ['ident', 'cmS']))
                    if kt == 4:
                        extra.append((ident, cmS[:, 0, :], ['ident', 'cmS']))
                    pi = scores(2 + kt % 3, KW, 'KW', slice(128 * kt, 128 * kt + 128), 1, sq_i, extra)
                    if kt > 0:
                        pv(6, pprev[0], lambda g, kt=kt: VW[:, kt - 1, g, :], 'VW', 65, kt - 1 == 0, False)
                    pprev = [pi]
                pv(6, pprev[0], lambda g: VW[:, 4, g, :], 'VW', 65, False, True)
                store_branch(2, sq_i, 6, 65)
                pb7 = banks[7].bitcast(BF16)
                for g in range(4):
                    S.op('pe', lambda en, g=g, pb7=pb7: en.transpose(out=pb7[0:33, 128 * g:128 * g + 128], in_=negb[:, g, :], identity=ident),
                         reads=['negb', 'ident'], writes=[('bank', 7)])
                S.op('act', lambda en, pb7=pb7: en.copy(out=negT, in_=pb7[0:33, 0:512].rearrange("p (g n) -> p g n", g=4)),
                     reads=[('bank', 7)], writes=['negT'])
                for g in range(4):
                    S.op('dve', lambda en, g=g, sq_i=sq_i: en.tensor_copy(
                        out=negTx[:, g, :, :], in_=negT[:, g:g + 1, 8 * sq_i:8 * sq_i + 8].broadcast_to([33, 4, 8])),
                        reads=['negT'], writes=['negTx'])
                for kt in range(17):
                    extra = [(E_s[:, kt, :], negTx.rearrange("p g r q -> p (g r q)"), ['E_s', 'negTx'])]
                    if kt == 16:
                        extra.append((ident, cmS[:, 0, :], ['ident', 'cmS']))
                    pi = scores(2 + kt % 3, KS, 'KS', slice(128 * kt, 128 * kt + 128), 0, sq_i, extra)
                    if kt > 0:
                        pv(5, pprev[0], lambda g, kt=kt: VS[:, kt - 1, g, :], 'VS', 65, kt - 1 == 0, False)
                    pprev = [pi]
                pv(5, pprev[0], lambda g: VS[:, 16, g, :], 'VS', 65, False, True)
                store_branch(1, sq_i, 5, 65)
            S.barrier()
        with ExitStack() as esF:
            def sF(name, shape, dt=F32):
                return esF.enter_context(nc.sbuf_tensor(name, list(shape), dt)).ap()
            ob = {}
            for br in range(3):
                ob[br] = sF(f"obd{br}", [128, 16, 65])
                S.dma('sp', ob[br], osc_s[br].rearrange("s q g r c -> (s q) (g r) c"), reads=[('osc_s', br)], writes=[('ob', br)])
            gt = sF("gtd", [128, 48])
            S.dma('sp', gt, gate_s[16], reads=[('gate_s', 16)], writes=['gt'])
            wgt = sF("wgtd", [128, 16])
            nsa = sF("nsad", [128, 16, 64])
            ntmp = sF("ntmpd", [128, 16, 64])
            nsab = sF("nsabd", [128, 1024], BF16)
            for br in range(3):
                S.op('dve', lambda en, br=br: en.tensor_scalar(out=wgt, in0=ob[br][:, :, 64], scalar1=1e-30, scalar2=None, op0=ALU.add),
                     reads=[('ob', br)], writes=['wgt'])
                S.op('dve', lambda en: en.reciprocal(out=wgt, in_=wgt), reads=['wgt'], writes=['wgt'])
                S.op('dve', lambda en, br=br: en.tensor_tensor(
                    out=wgt, in0=wgt, in1=gt.rearrange("p (h b) -> p h b", b=3)[:, :, br], op=ALU.mult),
                    reads=['wgt', 'gt'], writes=['wgt'])
                dst = nsa if br == 0 else ntmp
                S.op('dve', lambda en, br=br, dst=dst: en.tensor_tensor(
                    out=dst, in0=ob[br][:, :, 0:64], in1=wgt.rearrange("p (h o) -> p h o", o=1).broadcast_to([128, 16, 64]),
                    op=ALU.mult), reads=[('ob', br), 'wgt'], writes=['nsa' if br == 0 else 'ntmp'])
                if br > 0:
                    S.op('dve', lambda en: en.tensor_tensor(out=nsa, in0=nsa, in1=ntmp, op=ALU.add),
                         reads=['nsa', 'ntmp'], writes=['nsa'])
            S.op('act', lambda en: en.copy(out=nsab, in_=nsa.rearrange("p h d -> p (h d)")), reads=['nsa'], writes=['nsab'])
            S.dma('sp', mix_s[16, :, 1024:2048], nsab, reads=['nsab'], writes=[('mix_s', 16, 1)])
            S.barrier()
    if stage >= 7:
        stage_D()

    def stage_E():
        with ExitStack() as esE:
            def sE(name, shape, dt=F32):
                return esE.enter_context(nc.sbuf_tensor(name, list(shape), dt)).ap()
            hres = sE("hres", [128, 10, D])
            for n_, t in enumerate(QT):
                src = xp[128 * t:128 * (t + 1), :] if t < 16 else xs[:, :]
                S.dma('sp', hres[:, n_, :], src, writes=[('hres', n_)])
            with ExitStack() as es1:
                wo = es1.enter_context(nc.sbuf_tensor("wo", [128, 16, D], BF16)).ap()
                mxt = [es1.enter_context(nc.sbuf_tensor(f"mxt{i}", [128, D], BF16)).ap() for i in range(2)]
                mxT = [es1.enter_context(nc.sbuf_tensor(f"mxT{i}", [128, 16, 128], BF16)).ap() for i in range(2)]
                w_out_v = w_out.rearrange("(k p) n -> p k n", p=128)
                for j in range(4):
                    S.dma('pool', wo[:, :, 512 * j:512 * j + 512], w_out_v[:, :, 512 * j:512 * j + 512], writes=[('wo', j)])
                for n_, t in enumerate(QT):
                    i = n_ % 2
                    S.dma('sp', mxt[i], mix_s[t], reads=[('mix_s', t, 0), ('mix_s', t, 1)], writes=[('mxt', i)])
                    for half in range(2):
                        pb = banks[half].bitcast(BF16)
                        for kk in range(8):
                            k = 8 * half + kk
                            S.op('pe', lambda en, pb=pb, kk=kk, k=k, i=i: en.transpose(
                                out=pb[:, 128 * kk:128 * kk + 128], in_=mxt[i][:, 128 * k:128 * k + 128], identity=ident),
                                reads=[('mxt', i), 'ident'], writes=[('bank', half)])
                        S.op('act', lambda en, pb=pb, half=half, i=i: en.copy(
                            out=mxT[i][:, 8 * half:8 * half + 8, :], in_=pb.rearrange("p (k n) -> p k n", k=8)),
                            reads=[('bank', half)], writes=[('mxT', i)])
                    for j in range(4):
                        bk = next_bank()
                        for k in range(16):
                            S.op('pe', lambda en, bk=bk, k=k, j=j, i=i: en.matmul(
                                banks[bk], lhsT=mxT[i][:, k, :], rhs=wo[:, k, 512 * j:512 * j + 512],
                                start=(k == 0), stop=(k == 15)), reads=[('mxT', i), ('wo', j)], writes=[('bank', bk)])
                        hv = hres[:, n_, 512 * j:512 * j + 512]
                        S.op('dve', lambda en, hv=hv, bk=bk: en.tensor_tensor(out=hv, in0=hv, in1=banks[bk], op=ALU.add),
                             reads=[('bank', bk), ('hres', n_)], writes=[('hres', n_)])
                S.barrier()
            hnT = sE("hnT", [128, 10, 16, 128], BF16)
            gfT = sE("gfT", [128, 16])
            S.dma('sp', gfT, g_ffn.rearrange("(k p) -> p k", p=128), writes=['gfT'])
            with ExitStack() as es2:
                hb = [es2.enter_context(nc.sbuf_tensor(f"hb{i}", [128, D], BF16)).ap() for i in range(2)]
                hss = [es2.enter_context(nc.sbuf_tensor(f"hss{i}", [128, 1], F32)).ap() for i in range(2)]
                for n_ in range(10):
                    i = n_ % 2
                    hv = hres[:, n_, :]
                    S.op('act', lambda en, i=i, hv=hv: en.activation(out=hb[i], in_=hv, func=AF.Square, accum_out=hss[i]),
                         reads=[('hres', n_)], writes=[('hb', i), ('hss', i)])
                    S.op('act', lambda en, i=i: en.activation(out=hss[i], in_=hss[i], func=AF.Sqrt, scale=1.0 / D, bias=epsb),
                         reads=[('hss', i), 'epsb'], writes=[('hss', i)])
                    S.op('dve', lambda en, i=i: en.reciprocal(out=hss[i], in_=hss[i]), reads=[('hss', i)], writes=[('hss', i)])
                    S.op('act', lambda en, i=i, hv=hv: en.activation(out=hb[i], in_=hv, func=AF.Copy, scale=hss[i]),
                         reads=[('hres', n_), ('hss', i)], writes=[('hb', i)])
                    for half in range(2):
                        pb = banks[half].bitcast(BF16)
                        for kk in range(8):
                            k = 8 * half + kk
                            S.op('pe', lambda en, pb=pb, kk=kk, k=k, i=i: en.transpose(
                                out=pb[:, 128 * kk:128 * kk + 128], in_=hb[i][:, 128 * k:128 * k + 128], identity=ident),
                                reads=[('hb', i), 'ident'], writes=[('bank', half)])
                        S.op('dve', lambda en, pb=pb, half=half, n_=n_: en.tensor_tensor(
                            out=hnT[:, n_, 8 * half:8 * half + 8, :], in0=pb.rearrange("p (k n) -> p k n", k=8),
                            in1=gfT[:, 8 * half:8 * half + 8].rearrange("p (k o) -> p k o", o=1).broadcast_to([128, 8, 128]),
                            op=ALU.mult), reads=[('bank', half), 'gfT'], writes=['hnT'])
                S.barrier()
            SBW = 256
            NSB = 5632 // SBW
            wu = {(ab, i): sE(f"wu{ab}{i}", [128, 16, SBW], BF16) for ab in range(2) for i in range(2)}
            wd = [hist_all.rearrange("p a n -> p (a n)").bitcast(BF16).rearrange("p (c n) -> p c n", c=2), sE("wd1", [128, 2, D], BF16)]
            identf = sE("identf_sb", [128, 128])
            S.dma('sp', identf, identf_d, writes=['identf'])
            hfl = sE("hfl", [128, 1])
            ust = sE("ust", [128, 32])
            S.dma('sp', hfl, hflag_d.partition_broadcast(128), writes=['hfl'])
            cw = sE("cw", [128, 3, 88])
            cbv = sE("cbv", [128, 88])
            for j in range(3):
                S.dma('sp', cw[:, j, :], conv_w[j].rearrange("(k p) -> p k", p=128), writes=['cw'])
            S.dma('sp', cbv, conv_b.rearrange("(k p) -> p k", p=128), writes=['cbv'])
            up_ = {ab: sE(f"up{ab}", [128, 1154]) for ab in range(2)}
            us_ = {ab: sE(f"us{ab}", [128, 16, 10]) for ab in range(2)}
            cp_ = {ab: sE(f"cp{ab}", [128, 1152]) for ab in range(2)}
            cs2_ = {ab: sE(f"cs2{ab}", [128, 16, 8]) for ab in range(2)}
            actT = [sE(f"actT{i}", [128, 2, 1280], BF16) for i in range(2)]
            pvt0 = sE("pvt0", [32, 2, SBW]); pvt = [pvt0, pvt0]
            cvo0 = sE("cvo0", [34, 2, SBW]); cvo = [cvo0, cvo0]
            for ab in range(2):
                S.op('pool', lambda en, ab=ab: en.memset(up_[ab][:, 0:2], 0.0), writes=[('up', ab)])
            w_up_v = w_up.rearrange("(k p) n -> p k n", p=128)
            w_down_v = w_down.rearrange("(c p) n -> p c n", p=128)
            state_conv_v = state_conv_d.rearrange("s j n -> (s j) n")
            TG = [(0, 4), (4, 4), (8, 2)]
            pend_down = [None]
            for sbi in range(NSB):
                i = sbi % 2
                for ab in range(2):
                    S.dma('pool', wu[(ab, i)], w_up_v[:, :, 5632 * ab + SBW * sbi:5632 * ab + SBW * sbi + SBW],
                          writes=[('wu', ab, i)])
                    S.dma('sp', pvt[i][:, ab, :], state_conv_v[:, 5632 * ab + SBW * sbi:5632 * ab + SBW * sbi + SBW],
                          writes=[('pvt', 0)])
                S.dma('pool', wd[i], w_down_v[:, 2 * sbi:2 * sbi + 2, :], writes=[('wd', i)])
                for fc in range(2):
                    kch = 2 * sbi + fc
                    for ab in range(2):
                        kk = 44 * ab + kch
                        bkp = next_bank()
                        S.op('pe', lambda en, bkp=bkp, i=i, ab=ab, fc=fc: en.transpose(
                            out=banks[bkp][:, 0:32], in_=pvt[i][:, ab, 128 * fc:128 * fc + 128], identity=identf[0:32, 0:32]),
                            reads=[('pvt', 0), 'identf'], writes=[('bank', bkp)])
                        S.op('act', lambda en, bkp=bkp, ab=ab: en.copy(
                            out=us_[ab][:, :, 0:2], in_=banks[bkp][:, 0:32].rearrange("p (s j) -> p s j", j=2)),
                            reads=[('bank', bkp)], writes=[('us', ab)])
                        for (t0, nt) in TG:
                            bk = next_bank()
                            for k in range(16):
                                S.op('pe', lambda en, bk=bk, k=k, ab=ab, i=i, fc=fc, t0=t0, nt=nt: en.matmul(
                                    banks[bk][:, 0:128 * nt], lhsT=wu[(ab, i)][:, k, 128 * fc:128 * fc + 128],
                                    rhs=hnT[:, t0:t0 + nt, k, :], start=(k == 0), stop=(k == 15)),
                                    reads=[('wu', ab, i), 'hnT'], writes=[('bank', bk)])
                            if t0 == 0:
                                S.op('act', lambda en, bk=bk, ab=ab: en.activation(
                                    out=up_[ab][:, 2:130], in_=banks[bk][:, 0:128], func=AF.Copy, scale=hfl),
                                    reads=[('bank', bk), 'hfl'], writes=[('up', ab)])
                                S.op('act', lambda en, bk=bk, ab=ab: en.copy(out=up_[ab][:, 130:514], in_=banks[bk][:, 128:512]),
                                     reads=[('bank', bk)], writes=[('up', ab)])
                            elif t0 == 4:
                                S.op('act', lambda en, bk=bk, ab=ab: en.copy(out=up_[ab][:, 514:1026], in_=banks[bk]),
                                     reads=[('bank', bk)], writes=[('up', ab)])
                            else:
                                S.op('act', lambda en, bk=bk, ab=ab: en.copy(out=up_[ab][:, 1026:1154], in_=banks[bk][:, 0:128]),
                                     reads=[('bank', bk)], writes=[('up', ab)])
                                S.op('act', lambda en, bk=bk, ab=ab: en.copy(
                                    out=us_[ab][:, :, 2:10], in_=banks[bk][:, 128:256].rearrange("p (s q) -> p s q", q=8)),
                                    reads=[('bank', bk)], writes=[('us', ab)])
                        bkc = next_bank()
                        S.op('dve', lambda en, ab=ab: en.tensor_copy(
                            out=ust.rearrange("p (s j) -> p s j", j=2), in_=us_[ab][:, :, 8:10]),
                            reads=[('us', ab)], writes=['ust'])
                        S.op('pe', lambda en, bkc=bkc, ab=ab: en.transpose(
                            out=banks[bkc][0:32, 0:128], in_=ust, identity=identf),
                            reads=['ust', 'identf'], writes=[('bank', bkc)])
                        S.op('pe', lambda en, bkc=bkc, ab=ab: en.transpose(
                            out=banks[bkc][0:2, 128:256], in_=up_[ab][:, 1152:1154], identity=identf),
                            reads=[('up', ab), 'identf'], writes=[('bank', bkc)])
                        S.op('act', lambda en, bkc=bkc, ab=ab, i=i, fc=fc: en.copy(
                            out=cvo[i][0:32, ab, 128 * fc:128 * fc + 128], in_=banks[bkc][0:32, 0:128]),
                            reads=[('bank', bkc)], writes=[('cvo', 0, 0)])
                        S.op('act', lambda en, bkc=bkc, ab=ab, i=i, fc=fc: en.copy(
                            out=cvo[i][32:34, ab, 128 * fc:128 * fc + 128], in_=banks[bkc][0:2, 128:256]),
                            reads=[('bank', bkc)], writes=[('cvo', 0, 1)])
                        w0, w1_, w2_ = cw[:, 0, kk:kk + 1], cw[:, 1, kk:kk + 1], cw[:, 2, kk:kk + 1]
                        bb = cbv[:, kk:kk + 1]
                        for (u, c, ku, kc_, sl) in ((up_[ab], cp_[ab], ('up', ab), ('cp', ab),
                                                     (slice(2, 1154), slice(1, 1153), slice(0, 1152))),
                                                    (us_[ab], cs2_[ab], ('us', ab), ('cs2', ab),
                                                     (slice(2, 10), slice(1, 9), slice(0, 8)))):
                            if u is up_[ab]:
                                u2, u1, u0 = u[:, sl[0]], u[:, sl[1]], u[:, sl[2]]
                            else:
                                u2, u1, u0 = u[:, :, sl[0]], u[:, :, sl[1]], u[:, :, sl[2]]
                            S.op('act', lambda en, c=c, u2=u2, w2_=w2_, bb=bb: en.activation(
                                out=c, in_=u2, func=AF.Identity, scale=w2_, bias=bb), reads=[ku, 'cw', 'cbv'], writes=[kc_])
                            S.op('dve', lambda en, c=c, u1=u1, w1_=w1_: en.scalar_tensor_tensor(
                                out=c, in0=u1, scalar=w1_, in1=c, op0=ALU.mult, op1=ALU.add), reads=[ku, kc_, 'cw'], writes=[kc_])
                            S.op('dve', lambda en, c=c, u0=u0, w0=w0: en.scalar_tensor_tensor(
                                out=c, in0=u0, scalar=w0, in1=c, op0=ALU.mult, op1=ALU.add), reads=[ku, kc_, 'cw'], writes=[kc_])
                    S.op('act', lambda en: en.activation(out=cp_[0], in_=cp_[0], func=AF.Silu), reads=[('cp', 0)], writes=[('cp', 0)])
                    S.op('act', lambda en: en.activation(out=cs2_[0], in_=cs2_[0], func=AF.Silu), reads=[('cs2', 0)], writes=[('cs2', 0)])
                    S.op('dve', lambda en, i=i, fc=fc: en.tensor_tensor(out=actT[i][:, fc, 0:1152], in0=cp_[0], in1=cp_[1], op=ALU.mult),
                         reads=[('cp', 0), ('cp', 1)], writes=[('actT', i)])
                    S.op('dve', lambda en, i=i, fc=fc: en.tensor_tensor(
                        out=actT[i][:, fc, 1152:1280].rearrange("p (s q) -> p s q", q=8), in0=cs2_[0], in1=cs2_[1], op=ALU.mult),
                        reads=[('cs2', 0), ('cs2', 1)], writes=[('actT', i)])
                for ab in range(2):
                    S.dma('sp', conv_s_out[:, 5632 * ab + SBW * sbi:5632 * ab + SBW * sbi + SBW], cvo[i][0:32, ab, :],
                          reads=[('cvo', 0, 0)], writes=[('conv_s_out', sbi, ab)])
                    S.dma('sp', conv_p_out[:, 5632 * ab + SBW * sbi:5632 * ab + SBW * sbi + SBW], cvo[i][32:34, ab, :],
                          reads=[('cvo', 0, 1)], writes=[('conv_p_out', sbi, ab)])
                def emit_down(i=i):
                    for n_ in range(10):
                        for j in range(4):
                            bk = next_bank()
                            for fc in range(2):
                                S.op('pe', lambda en, bk=bk, fc=fc, i=i, n_=n_, j=j: en.matmul(
                                    banks[bk], lhsT=actT[i][:, fc, 128 * n_:128 * n_ + 128], rhs=wd[i][:, fc, 512 * j:512 * j + 512],
                                    start=(fc == 0), stop=(fc == 1)), reads=[('actT', i), ('wd', i)], writes=[('bank', bk)])
                            hv = hres[:, n_, 512 * j:512 * j + 512]
                            S.op('dve', lambda en, hv=hv, bk=bk: en.tensor_tensor(out=hv, in0=hv, in1=banks[bk], op=ALU.add),
                                 reads=[('bank', bk), ('hres', n_)], writes=[('hres', n_)])
                if pend_down[0] is not None:
                    pend_down[0]()
                pend_down[0] = emit_down
            pend_down[0]()
            for n_ in range(1, 10):
                S.dma('sp', y_out[128 * (n_ - 1):128 * n_, :], hres[:, n_, :], reads=[('hres', n_)], writes=[('y_out', n_)])
            S.barrier()
    if stage >= 8:
        stage_E()

    S.finish()
    with nc.allow_non_contiguous_dma(reason="small constant tables / strided layouts"):
        S.emit()
    return nc, S


def _host_inputs(inputs, c):
    b, h = c // 2, c % 2
    f = lambda a: np.ascontiguousarray(np.asarray(a, dtype=np.float32))
    xpb = np.asarray(inputs['x_prompt'][b], dtype=np.float32)
    if h == 0:
        xp = np.concatenate([np.zeros((1024, D), np.float32), xpb[:1024]], axis=0)
    else:
        xp = xpb
    pos = np.zeros((17, 128), np.float64)
    for t in range(16):
        pos[t] = 128 * t + np.arange(128) - (1024 if h == 0 else 0)
    pos[16] = 2048 + (np.arange(128) % 8)
    inv = (1.0 / (10000.0 ** np.linspace(0.0, 1.0, 128, dtype=np.float32))).astype(np.float32)
    ang = (pos.astype(np.float32)[:, :, None] * inv[None, None, :]).astype(np.float32)
    cs = np.stack([np.cos(ang), np.sin(ang)], axis=2).astype(np.float32)
    gam = 1.0 - 2.0 ** (-5.0 - np.arange(4))
    ii = np.arange(128)
    sct = np.zeros((128, 2, 2, 4), np.float32)
    sct[:, 0, 0, :] = (gam[None, :] ** (-(ii[:, None] + 1.0))) / 16.0
    sct[:, 0, 1, :] = gam[None, :] ** (ii[:, None] + 1.0)
    sct[:, 1, 0, :] = (gam[None, :] ** (-((ii[:, None] % 8) + 1.0))) / 16.0
    sct[:, 1, 1, :] = gam[None, :] ** ((ii[:, None] % 8) + 1.0)
    rmask = (ii[:, None] // 8 == np.arange(16)[None, :]).astype(np.float32)
    cmask = np.broadcast_to((np.arange(16)[:, None] == ii[None, :] // 8)[None], (128, 16, 128)).astype(ml_dtypes.bfloat16)
    mT = np.zeros((128, 2, 128), np.float32)
    mT[:, 0, :] = (ii[:, None] <= ii[None, :])
    mT[:, 1, :] = (ii[:, None] <= ii[None, :]) & (ii[:, None] // 8 == ii[None, :] // 8)
    bf = ml_dtypes.bfloat16

    def split3(a):
        a = np.asarray(a, np.float32)
        hi = a.astype(bf).astype(np.float32)
        mid = (a - hi).astype(bf).astype(np.float32)
        lo = (a - hi - mid).astype(bf).astype(np.float32)
        return hi, mid, lo
    slopes = np.exp2(-8.0 * np.arange(1, 17, dtype=np.float32) / 16).astype(np.float32)
    kpos = np.arange(2048)
    kaug = np.zeros((10, 2048), np.float32)
    kaug[0:3] = 64.0 * (kpos // 64)
    kaug[3:6] = kpos % 64
    kaug[6:9] = 1.0
    kaug[9] = (kpos < 1024) if h == 0 else 0.0
    nblk = np.arange(128)
    cend = 16 * nblk + 31
    kaugc = np.zeros((10, 128), np.float32)
    kaugc[0:3] = 64.0 * (cend // 64)
    kaugc[3:6] = cend % 64
    kaugc[6:9] = 1.0
    kaugc[9] = (nblk >= 127) | ((nblk < 64) if h == 0 else False)
    qpos = 896 + np.arange(1152)
    qaug = np.zeros((10, 16, 1152), np.float32)
    s3 = split3(slopes)
    m3 = split3(-(slopes[:, None] * qpos[None, :].astype(np.float32)))
    for r_ in range(3):
        qaug[r_] = s3[r_][:, None]
        qaug[3 + r_] = s3[r_][:, None]
        qaug[6 + r_] = m3[r_]
    qaug[9] = -30000.0
    E_tab = np.zeros((32, 16, 128), np.float32)
    for kt in range(16):
        for k_ in range(128):
            E_tab[2 * kt + k_ // 64, kt, k_] = 1.0
    cm_tab = np.zeros((128, 2, 128), np.float32)
    cm_tab[:, 0, :] = np.where(ii[:, None] <= ii[None, :], 0.0, -30000.0)
    cm_tab[:, 1, :] = np.where(ii[:, None] > ii[None, :], 0.0, -30000.0)
    cmpm = np.zeros((128, 9, 128), np.float32)
    for qi_ in range(9):
        qp = 896 + 128 * qi_ + ii
        cmpm[:, qi_, :] = np.where(cend[:, None] <= qp[None, :], 0.0, -30000.0)
    cs_ = nblk[:, None] * 16
    js_ = np.arange(32)[None, :] * 64
    cov = (np.clip(np.minimum(cs_ + 32, js_ + 64) - np.maximum(cs_, js_), 0, None) / 32.0).astype(np.float32)
    cov[127] = 0.0
    AB = np.zeros((128, 2, 9, 32), np.float32)
    j0 = 0 if h == 1 else 16
    jj = np.arange(32)
    for qi_ in range(9):
        qp = 896 + 128 * qi_ + ii
        cur = qp // 64
        valid = (jj[None, :] >= j0) & (jj[None, :] <= cur[:, None])
        forced = valid & ((jj[None, :] == j0) | (jj[None, :] == cur[:, None]) | (jj[None, :] == cur[:, None] - 1))
        AB[:, 0, qi_, :] = (valid & ~forced)
        AB[:, 1, qi_, :] = np.where(forced, 1.0e4 + jj[None, :], np.where(valid, 0.0, -1.0e4 - jj[None, :]))
    def aug_k(kp, invalid):
        a = np.zeros((10, kp.shape[0]), np.float32)
        a[0:3] = 64.0 * (kp // 64)
        a[3:6] = kp % 64
        a[6:9] = 1.0
        a[9] = invalid
        return a
    colS = np.arange(2176)
    ktS, pS = colS // 128, colS % 128
    kpS = np.where(ktS < 16, 8 * (128 * (ktS % 2) + pS) + ktS // 2, 2048 + pS)
    kaugS = aug_k(kpS, (kpS >= 2056))
    kpW = np.arange(640)
    kaugW = aug_k(kpW, (kpW >= 520))
    kaugcS = aug_k(cend, (nblk >= 127))

    def aug_q(qp):
        a = np.zeros((10, 16, 128), np.float32)
        mm = split3(-(slopes[:, None] * qp[None, :].astype(np.float32)))
        for r_ in range(3):
            a[r_] = s3[r_][:, None]
            a[3 + r_] = s3[r_][:, None]
            a[6 + r_] = mm[r_]
        a[9] = -30000.0
        return a
    qq = ii % 8
    qaugs = np.stack([aug_q(2048 + qq), aug_q(512 + qq)], axis=0)
    Es = np.zeros((33, 17, 128), np.float32)
    for kt in range(16):
        for k_ in range(128):
            pos_ = 8 * (128 * (kt % 2) + k_) + kt // 2
            Es[pos_ // 64, kt, k_] = 1.0
    Es[32, 16, :] = 1.0
    colq = ii % 8
    cmS = np.zeros((128, 2, 128), np.float32)
    cmS[:, 0, :] = np.where((ii[:, None] <= colq[None, :]) & (ii[:, None] < 8), 0.0, -30000.0)
    cmS[:, 1, :] = np.where(ii[:, None] > colq[None, :], 0.0, -30000.0)
    js33 = np.arange(33)[None, :] * 64
    covS = (np.clip(np.minimum(cs_ + 32, js33 + 64) - np.maximum(cs_, js33), 0, None) / 32.0).astype(np.float32)
    covS[127] = 0.0
    ABs = np.zeros((128, 2, 33), np.float32)
    j33 = np.arange(33)
    forcedS = (j33 == 0) | (j33 == 32) | (j33 == 31)
    ABs[:, 0, :] = (~forcedS)[None, :]
    ABs[:, 1, :] = np.where(forcedS, 1.0e4 + j33, 0.0)[None, :]
    Rsum = np.zeros((32, 16, 128), np.float32)
    for r_ in range(4):
        for q_ in range(8):
            for s_ in range(16):
                Rsum[8 * r_ + q_, s_, 8 * s_ + q_] = 1.0
    m = {
        'cache_kv': f(inputs['cache_kv'][0]).reshape(2560 * 16, 8192), 'page_table': np.ascontiguousarray(np.asarray(inputs['page_table'][16 * c:16 * c + 16], np.int32)),
        'upt_tab': (ii % 16).astype(np.float32).reshape(128, 1), 'Es_tab': Es.astype(bf), 'cmS_tab': cmS.astype(bf), 'covS_tab': covS,
        'ABs_tab': ABs, 'Rsum_tab': Rsum.astype(bf), 'kaugS_tab': kaugS.astype(bf), 'kaugW_tab': kaugW.astype(bf),
        'kaugcS_tab': kaugcS.astype(bf), 'qaugs_tab': qaugs.astype(bf),
        'w_out': f(inputs['w_out'][0]), 'g_ffn': f(inputs['g_ffn'][0]), 'w_up': f(inputs['w_up'][0]),
        'conv_w': f(inputs['conv_w'][0]), 'conv_b': f(inputs['conv_b'][0]), 'w_down': f(inputs['w_down'][0]),
        'state_conv': f(inputs['state_conv'][0, 16 * c:16 * c + 16]), 'identf': np.eye(128, dtype=np.float32),
        'hflag': np.array([float(h)], np.float32),
        'k_norm_cmp': f(inputs['k_norm_cmp'][0]), 'cmp_pe': f(inputs['cmp_pe'][0]), 'cmp_w1': f(inputs['cmp_w1'][0]),
        'cmp_b1': f(inputs['cmp_b1'][0]), 'cmp_w2': f(inputs['cmp_w2'][0]), 'cmp_b2': f(inputs['cmp_b2'][0]),
        'E_tab': E_tab.astype(bf), 'cm_tab': cm_tab.astype(bf), 'cmpm_tab': cmpm.astype(bf), 'cov_tab': cov, 'AB_tab': AB,
        'kaug_tab': kaug.astype(bf), 'kaugc_tab': kaugc.astype(bf), 'qaug_tab': qaug.astype(bf),
        'xp': np.ascontiguousarray(xp),
        'xs': f(inputs['x_sample'][16 * c:16 * c + 16]).reshape(128, D),
        'w_in': f(inputs['w_in'][0]),
        'g_attn': f(inputs['g_attn'][0]),
        'k_norm_slc': f(inputs['k_norm_slc'][0]),
        'k_norm_win': f(inputs['k_norm_win'][0]),
        'ident': np.eye(128, dtype=np.float32).astype(ml_dtypes.bfloat16),
        'cache_win': f(inputs['cache_win'][0, 16 * c:16 * c + 16]).reshape(16, 512, 512),
        'state_ret': f(inputs['state_ret'][0, 16 * c:16 * c + 16]),
        'cs_tab': cs, 'sct_tab': sct, 'rmask_tab': rmask, 'cmask_tab': np.ascontiguousarray(cmask), 'mT_tab': mT,
        'q_norm': f(inputs['q_norm'][0]), 'ret_gn': f(inputs['ret_gn'][0]),
    }
    return m


_CACHE = {}


def run_device(inputs, stage=99, trace=False):
    if stage not in _CACHE:
        _CACHE[stage] = build(stage)
    nc, S = _CACHE[stage]
    in_maps = [_host_inputs(inputs, c) for c in range(NCORES)]
    res = run_bass_kernel_spmd(nc, in_maps, core_ids=list(range(NCORES)), trace=trace)
    return res


def kernel(**inputs):
    res = run_device(inputs)
    R = res.results
    y_p = np.zeros((4, 2048, D), np.float32)
    y_s = np.zeros((128, 8, D), np.float32)
    kv_p = np.zeros((1, 4, 2048, 4, 4, 64), np.float32)
    kv_s = np.zeros((1, 128, 8, 4, 4, 64), np.float32)
    win_p = np.zeros((1, 4, 512, 2, 4, 64), np.float32)
    win_s = np.zeros((1, 128, 512, 2, 4, 64), np.float32)
    ret_p = np.zeros((1, 4, 4, 256, 256), np.float32)
    ret_s = np.zeros((1, 128, 4, 256, 256), np.float32)
    conv_p = np.zeros((1, 4, 2, 11264), np.float32)
    conv_s = np.zeros((1, 128, 2, 11264), np.float32)
    for c in range(NCORES):
        b, h = c // 2, c % 2
        r = R[c]
        kv_p[0, b, 1024 * h:1024 * h + 1024] = r['kv_out'][:1024].reshape(1024, 4, 4, 64)
        kv_s[0, 16 * c:16 * c + 16] = r['kv_out'][1024:].reshape(16, 8, 4, 4, 64)
        win_s[0, 16 * c:16 * c + 16] = r['win_s_out'].reshape(16, 512, 2, 4, 64)
        ret_s[0, 16 * c:16 * c + 16] = r['ret_s_out']
        y_p[b, 1024 * h:1024 * h + 1024] = r['y_out'][:1024]
        y_s[16 * c:16 * c + 16] = r['y_out'][1024:].reshape(16, 8, D)
        conv_s[0, 16 * c:16 * c + 16] = r['conv_s_out'].reshape(16, 2, 11264)
        if h == 1:
            conv_p[0, b] = r['conv_p_out']
            win_p[0, b] = r['win_p_out'].reshape(512, 2, 4, 64)
            ret_p[0, b] = r['ret_p_out']
    return (y_p, y_s, kv_p, kv_s, win_p, win_s, ret_p, ret_s, conv_p, conv_s)
```
